# Optimizing a Trainium2 kernel written in Bass

```python
import math
import jax, jax.numpy as jnp
from jax import lax
import numpy as np

D_MODEL = 2048
BATCH = 4
SEQ = 4096
DEPTH = 1

D_SSM = D_MODEL // 2
SSM_GROUP = 16
N_SSM_GROUPS = D_SSM // SSM_GROUP
SSM_STATE = 64
D_GMLP = D_MODEL - D_SSM
GMLP_HEAD = 128
N_GMLP_HEADS = D_GMLP // GMLP_HEAD
CHUNK = 128
D_IN = D_SSM + 2 * D_GMLP
D_FF = 5632
D_PLE = 256
EPS = 1e-6
DT_MIN = 1e-3
DT_MAX = 1e-1

kernel_name = "hybrid_s5_gmlp_macaron_ple"


def rms_norm(x, g):
    xf = x.astype(jnp.float32)
    y = xf * lax.rsqrt(jnp.mean(xf * xf, axis=-1, keepdims=True) + EPS) * g.astype(jnp.float32)
    return y.astype(x.dtype)


def layer_norm(x, g):
    xf = x.astype(jnp.float32)
    xc = xf - jnp.mean(xf, axis=-1, keepdims=True)
    y = xc * lax.rsqrt(jnp.mean(xc * xc, axis=-1, keepdims=True) + EPS) * g.astype(jnp.float32)
    return y.astype(x.dtype)


def swiglu(x, w_gate, w_up, w_down):
    return (jax.nn.silu(x @ w_gate) * (x @ w_up)) @ w_down


def _complex_affine_combine(e1, e2):
    a1r, a1i, b1r, b1i = e1
    a2r, a2i, b2r, b2i = e2
    return (a2r * a1r - a2i * a1i,
            a2r * a1i + a2i * a1r,
            a2r * b1r - a2i * b1i + b2r,
            a2r * b1i + a2i * b1r + b2i)


def s5_mixer(u, log_dt, a_re, a_im, b_re, b_im, c_re, c_im, d, w_glu):
    bsz, seqlen, _ = u.shape
    f32 = jnp.float32
    uf = u.astype(f32).reshape(bsz, seqlen, N_SSM_GROUPS, SSM_GROUP)
    dt = jnp.exp(log_dt.astype(f32))[:, None]
    lr = jnp.minimum(a_re.astype(f32), -1e-4)
    li = a_im.astype(f32)
    mag = jnp.exp(lr * dt)
    ang = li * dt
    abar_r = mag * jnp.cos(ang)
    abar_i = mag * jnp.sin(ang)
    den = lr * lr + li * li
    xr = abar_r - 1.0
    xi = abar_i
    zr = (xr * lr + xi * li) / den
    zi = (xi * lr - xr * li) / den
    br = b_re.astype(f32)
    bi = b_im.astype(f32)
    bbar_r = zr[..., None] * br - zi[..., None] * bi
    bbar_i = zr[..., None] * bi + zi[..., None] * br
    drive_r = jnp.einsum("blgp,gnp->lbgn", uf, bbar_r)
    drive_i = jnp.einsum("blgp,gnp->lbgn", uf, bbar_i)
    ar = jnp.broadcast_to(abar_r[None, None], (seqlen, 1, N_SSM_GROUPS, SSM_STATE))
    ai = jnp.broadcast_to(abar_i[None, None], (seqlen, 1, N_SSM_GROUPS, SSM_STATE))
    _, _, sr, si = lax.associative_scan(_complex_affine_combine, (ar, ai, drive_r, drive_i), axis=0)
    y = (jnp.einsum("lbgn,gpn->blgp", sr, c_re.astype(f32))
         - jnp.einsum("lbgn,gpn->blgp", si, c_im.astype(f32)))
    y = y + d.astype(f32).reshape(N_SSM_GROUPS, SSM_GROUP) * uf
    y = jax.nn.gelu(y.reshape(bsz, seqlen, D_SSM))
    y = y * jax.nn.sigmoid(y @ w_glu.astype(f32))
    return y.astype(u.dtype)


def gmlp_mixer(z_u, z_v, norm_v, w_s, b_s):
    bsz, seqlen, _ = z_u.shape
    n_chunks = seqlen // CHUNK
    u = jax.nn.gelu(z_u)
    v = layer_norm(jax.nn.gelu(z_v), norm_v)
    causal = jnp.tril(jnp.ones((CHUNK, CHUNK), dtype=bool))
    w = jnp.where(causal[None], w_s, jnp.zeros_like(w_s))
    vc = v.reshape(bsz, n_chunks, CHUNK, N_GMLP_HEADS, GMLP_HEAD)
    s = jnp.einsum("hts,bcshp->bcthp", w, vc) + b_s.T[None, None, :, :, None]
    out = u.reshape(bsz, n_chunks, CHUNK, N_GMLP_HEADS, GMLP_HEAD) * s
    return out.reshape(bsz, seqlen, D_GMLP)


def setup_inputs(seed: int = 0) -> dict:
    key = jax.random.key(seed)
    ks = jax.random.split(key, 32)
    f32 = jnp.float32

    def nrm(k, shape, std):
        return (jax.random.normal(k, shape, f32) * std).astype(f32)

    def gain(k, shape):
        return 1.0 + nrm(k, shape, 0.02)

    L = DEPTH
    x = nrm(ks[0], (BATCH, SEQ, D_MODEL), 1.0)
    p = nrm(ks[1], (L, BATCH, SEQ, D_PLE), 1.0)
    n_idx = jnp.arange(SSM_STATE, dtype=f32)
    return {
        "x": x,
        "p": p,
        "norm_ffn1": gain(ks[2], (L, D_MODEL)),
        "w1_gate": nrm(ks[3], (L, D_MODEL, D_FF), D_MODEL ** -0.5),
        "w1_up": nrm(ks[4], (L, D_MODEL, D_FF), D_MODEL ** -0.5),
        "w1_down": nrm(ks[5], (L, D_FF, D_MODEL), D_FF ** -0.5),
        "norm_mix": gain(ks[6], (L, D_MODEL)),
        "w_in": nrm(ks[7], (L, D_MODEL, D_IN), D_MODEL ** -0.5),
        "ssm_log_dt": jax.random.uniform(ks[8], (L, N_SSM_GROUPS), f32, math.log(DT_MIN), math.log(DT_MAX)),
        "ssm_a_re": -0.5 + nrm(ks[9], (L, N_SSM_GROUPS, SSM_STATE), 0.01),
        "ssm_a_im": math.pi * n_idx[None, None, :] + nrm(ks[10], (L, N_SSM_GROUPS, SSM_STATE), 0.01),
        "ssm_b_re": nrm(ks[11], (L, N_SSM_GROUPS, SSM_STATE, SSM_GROUP), (2 * SSM_GROUP) ** -0.5),
        "ssm_b_im": nrm(ks[12], (L, N_SSM_GROUPS, SSM_STATE, SSM_GROUP), (2 * SSM_GROUP) ** -0.5),
        "ssm_c_re": nrm(ks[13], (L, N_SSM_GROUPS, SSM_GROUP, SSM_STATE), 0.5 ** 0.5),
        "ssm_c_im": nrm(ks[14], (L, N_SSM_GROUPS, SSM_GROUP, SSM_STATE), 0.5 ** 0.5),
        "ssm_d": nrm(ks[15], (L, D_SSM), 1.0),
        "ssm_w_glu": nrm(ks[16], (L, D_SSM, D_SSM), D_SSM ** -0.5),
        "gmlp_norm_v": gain(ks[17], (L, D_GMLP)),
        "gmlp_w_s": nrm(ks[18], (L, N_GMLP_HEADS, CHUNK, CHUNK), CHUNK ** -0.5),
        "gmlp_b_s": 1.0 + nrm(ks[19], (L, N_GMLP_HEADS, CHUNK), 0.01),
        "norm_ssm_out": gain(ks[20], (L, D_SSM)),
        "norm_gmlp_out": gain(ks[21], (L, D_GMLP)),
        "w_out": nrm(ks[22], (L, D_MODEL, D_MODEL), D_MODEL ** -0.5),
        "norm_ffn2": gain(ks[23], (L, D_MODEL)),
        "w2_gate": nrm(ks[24], (L, D_MODEL, D_FF), D_MODEL ** -0.5),
        "w2_up": nrm(ks[25], (L, D_MODEL, D_FF), D_MODEL ** -0.5),
        "w2_down": nrm(ks[26], (L, D_FF, D_MODEL), D_FF ** -0.5),
        "norm_ple": gain(ks[27], (L, D_MODEL)),
        "w_ple_gate": nrm(ks[28], (L, D_MODEL, D_MODEL), D_MODEL ** -0.5),
        "w_ple_proj": nrm(ks[29], (L, D_PLE, D_MODEL), D_PLE ** -0.5),
        "norm_final": gain(ks[30], (D_MODEL,)),
    }


def reference(x, p, norm_ffn1, w1_gate, w1_up, w1_down, norm_mix, w_in,
              ssm_log_dt, ssm_a_re, ssm_a_im, ssm_b_re, ssm_b_im, ssm_c_re, ssm_c_im,
              ssm_d, ssm_w_glu, gmlp_norm_v, gmlp_w_s, gmlp_b_s,
              norm_ssm_out, norm_gmlp_out, w_out, norm_ffn2, w2_gate, w2_up, w2_down,
              norm_ple, w_ple_gate, w_ple_proj, norm_final):
    h = x
    for i in range(DEPTH):
        h = h + 0.5 * swiglu(rms_norm(h, norm_ffn1[i]), w1_gate[i], w1_up[i], w1_down[i])
        z = rms_norm(h, norm_mix[i]) @ w_in[i]
        z_ssm = z[..., :D_SSM]
        z_u = z[..., D_SSM:D_SSM + D_GMLP]
        z_v = z[..., D_SSM + D_GMLP:]
        y_ssm = s5_mixer(z_ssm, ssm_log_dt[i], ssm_a_re[i], ssm_a_im[i], ssm_b_re[i], ssm_b_im[i],
                         ssm_c_re[i], ssm_c_im[i], ssm_d[i], ssm_w_glu[i])
        y_gmlp = gmlp_mixer(z_u, z_v, gmlp_norm_v[i], gmlp_w_s[i], gmlp_b_s[i])
        y = jnp.concatenate([rms_norm(y_ssm, norm_ssm_out[i]), rms_norm(y_gmlp, norm_gmlp_out[i])], axis=-1)
        h = h + y @ w_out[i]
        h = h + 0.5 * swiglu(rms_norm(h, norm_ffn2[i]), w2_gate[i], w2_up[i], w2_down[i])
        gate = jax.nn.sigmoid(rms_norm(h, norm_ple[i]) @ w_ple_gate[i])
        h = h + gate * (p[i] @ w_ple_proj[i])
    return rms_norm(h, norm_final)
```

```python
import contextlib
import math
import numpy as np
import concourse.bass as bass
import concourse.mybir as mybir
from concourse.bass_utils import run_bass_kernel_spmd

F32 = mybir.dt.float32
BF16 = mybir.dt.bfloat16
I32 = mybir.dt.int32
AF = mybir.ActivationFunctionType
ALU = mybir.AluOpType

D = 2048
DFF = 5632
DSSM = 1024
TB = 512
KT = D // 128
SEG = 64
NSEG = TB // SEG
EPS = 1e-6
N_CORES = 8
SEQ = 4096

C_G1, C_GM, C_G2, C_GP, C_GF = 0, 16, 32, 48, 64
C_GS, C_GG, C_DD = 80, 88, 96
NCOL = 104


class Buf:
    __slots__ = ("name", "w", "r", "dsem", "dcnt")

    def __init__(self, name):
        self.name = name
        self.w = None
        self.r = {}
        self.dsem = None
        self.dcnt = 0


class Eng:
    def __init__(self, e, semidx):
        self.e = e
        self.semidx = semidx
        self.n = 0
        self.known = {}


class Tracker:
    def __init__(self, nc, es):
        self.nc = nc
        self.es = es
        self.sems = []
        self.dsem_by_name = {}

    def new_sem(self, name):
        s = self.es.enter_context(self.nc.semaphore(name))
        self.sems.append(s)
        return len(self.sems) - 1

    def eng(self, e, name):
        return Eng(e, self.new_sem("e_" + name))

    def _deps(self, reads, writes):
        deps = {}

        def add(k, v):
            if deps.get(k, 0) < v:
                deps[k] = v
        for b in reads:
            if b.w is not None:
                add(*b.w)
        for b in writes:
            if b.w is not None:
                add(*b.w)
            for k, v in b.r.items():
                add(k, v)
        return deps

    def _wait(self, E, deps):
        for k, v in deps.items():
            if E.known.get(k, 0) < v:
                E.e.wait_ge(self.sems[k], v)
                E.known[k] = v

    def _mark(self, tok, reads, writes):
        k, v = tok
        for b in reads:
            if b.r.get(k, 0) < v:
                b.r[k] = v
        for b in writes:
            b.w = tok
            b.r = {}

    def op(self, E, fn, reads=(), writes=()):
        self._wait(E, self._deps(reads, writes))
        ins = fn()
        E.n += 1
        ins.then_inc(self.sems[E.semidx], 1)
        tok = (E.semidx, E.n)
        self._mark(tok, reads, writes)
        return tok

    def group(self, E, fns, reads=(), writes=()):
        self._wait(E, self._deps(reads, writes))
        ins = None
        for fn in fns:
            ins = fn()
        E.n += 1
        ins.then_inc(self.sems[E.semidx], 1)
        tok = (E.semidx, E.n)
        self._mark(tok, reads, writes)
        return tok

    def custom(self, E, fn, name, reads=(), writes=()):
        self._wait(E, self._deps(reads, writes))
        ins = fn()
        k = self.new_sem(name)
        ins.then_inc(self.sems[k], 1)
        tok = (k, 1)
        self._mark(tok, reads, writes)
        return tok

    def dma(self, E, out, in_, dbuf, reads=(), writes=(), **kw):
        if dbuf.name not in self.dsem_by_name:
            self.dsem_by_name[dbuf.name] = [self.new_sem("d_" + dbuf.name), 0]
        ent = self.dsem_by_name[dbuf.name]
        self._wait(E, self._deps(reads, writes))
        ins = E.e.dma_start(out=out, in_=in_, **kw)
        ent[1] += 16
        ins.then_inc(self.sems[ent[0]], 16)
        tok = (ent[0], ent[1])
        self._mark(tok, reads, writes)
        return tok


def build_nc(nblk, stage=99, use_cc=False):
    ntok = nblk * TB
    nc = bass.Bass("TRN2", target_bir_lowering=False)

    def din(name, shape):
        return nc.dram_tensor(name, list(shape), F32, kind="ExternalInput").ap()

    x_d = din("x", [ntok, D])
    p_d = din("p", [ntok, 256])
    out_d = nc.dram_tensor("out", [ntok, D], F32, kind="ExternalOutput").ap()
    w1g_d, w1u_d, w1d_d = din("w1g", [D, DFF]), din("w1u", [D, DFF]), din("w1d", [DFF, D])
    w2g_d, w2u_d, w2d_d = din("w2g", [D, DFF]), din("w2u", [D, DFF]), din("w2d", [DFF, D])
    win_d = din("win", [D, 3072])
    wglu_d = din("wglu", [DSSM, DSSM])
    wout_d = din("wout", [D, D])
    wpg_d = din("wpg", [D, D])
    wpp_d = din("wpp", [256, D])
    cols_d = din("cols", [128, NCOL])
    nvbc_d = din("nvbc", [128, 1024])
    bsbc_d = din("bsbc", [128, 1024])
    wst_d = din("wst", [128, 1024])
    mask_d = din("mask", [128, 128])
    ident_d = din("ident", [128, 128])
    tau_d = din("tau", [128, SEG])
    ssa_d = din("ssa", [128, 96])
    ssb_d = din("ssb", [128, 2, 512])
    ssc_d = din("ssc", [128, 2, 1024])
    if use_cc:
        flags_d = din("flags", [128, 3 * nblk])
        ccin_d = [nc.dram_tensor(f"cc_in{k}", [128, 64], F32) for k in range(nblk)]
        ccout_d = [nc.dram_tensor(f"cc_out{k}", [256, 64], F32) for k in range(nblk)]

    es = contextlib.ExitStack()
    with es:
        T = Tracker(nc, es)
        PE = T.eng(nc.tensor, "pe")
        ACT = T.eng(nc.scalar, "act")
        DVE = T.eng(nc.vector, "dve")
        POOL = T.eng(nc.gpsimd, "pool")
        SP = T.eng(nc.sync, "sp")

        def sb(name, shape, dt):
            return es.enter_context(nc.sbuf_tensor("sb_" + name, list(shape), dt))

        h_t = sb("h", [128, KT, TB], F32)
        xn_t = sb("xn", [128, KT, TB], BF16)
        hB = [Buf(f"h{i}") for i in range(KT)]
        xnB = [Buf(f"xn{i}") for i in range(KT)]
        cols_t = sb("cols", [128, NCOL], F32)
        ident_t = sb("ident", [128, 128], F32)
        ones_t = sb("ones", [128, 128], BF16)
        nvbc_t = sb("nvbc", [128, 1024], F32)
        bsbc_t = sb("bsbc", [128, 1024], F32)
        wct_t = sb("wct", [128, 1024], BF16)
        cosT = sb("cosT", [128, 32, SEG], F32)
        sinT = sb("sinT", [128, 32, SEG], F32)
        mtab = sb("mtab", [128, 32, SEG], F32)
        bbT = sb("bbT", [128, 8, 2, 128], BF16)
        c32b = sb("c32b", [128, 32, 2, 32], BF16)
        rotm = sb("rotm", [128, 2, 2, 32], F32)
        carry = sb("carry", [128, 2, 32], F32)
        small = sb("small", [128, 64], F32)
        ptab = sb("ptab", [128, 32, 2, SEG], BF16)
        ttab = sb("ttab", [128, 2, NSEG, 32], F32)
        a512 = sb("a512", [128, 2, 32], F32)
        sprev = sb("sprev", [128, 2, 32], F32)
        ssms = sb("ssms", [128, 12, 32], F32)
        gbuf = sb("gbuf", [128, 2, 64], F32)
        kbuf = sb("kbuf", [128, 2, NSEG, 32], F32)
        wbuf = sb("wbuf", [128, 2, 2, 4, NSEG, 32], BF16)
        flags_t = sb("flags", [128, 3 * nblk], F32)
        ssmsB, gB, kB = Buf("ssms"), Buf("gbuf"), Buf("kbuf")
        wB = [Buf("wbuf0"), Buf("wbuf1")]
        sprevB = Buf("sprev")
        constB = Buf("const")
        carryB = [Buf(f"carry{i}") for i in range(8)]
        smallB = Buf("small")

        ARENA_W = 22528
        arena = sb("arena", [128, ARENA_W], F32)
        SLOT_W = 2048

        def a32(off, n):
            return arena[:, off:off + n]

        def a16(off, n_bf):
            return arena[:, off:off + n_bf // 2].bitcast(BF16)

        NSLOT_FFN = 8
        slot_off = [i * SLOT_W for i in range(NSLOT_FFN)]
        OFF_ACT = 16384
        OFF_SILU = 17408
        OFF_XIN = 0
        OFF_PIN = 4096
        OFF_OST = 4096
        OFF_ZB = 4096
        OFF_GU = OFF_ZB + 2048
        OFF_V = OFF_GU + 2048
        OFF_E = OFF_V + 2048
        OFF_S = OFF_E + 4096
        OFF_TMP = OFF_S + 1024
        assert OFF_TMP + 3072 == 18432
        OFF_SQ = 20480
        OFF_RT = 20992
        OFF_PT = 22016
        sqB = [Buf("sq0"), Buf("sq1")]
        rtB = [Buf("rt"), Buf("rstd")]
        ptB = Buf("pt")

        ps_t = [es.enter_context(nc.psum_tensor(f"ps{i}", [128, 512], F32)) for i in range(8)]
        psB = [Buf(f"ps{i}") for i in range(8)]
        ps_ctr = [0]

        def nextps():
            i = ps_ctr[0] % 8
            ps_ctr[0] += 1
            return ps_t[i], psB[i]

        arena_live = []

        def new_phase_bufs(names):
            inherit = {}
            for b in arena_live:
                if b.w is not None:
                    k, v = b.w
                    if inherit.get(k, 0) < v:
                        inherit[k] = v
                for k, v in b.r.items():
                    if inherit.get(k, 0) < v:
                        inherit[k] = v
            out = []
            for n in names:
                b = Buf(n)
                b.r = dict(inherit)
                out.append(b)
            arena_live.clear()
            arena_live.extend(out)
            return out

        scope_state = {"cm": None}

        def scope(name):
            if scope_state["cm"] is not None:
                scope_state["cm"].__exit__(None, None, None)
                scope_state["cm"] = None
            if name is not None:
                cm = nc.named_scope(name)
                cm.__enter__()
                scope_state["cm"] = cm

        scope("setup")

        def cload(dst, src):
            T.dma(SP, dst, src, constB, writes=[constB])

        cload(cols_t[:], cols_d)
        cload(ident_t[:], ident_d)
        cload(nvbc_t[:], nvbc_d)
        cload(bsbc_t[:], bsbc_d)
        if use_cc:
            cload(flags_t[:], flags_d)
        setupB = new_phase_bufs(["setup"])[0]
        wst_s = a32(0, 1024)
        mask_s = a32(1024, 128)
        tau_s = a32(1152, SEG)
        ssa_s = a32(1216, 96)
        ssb_s = a32(1312, 1024)
        ssc_s = a32(2336, 2048)
        cload(wst_s, wst_d)
        cload(mask_s, mask_d)
        cload(tau_s, tau_d)
        cload(ssa_s, ssa_d)
        cload(ssb_s, ssb_d.rearrange("p a b -> p (a b)"))
        cload(ssc_s, ssc_d.rearrange("p a b -> p (a b)"))
        W0 = 4384

        def V(fn, reads=(), writes=()):
            return T.op(DVE, fn, reads=list(reads) + [constB], writes=list(writes) + [setupB])

        def A(fn, reads=(), writes=()):
            return T.op(ACT, fn, reads=list(reads) + [constB], writes=list(writes) + [setupB])

        dv, ac = nc.vector, nc.scalar
        V(lambda: dv.memset(ones_t[:], 1.0))
        V(lambda: dv.memset(carry[:], 0.0))
        V(lambda: dv.tensor_tensor(out=wct_t[:].rearrange("p (h t) -> p h t", h=8),
                                   in0=wst_s.rearrange("p (h t) -> p h t", h=8),
                                   in1=mask_s.unsqueeze(1).to_broadcast([128, 8, 128]), op=ALU.mult))
        ldt, are, aim = ssa_s[:, 0:32], ssa_s[:, 32:64], ssa_s[:, 64:96]
        sc = [a32(W0 + 32 * i, 32) for i in range(24)]
        dtv, lr, mag, ang, cs, sn, abr, abi, den, xr, zr, zi, u1, u2, u3, u4 = sc[:16]
        A(lambda: ac.activation(out=dtv, in_=ldt, func=AF.Exp))
        V(lambda: dv.tensor_scalar(out=lr, in0=are, scalar1=-1e-4, scalar2=None, op0=ALU.min))
        lrdt = sc[16]
        V(lambda: dv.tensor_tensor(out=lrdt, in0=lr, in1=dtv, op=ALU.mult))
        A(lambda: ac.activation(out=mag, in_=lrdt, func=AF.Exp))
        V(lambda: dv.tensor_tensor(out=ang, in0=aim, in1=dtv, op=ALU.mult))

        BIGW = W0 + 1024

        def sincos(src, n, out_sin, out_cos, scale=1.0):
            y = a32(BIGW, n)
            yi = a32(BIGW + n, n).bitcast(I32)
            yf = a32(BIGW + 2 * n, n)
            m1 = a32(BIGW + 3 * n, n)
            for off, dst in ((0.0, out_sin), (0.25, out_cos)):
                V(lambda: dv.tensor_scalar(out=y, in0=src, scalar1=scale / (2 * math.pi), scalar2=off,
                                           op0=ALU.mult, op1=ALU.add))
                V(lambda: dv.tensor_copy(out=yi, in_=y))
                V(lambda: dv.tensor_copy(out=yf, in_=yi))
                V(lambda: dv.tensor_tensor(out=y, in0=y, in1=yf, op=ALU.subtract))
                V(lambda: dv.tensor_scalar(out=m1, in0=y, scalar1=0.5, scalar2=None, op0=ALU.is_gt))
                V(lambda: dv.tensor_tensor(out=y, in0=y, in1=m1, op=ALU.subtract))
                V(lambda: dv.tensor_scalar(out=m1, in0=y, scalar1=-0.5, scalar2=None, op0=ALU.is_lt))
                V(lambda: dv.tensor_tensor(out=y, in0=y, in1=m1, op=ALU.add))
                A(lambda: ac.activation(out=dst, in_=y, func=AF.Sin, scale=2 * math.pi))

        sincos(ang, 32, sn, cs)
        V(lambda: dv.tensor_tensor(out=abr, in0=mag, in1=cs, op=ALU.mult))
        V(lambda: dv.tensor_tensor(out=abi, in0=mag, in1=sn, op=ALU.mult))
        V(lambda: dv.tensor_tensor(out=u1, in0=lr, in1=lr, op=ALU.mult))
        V(lambda: dv.tensor_tensor(out=u2, in0=aim, in1=aim, op=ALU.mult))
        V(lambda: dv.tensor_tensor(out=den, in0=u1, in1=u2, op=ALU.add))
        V(lambda: dv.reciprocal(out=den, in_=den))
        V(lambda: dv.tensor_scalar(out=xr, in0=abr, scalar1=-1.0, scalar2=None, op0=ALU.add))
        V(lambda: dv.tensor_tensor(out=u1, in0=xr, in1=lr, op=ALU.mult))
        V(lambda: dv.tensor_tensor(out=u2, in0=abi, in1=aim, op=ALU.mult))
        V(lambda: dv.tensor_tensor(out=u1, in0=u1, in1=u2, op=ALU.add))
        V(lambda: dv.tensor_tensor(out=zr, in0=u1, in1=den, op=ALU.mult))
        V(lambda: dv.tensor_tensor(out=u1, in0=abi, in1=lr, op=ALU.mult))
        V(lambda: dv.tensor_tensor(out=u2, in0=xr, in1=aim, op=ALU.mult))
        V(lambda: dv.tensor_tensor(out=u1, in0=u1, in1=u2, op=ALU.subtract))
        V(lambda: dv.tensor_tensor(out=zi, in0=u1, in1=den, op=ALU.mult))
        bre = ssb_s[:, 0:512].rearrange("p (g q) -> p g q", q=16)
        bim = ssb_s[:, 512:1024].rearrange("p (g q) -> p g q", q=16)
        bw = BIGW + 8192
        bt1 = a32(bw, 512).rearrange("p (g q) -> p g q", q=16)
        bt2 = a32(bw + 512, 512).rearrange("p (g q) -> p g q", q=16)
        bbr = a32(bw + 1024, 512).rearrange("p (g q) -> p g q", q=16)
        bbi = a32(bw + 1536, 512).rearrange("p (g q) -> p g q", q=16)
        blk = [a32(bw + 2048, 1024), a32(bw + 3072, 1024)]
        zrb = zr.unsqueeze(2).to_broadcast([128, 32, 16])
        zib = zi.unsqueeze(2).to_broadcast([128, 32, 16])
        V(lambda: dv.tensor_tensor(out=bt1, in0=bre, in1=zrb, op=ALU.mult))
        V(lambda: dv.tensor_tensor(out=bt2, in0=bim, in1=zib, op=ALU.mult))
        V(lambda: dv.tensor_tensor(out=bbr, in0=bt1, in1=bt2, op=ALU.subtract))
        V(lambda: dv.tensor_tensor(out=bt1, in0=bim, in1=zrb, op=ALU.mult))
        V(lambda: dv.tensor_tensor(out=bt2, in0=bre, in1=zib, op=ALU.mult))
        V(lambda: dv.tensor_tensor(out=bbi, in0=bt1, in1=bt2, op=ALU.add))
        for ri, src in ((0, bbr), (1, bbi)):
            b3 = blk[ri].rearrange("p (g c) -> p g c", c=32)
            V(lambda: dv.memset(blk[ri], 0.0))
            V(lambda: dv.tensor_copy(out=b3[0:64, :, 0:16], in_=src[0:64]))
            V(lambda: dv.tensor_copy(out=b3[64:128, :, 16:32], in_=src[64:128]))
            for bt_ in range(8):
                pst, psb = nextps()
                T.op(PE, lambda: nc.tensor.transpose(pst[:, 0:128], blk[ri][:, bt_ * 128:(bt_ + 1) * 128], ident_t[:]),
                     reads=[setupB, constB], writes=[psb])
                T.op(DVE, lambda: dv.tensor_copy(out=bbT[:, bt_, ri, :], in_=pst[:, 0:128]), reads=[psb], writes=[setupB])
        cre = ssc_s[:, 0:1024].rearrange("p (g c) -> p g c", c=32)
        cim = ssc_s[:, 1024:2048].rearrange("p (g c) -> p g c", c=32)
        V(lambda: dv.tensor_copy(out=c32b[:, :, 0, :], in_=cre))
        V(lambda: dv.tensor_scalar(out=c32b[:, :, 1, :], in0=cim, scalar1=-1.0, scalar2=None, op0=ALU.mult))
        ph = a32(BIGW + 8192 + 4096, 2048)
        V(lambda: dv.tensor_tensor(out=ph.rearrange("p (g t) -> p g t", t=SEG),
                                   in0=ang.unsqueeze(2).to_broadcast([128, 32, SEG]),
                                   in1=tau_s.unsqueeze(1).to_broadcast([128, 32, SEG]), op=ALU.mult))
        sincos(ph, 2048, sinT[:].rearrange("p g t -> p (g t)"), cosT[:].rearrange("p g t -> p (g t)"))
        tm = a32(BIGW + 8192 + 4096 + 2048, SEG)
        V(lambda: dv.tensor_scalar(out=tm, in0=tau_s, scalar1=0.5, scalar2=None, op0=ALU.is_gt))
        V(lambda: dv.tensor_tensor(out=mtab[:], in0=mag.unsqueeze(2).to_broadcast([128, 32, SEG]),
                                   in1=tm.unsqueeze(1).to_broadcast([128, 32, SEG]), op=ALU.mult))
        sincos(ang, 32, u3, u4, scale=float(SEG))
        V(lambda: dv.tensor_tensor(out=rotm[:, 0, 0, :], in0=u4, in1=mag, op=ALU.mult))
        V(lambda: dv.tensor_copy(out=rotm[:, 0, 1, :], in_=rotm[:, 0, 0, :]))
        V(lambda: dv.tensor_tensor(out=rotm[:, 1, 1, :], in0=u3, in1=mag, op=ALU.mult))
        V(lambda: dv.tensor_scalar(out=rotm[:, 1, 0, :], in0=rotm[:, 1, 1, :], scalar1=-1.0, scalar2=None, op0=ALU.mult))
        V(lambda: dv.memset(sprev[:], 0.0))
        phm = a32(BIGW + 8192 + 4096, 2048)
        V(lambda: dv.tensor_tensor(out=phm.rearrange("p (g t) -> p g t", t=SEG),
                                   in0=lrdt.unsqueeze(2).to_broadcast([128, 32, SEG]),
                                   in1=tau_s.unsqueeze(1).to_broadcast([128, 32, SEG]), op=ALU.mult))
        A(lambda: ac.activation(out=phm, in_=phm, func=AF.Exp))
        V(lambda: dv.tensor_tensor(out=ptab[:, :, 0, :], in0=phm.rearrange("p (g t) -> p g t", t=SEG), in1=cosT[:], op=ALU.mult))
        V(lambda: dv.tensor_tensor(out=ptab[:, :, 1, :], in0=phm.rearrange("p (g t) -> p g t", t=SEG), in1=sinT[:], op=ALU.mult))
        pw = [(sc[17], sc[18]), (sc[19], sc[20])]
        V(lambda: dv.tensor_copy(out=pw[0][0], in_=abr))
        V(lambda: dv.tensor_copy(out=pw[0][1], in_=abi))
        a64 = (sc[21], sc[22])
        cur = 0
        for it in range(9):
            r_, i_ = pw[cur]
            nr, ni = pw[1 - cur]
            V(lambda: dv.tensor_tensor(out=u1, in0=r_, in1=r_, op=ALU.mult))
            V(lambda: dv.tensor_tensor(out=u2, in0=i_, in1=i_, op=ALU.mult))
            V(lambda: dv.tensor_tensor(out=nr, in0=u1, in1=u2, op=ALU.subtract))
            V(lambda: dv.tensor_tensor(out=u1, in0=r_, in1=i_, op=ALU.mult))
            V(lambda: dv.tensor_scalar(out=ni, in0=u1, scalar1=2.0, scalar2=None, op0=ALU.mult))
            cur = 1 - cur
            if it == 5:
                V(lambda: dv.tensor_copy(out=a64[0], in_=nr))
                V(lambda: dv.tensor_copy(out=a64[1], in_=ni))
        V(lambda: dv.tensor_copy(out=a512[:, 0, :], in_=pw[cur][0]))
        V(lambda: dv.tensor_copy(out=a512[:, 1, :], in_=pw[cur][1]))
        V(lambda: dv.tensor_copy(out=ttab[:, 0, 0, :], in_=abr))
        V(lambda: dv.tensor_copy(out=ttab[:, 1, 0, :], in_=abi))
        for s_ in range(1, NSEG):
            pr, pi = ttab[:, 0, s_ - 1, :], ttab[:, 1, s_ - 1, :]
            V(lambda: dv.tensor_tensor(out=u1, in0=pr, in1=a64[0], op=ALU.mult))
            V(lambda: dv.tensor_tensor(out=u2, in0=pi, in1=a64[1], op=ALU.mult))
            V(lambda: dv.tensor_tensor(out=ttab[:, 0, s_, :], in0=u1, in1=u2, op=ALU.subtract))
            V(lambda: dv.tensor_tensor(out=u1, in0=pr, in1=a64[1], op=ALU.mult))
            V(lambda: dv.tensor_tensor(out=u2, in0=pi, in1=a64[0], op=ALU.mult))
            V(lambda: dv.tensor_tensor(out=ttab[:, 1, s_, :], in0=u1, in1=u2, op=ALU.add))
        T.op(DVE, lambda: dv.memset(small[:], 0.0), reads=[setupB, constB], writes=[smallB, constB])

        ring = {"slots": [], "i": 0, "offs": slot_off}

        def set_ring(nslots):
            bufs = [Buf(f"slot{j}") for j in range(nslots)]
            ring["slots"] = bufs
            ring["i"] = 0
            return bufs

        def load_w(view_fn, src):
            j = ring["i"] % len(ring["slots"])
            ring["i"] += 1
            b = ring["slots"][j]
            sl = a16(ring["offs"][j], 4096)
            dst = view_fn(sl)
            T.dma(POOL, dst, src, b, writes=[b])
            return dst, b

        def wview_k(nk, ncols):
            return lambda sl: sl[:, 0:nk * ncols].rearrange("p (k c) -> p k c", k=nk)

        pe, gp = nc.tensor, nc.gpsimd

        def norm(srcs, gcol, dsts, dim, sqB, rtB, in_place_fp32=False):
            n = len(srcs)
            pst, psb = nextps()
            for i, (sap, sbuf) in enumerate(srcs):
                sq = a16(OFF_SQ + 256 * (i % 2), 512)
                T.op(ACT, lambda: ac.activation(out=sq, in_=sap, func=AF.Square), reads=[sbuf], writes=[sqB[i % 2]])
                T.op(PE, lambda: pe.matmul(pst[:], lhsT=ones_t[:], rhs=sq, start=(i == 0), stop=(i == n - 1)),
                     reads=[sqB[i % 2], constB], writes=[psb])
            rt = a32(OFF_RT, 512)
            rstd = a32(OFF_RT + 512, 512)
            T.op(ACT, lambda: ac.activation(out=rt, in_=pst[:], func=AF.Sqrt, scale=1.0 / dim, bias=small[:, 1:2]),
                 reads=[psb, smallB], writes=[rtB[0]])
            T.op(DVE, lambda: dv.reciprocal(out=rstd, in_=rt), reads=[rtB[0]], writes=[rtB[1]])
            for i, (sap, sbuf) in enumerate(srcs):
                dap, dbuf = dsts[i]
                T.op(DVE, lambda: dv.scalar_tensor_tensor(out=dap, in0=sap, scalar=gcol[:, i:i + 1], in1=rstd,
                                                          op0=ALU.mult, op1=ALU.mult),
                     reads=[sbuf, rtB[1], constB], writes=[dbuf])

        T.op(DVE, lambda: dv.memset(small[:, 1:2], EPS), reads=[constB], writes=[smallB])

        hT = [(h_t[:, i, :], hB[i]) for i in range(KT)]
        xnT = [(xn_t[:, i, :], xnB[i]) for i in range(KT)]

        def ffn(wg_d, wu_d, wd_d):
            set_ring(NSLOT_FFN)
            fb = new_phase_bufs([f"slot{j}" for j in range(NSLOT_FFN)] + ["act0", "act1", "silu0", "silu1", "wpp"])
            ring["slots"] = fb[:NSLOT_FFN]
            ring["offs"] = slot_off
            actB, siluB = fb[8:10], fb[10:12]
            wgv = wg_d.rearrange("(k p) c -> p k c", p=128)
            wuv = wu_d.rearrange("(k p) c -> p k c", p=128)
            wdv = wd_d.rearrange("(j p) f -> p j f", p=128)
            NCH = DFF // 256
            pend = None

            def down(c, wd_ap, wdb, actv, ab):
                for i in range(KT):
                    pst, psb = nextps()
                    T.group(PE, [(lambda jj=jj: pe.matmul(pst[:], lhsT=wd_ap[:, jj, i * 128:(i + 1) * 128], rhs=actv[:, jj, :],
                                                         start=(jj == 0), stop=(jj == 1))) for jj in range(2)],
                            reads=[wdb, ab], writes=[psb])
                    T.op(DVE, lambda: dv.scalar_tensor_tensor(out=h_t[:, i, :], in0=pst[:], scalar=0.5, in1=h_t[:, i, :],
                                                              op0=ALU.mult, op1=ALU.add),
                         reads=[psb], writes=[hB[i]])

            for c in range(NCH):
                wg_ap, wgb = load_w(wview_k(16, 256), wgv[:, :, c * 256:(c + 1) * 256])
                wu_ap, wub = load_w(wview_k(16, 256), wuv[:, :, c * 256:(c + 1) * 256])
                wd_ap, wdb = load_w(wview_k(2, 2048), wdv[:, 2 * c:2 * c + 2, :])
                actv = a16(OFF_ACT + 512 * (c % 2), 1024).rearrange("p (j t) -> p j t", j=2)
                ab = actB[c % 2]
                for jj in range(2):
                    gps, gpb = nextps()
                    ups, upb = nextps()
                    T.group(PE, [(lambda k=k: pe.matmul(gps[:], lhsT=wg_ap[:, k, jj * 128:(jj + 1) * 128], rhs=xn_t[:, k, :],
                                                       start=(k == 0), stop=(k == KT - 1))) for k in range(KT)],
                            reads=[wgb] + xnB, writes=[gpb])
                    T.group(PE, [(lambda k=k: pe.matmul(ups[:], lhsT=wu_ap[:, k, jj * 128:(jj + 1) * 128], rhs=xn_t[:, k, :],
                                                       start=(k == 0), stop=(k == KT - 1))) for k in range(KT)],
                            reads=[wub] + xnB, writes=[upb])
                    sl = a32(OFF_SILU + 512 * jj, 512)
                    T.op(ACT, lambda: ac.activation(out=sl, in_=gps[:], func=AF.Silu), reads=[gpb], writes=[siluB[jj]])
                    T.op(DVE, lambda: dv.tensor_tensor(out=actv[:, jj, :], in0=sl, in1=ups[:], op=ALU.mult),
                         reads=[siluB[jj], upb], writes=[ab])
                if pend is not None:
                    down(*pend)
                pend = (c, wd_ap, wdb, actv, ab)
            down(*pend)
            return fb

        for blk_i in range(nblk):
            t0 = blk_i * TB
            scope(f"b{blk_i}_load")
            lb = new_phase_bufs(["xin0", "xin1", "pin"])
            xinB, pinB = lb[0:2], lb[2]
            for tt in range(4):
                xin = a32(OFF_XIN + 2048 * (tt % 2), 2048)
                T.dma(SP, xin, x_d[t0 + tt * 128:t0 + (tt + 1) * 128, :], xinB[tt % 2], writes=[xinB[tt % 2]])
                for q in range(4):
                    pst, psb = nextps()
                    T.group(PE, [(lambda kk=kk: pe.transpose(pst[:, kk * 128:(kk + 1) * 128],
                                                            xin[:, (4 * q + kk) * 128:(4 * q + kk + 1) * 128], ident_t[:]))
                                 for kk in range(4)], reads=[xinB[tt % 2], constB], writes=[psb])
                    T.op(ACT, lambda: ac.activation(out=h_t[:, 4 * q:4 * q + 4, tt * 128:(tt + 1) * 128],
                                                    in_=pst[:].rearrange("p (a b) -> p a b", a=4), func=AF.Copy),
                         reads=[psb], writes=hB[4 * q:4 * q + 4])
            pin = a32(OFF_PIN, 1024).rearrange("p (a b) -> p a b", a=4)
            T.dma(SP, pin, p_d[t0:t0 + TB, :].rearrange("(a p) c -> p a c", p=128), pinB, writes=[pinB])
            pT = a16(OFF_PT, 1024).rearrange("p (k t) -> p k t", k=2)
            for k2 in range(2):
                pst, psb = nextps()
                T.group(PE, [(lambda a=a: pe.transpose(pst[:, a * 128:(a + 1) * 128], pin[:, a, k2 * 128:(k2 + 1) * 128], ident_t[:]))
                             for a in range(4)], reads=[pinB, constB], writes=[psb])
                T.op(ACT, lambda: ac.activation(out=pT[:, k2, :], in_=pst[:], func=AF.Copy), reads=[psb], writes=[ptB])

            if stage >= 1:
                scope(f"b{blk_i}_ffn1")
                norm(hT, cols_t[:, C_G1:C_G1 + 16], xnT, D, sqB, rtB)
                fb = ffn(w1g_d, w1u_d, w1d_d)
            if stage >= 2:
                mb = new_phase_bufs(["slot0", "slot1", "slot2"] + [f"zb{i}" for i in range(8)] + [f"gu{i}" for i in range(8)]
                                    + [f"v{i}" for i in range(4)] + [f"E{i}" for i in range(8)] + ["S0", "S1"] + [f"t{i}" for i in range(6)])
                ring["slots"] = mb[0:3]
                ring["i"] = 0
                ring["offs"] = [0, 2048, 18432]
                zbB, guB, vB = mb[3:11], mb[11:19], mb[19:23]
                EBf, SB_, tB = mb[23:31], mb[31:33], mb[33:39]

                def EBc(buf, r, jj):
                    return EBf[buf * 4 + r * 2 + jj]

                def EBall(buf):
                    return EBf[buf * 4:buf * 4 + 4]
                zb = a16(OFF_ZB, 4096).rearrange("p (k t) -> p k t", k=8)
                gu = a16(OFF_GU, 4096).rearrange("p (k t) -> p k t", k=8)
                vv = a16(OFF_V, 4096).rearrange("p (a f) -> p a f", a=4)
                Ev2 = [a32(OFF_E + 2048 * i, 2048).rearrange("p (r s j t) -> p r s j t", r=2, s=NSEG, j=2) for i in range(2)]
                tmp = [a32(OFF_TMP + 512 * i, 512) for i in range(6)]
                y1t = [a16(OFF_GU + 512 * i, 512) for i in range(8)]

                def ymB(b_):
                    return [guB[2 * b_], guB[2 * b_ + 1]] if b_ < 4 else [vB[b_ - 4]]
                y1B = [ymB(i) for i in range(8)]
                scope(f"b{blk_i}_inproj")
                norm(hT, cols_t[:, C_GM:C_GM + 16], xnT, D, sqB, rtB)
                winv = win_d.rearrange("(k p) c -> p k c", p=128)
                for c in range(8):
                    w_ap, wb = load_w(wview_k(16, 256), winv[:, :, c * 256:(c + 1) * 256])
                    for jj in range(2):
                        ft = 2 * c + jj
                        pst, psb = nextps()
                        T.group(PE, [(lambda k=k: pe.matmul(pst[:], lhsT=w_ap[:, k, jj * 128:(jj + 1) * 128], rhs=xn_t[:, k, :],
                                                           start=(k == 0), stop=(k == KT - 1))) for k in range(KT)],
                                reads=[wb] + xnB, writes=[psb])
                        if ft < 8:
                            T.op(ACT, lambda: ac.activation(out=zb[:, ft, :], in_=pst[:], func=AF.Copy), reads=[psb], writes=[zbB[ft]])
                        else:
                            T.op(ACT, lambda: ac.activation(out=gu[:, ft - 8, :], in_=pst[:], func=AF.Gelu_apprx_tanh),
                                 reads=[psb], writes=[guB[ft - 8]])
                for c in range(4):
                    w_ap, wb = load_w(wview_k(16, 256), winv[:, :, 2048 + c * 256:2048 + (c + 1) * 256])
                    for tt in range(4):
                        pst, psb = nextps()
                        T.group(PE, [(lambda k=k: pe.matmul(pst[:, 0:256], lhsT=xn_t[:, k, tt * 128:(tt + 1) * 128], rhs=w_ap[:, k, :],
                                                           start=(k == 0), stop=(k == KT - 1))) for k in range(KT)],
                                reads=[wb] + xnB, writes=[psb])
                        T.op(ACT, lambda: ac.activation(out=vv[:, tt, c * 256:(c + 1) * 256], in_=pst[:, 0:256], func=AF.Gelu_apprx_tanh),
                             reads=[psb], writes=[vB[tt]])
                scope(f"b{blk_i}_gmlp")
                t1024 = a32(OFF_TMP, 1024)
                for tt in range(4):
                    st = small[:, 8:20]
                    T.op(DVE, lambda: dv.bn_stats(out=small[:, 8:14], in_=vv[:, tt, 0:512]), reads=[vB[tt]], writes=[smallB])
                    T.op(DVE, lambda: dv.bn_stats(out=small[:, 14:20], in_=vv[:, tt, 512:1024]), reads=[vB[tt]], writes=[smallB])
                    T.op(DVE, lambda: dv.bn_aggr(out=small[:, 20:22], in_=st), reads=[smallB], writes=[smallB])
                    T.op(ACT, lambda: ac.activation(out=small[:, 22:23], in_=small[:, 21:22], func=AF.Sqrt, bias=small[:, 1:2]),
                         reads=[smallB], writes=[smallB])
                    T.op(DVE, lambda: dv.reciprocal(out=small[:, 23:24], in_=small[:, 22:23]), reads=[smallB], writes=[smallB])
                    T.op(DVE, lambda: dv.tensor_scalar(out=t1024, in0=vv[:, tt, :], scalar1=small[:, 20:21], scalar2=small[:, 23:24],
                                                       op0=ALU.subtract, op1=ALU.mult),
                         reads=[vB[tt], smallB], writes=[tB[0], tB[1]])
                    T.op(DVE, lambda: dv.tensor_tensor(out=vv[:, tt, :], in0=t1024, in1=nvbc_t[:], op=ALU.mult),
                         reads=[tB[0], tB[1], constB], writes=[vB[tt]])
                for hd in range(8):
                    pst, psb = nextps()
                    T.group(PE, [(lambda tt=tt: pe.matmul(pst[:, tt * 128:(tt + 1) * 128], lhsT=vv[:, tt, hd * 128:(hd + 1) * 128],
                                                         rhs=wct_t[:, hd * 128:(hd + 1) * 128], start=True, stop=True)) for tt in range(4)],
                            reads=vB + [constB], writes=[psb])
                    tq = tmp[2 + hd % 2]
                    T.op(DVE, lambda: dv.tensor_tensor(out=tq.rearrange("p (a t) -> p a t", a=4), in0=pst[:].rearrange("p (a t) -> p a t", a=4),
                                                       in1=bsbc_t[:, hd * 128:(hd + 1) * 128].unsqueeze(1).to_broadcast([128, 4, 128]), op=ALU.add),
                         reads=[psb, constB], writes=[tB[2 + hd % 2]])
                    T.op(DVE, lambda: dv.tensor_tensor(out=gu[:, hd, :], in0=gu[:, hd, :], in1=tq, op=ALU.mult),
                         reads=[tB[2 + hd % 2]], writes=[guB[hd]])
                norm([(gu[:, i, :], guB[i]) for i in range(8)], cols_t[:, C_GG:C_GG + 8], xnT[8:16], DSSM, sqB, rtB)
                scope(f"b{blk_i}_ssm")
                ymain = a32(OFF_GU, 4096).rearrange("p (k t) -> p k t", k=8)
                ypsd = {}

                def st_A(b2):
                    Eb, eb = Ev2[b2 % 2], b2 % 2
                    bt_ = b2 // 2
                    for jj in range(2):
                        j = 2 * (b2 % 2) + jj
                        g_ = 2 * b2 + jj
                        dre, dreb = nextps()
                        dim_, dimb = nextps()
                        T.op(PE, lambda: pe.matmul(dre[:], lhsT=bbT[32 * j:32 * j + 32, bt_, 0, :], rhs=zb[32 * j:32 * j + 32, bt_, :],
                                                   start=True, stop=True, tile_position=(32 * j, 0)), reads=[zbB[bt_], constB], writes=[dreb])
                        T.op(PE, lambda: pe.matmul(dim_[:], lhsT=bbT[32 * j:32 * j + 32, bt_, 1, :], rhs=zb[32 * j:32 * j + 32, bt_, :],
                                                   start=True, stop=True, tile_position=(32 * j, 0)), reads=[zbB[bt_], constB], writes=[dimb])
                        cb = cosT[:, g_, :].unsqueeze(1).to_broadcast([128, NSEG, SEG])
                        sbb = sinT[:, g_, :].unsqueeze(1).to_broadcast([128, NSEG, SEG])
                        d3r = dre[:].rearrange("p (s t) -> p s t", s=NSEG)
                        d3i = dim_[:].rearrange("p (s t) -> p s t", s=NSEG)
                        ta, tb_ = 2 + 2 * jj, 3 + 2 * jj
                        r1 = tmp[ta].rearrange("p (s t) -> p s t", s=NSEG)
                        r3 = tmp[tb_].rearrange("p (s t) -> p s t", s=NSEG)
                        ere, eim = Eb[:, 0, :, jj, :], Eb[:, 1, :, jj, :]
                        T.op(DVE, lambda: dv.tensor_tensor(out=ere, in0=d3r, in1=cb, op=ALU.mult), reads=[dreb, constB], writes=[EBc(eb, 0, jj)])
                        T.op(DVE, lambda: dv.tensor_tensor(out=r1, in0=d3i, in1=sbb, op=ALU.mult), reads=[dimb, constB], writes=[tB[ta]])
                        T.op(DVE, lambda: dv.tensor_tensor(out=eim, in0=d3i, in1=cb, op=ALU.mult), reads=[dimb, constB], writes=[EBc(eb, 1, jj)])
                        T.op(DVE, lambda: dv.tensor_tensor(out=r3, in0=d3r, in1=sbb, op=ALU.mult), reads=[dreb, constB], writes=[tB[tb_]])
                        T.op(POOL, lambda: gp.tensor_tensor(out=ere, in0=ere, in1=r1, op=ALU.add), reads=[tB[ta]], writes=[EBc(eb, 0, jj)])
                        T.op(POOL, lambda: gp.tensor_tensor(out=eim, in0=eim, in1=r3, op=ALU.subtract), reads=[tB[tb_]], writes=[EBc(eb, 1, jj)])

                def st_B(b2):
                    Eb, eb = Ev2[b2 % 2], b2 % 2
                    all4 = EBall(eb)
                    mt2 = mtab[:, 2 * b2:2 * b2 + 2, :].rearrange("p g t -> p (g t)")
                    rc = rotm[:, 0, :, 2 * b2:2 * b2 + 2]
                    rs = rotm[:, 1, :, 2 * b2:2 * b2 + 2]
                    for s_ in range(NSEG):
                        if s_ > 0:
                            last = Eb[:, :, s_ - 1, :, SEG - 1]
                            i1 = small[:, 24:28].rearrange("p (r j) -> p r j", r=2)
                            i2 = small[:, 32:36].rearrange("p (r j) -> p r j", r=2)
                            T.op(DVE, lambda: dv.tensor_tensor(out=i1, in0=last, in1=rc, op=ALU.mult), reads=all4 + [constB], writes=[smallB])
                            T.op(DVE, lambda: dv.tensor_tensor(out=i2[:, 0, :], in0=last[:, 1, :], in1=rs[:, 0, :], op=ALU.mult),
                                 reads=all4 + [constB], writes=[smallB])
                            T.op(DVE, lambda: dv.tensor_tensor(out=i2[:, 1, :], in0=last[:, 0, :], in1=rs[:, 1, :], op=ALU.mult),
                                 reads=all4 + [constB], writes=[smallB])
                            T.op(DVE, lambda: dv.tensor_tensor(out=i1, in0=i1, in1=i2, op=ALU.add), reads=[smallB], writes=[smallB])
                            T.op(DVE, lambda: dv.tensor_tensor(out=Eb[:, :, s_, :, 0], in0=Eb[:, :, s_, :, 0], in1=i1, op=ALU.add),
                                 reads=[smallB], writes=all4)
                        for r in range(2):
                            er = Eb[:, r, s_, :, :].rearrange("p j t -> p (j t)")
                            T.op(DVE, lambda: dv.tensor_tensor_scan(out=er, data0=mt2, data1=er, initial=0.0, op0=ALU.mult, op1=ALU.add),
                                 reads=[constB], writes=[EBc(eb, r, 0), EBc(eb, r, 1)])
                    T.op(DVE, lambda: dv.tensor_copy(out=carry[:, :, 2 * b2:2 * b2 + 2], in_=Eb[:, :, NSEG - 1, :, SEG - 1]),
                         reads=all4, writes=[carryB[b2 // 2]])

                def st_C(b2):
                    Eb, eb = Ev2[b2 % 2], b2 % 2
                    bt_ = b2 // 2
                    if b2 % 2 == 0:
                        ypsd[bt_] = nextps()
                    yps, ypb = ypsd[bt_]
                    u0 = tmp[0].rearrange("p (s t) -> p s t", s=NSEG)
                    u1 = tmp[1].rearrange("p (s t) -> p s t", s=NSEG)
                    for jj in range(2):
                        j = 2 * (b2 % 2) + jj
                        g_ = 2 * b2 + jj
                        cb = cosT[:, g_, :].unsqueeze(1).to_broadcast([128, NSEG, SEG])
                        sbb = sinT[:, g_, :].unsqueeze(1).to_broadcast([128, NSEG, SEG])
                        rre, rim = Eb[:, 0, :, jj, :], Eb[:, 1, :, jj, :]
                        sv = a16(OFF_S + 512 * jj, 1024).rearrange("p (r t) -> p r t", r=2)
                        s3 = [sv[:, r, :].rearrange("p (s t) -> p s t", s=NSEG) for r in range(2)]
                        T.op(POOL, lambda: gp.tensor_tensor(out=u0, in0=rre, in1=cb, op=ALU.mult), reads=[EBc(eb, 0, jj), constB], writes=[tB[0]])
                        T.op(POOL, lambda: gp.tensor_tensor(out=u1, in0=rim, in1=sbb, op=ALU.mult), reads=[EBc(eb, 1, jj), constB], writes=[tB[1]])
                        T.op(POOL, lambda: gp.tensor_tensor(out=s3[0], in0=u0, in1=u1, op=ALU.subtract), reads=[tB[0], tB[1]], writes=[SB_[jj]])
                        T.op(POOL, lambda: gp.tensor_tensor(out=u0, in0=rre, in1=sbb, op=ALU.mult), reads=[EBc(eb, 0, jj), constB], writes=[tB[0]])
                        T.op(POOL, lambda: gp.tensor_tensor(out=u1, in0=rim, in1=cb, op=ALU.mult), reads=[EBc(eb, 1, jj), constB], writes=[tB[1]])
                        T.op(POOL, lambda: gp.tensor_tensor(out=s3[1], in0=u0, in1=u1, op=ALU.add), reads=[tB[0], tB[1]], writes=[SB_[jj]])
                        T.group(PE, [(lambda r=r: pe.matmul(yps[32 * j:32 * j + 32, :], lhsT=c32b[:, g_, r, :], rhs=sv[:, r, :],
                                                           start=(r == 0), stop=(r == 1), tile_position=(0, 32 * j))) for r in range(2)],
                                reads=[SB_[jj], constB], writes=[ypb])
                    if b2 % 2 == 1:
                        T.op(DVE, lambda: dv.scalar_tensor_tensor(out=ymain[:, bt_, :], in0=zb[:, bt_, :], scalar=cols_t[:, C_DD + bt_:C_DD + bt_ + 1],
                                                                  in1=yps[:], op0=ALU.mult, op1=ALU.add),
                             reads=[zbB[bt_], ypb, constB], writes=ymB(bt_))

                NB2 = 16
                for it in range(NB2 + 1):
                    if it < NB2:
                        st_A(it)
                    if 0 <= it - 1 < NB2:
                        st_B(it - 1)
                        st_C(it - 1)

                def SV(fn, reads=(), writes=()):
                    return T.op(DVE, fn, reads=list(reads) + [ssmsB, constB], writes=list(writes) + [ssmsB])
                R_ = [ssms[:, i, :] for i in range(12)]
                c63, s63 = cosT[:, :, SEG - 1], sinT[:, :, SEG - 1]
                lfr, lfi = carry[:, 0, :], carry[:, 1, :]
                sllr, slli, spr, spi, sinr, sini, q1, q2 = R_[0], R_[1], R_[2], R_[3], R_[4], R_[5], R_[6], R_[7]
                SV(lambda: dv.tensor_tensor(out=q1, in0=lfr, in1=c63, op=ALU.mult), reads=carryB)
                SV(lambda: dv.tensor_tensor(out=q2, in0=lfi, in1=s63, op=ALU.mult), reads=carryB)
                SV(lambda: dv.tensor_tensor(out=sllr, in0=q1, in1=q2, op=ALU.subtract))
                SV(lambda: dv.tensor_tensor(out=q1, in0=lfr, in1=s63, op=ALU.mult), reads=carryB)
                SV(lambda: dv.tensor_tensor(out=q2, in0=lfi, in1=c63, op=ALU.mult), reads=carryB)
                SV(lambda: dv.tensor_tensor(out=slli, in0=q1, in1=q2, op=ALU.add))

                def sout_from(xr, xi, outr, outi, extra_r=(), extra_w=()):
                    SV(lambda: dv.tensor_tensor(out=q1, in0=a512[:, 0, :], in1=xr, op=ALU.mult), reads=extra_r)
                    SV(lambda: dv.tensor_tensor(out=q2, in0=a512[:, 1, :], in1=xi, op=ALU.mult), reads=extra_r)
                    SV(lambda: dv.tensor_tensor(out=q1, in0=q1, in1=q2, op=ALU.subtract))
                    SV(lambda: dv.tensor_tensor(out=q1, in0=q1, in1=sllr, op=ALU.add))
                    SV(lambda: dv.tensor_tensor(out=q2, in0=a512[:, 0, :], in1=xi, op=ALU.mult), reads=extra_r)
                    SV(lambda: dv.tensor_tensor(out=R_[8], in0=a512[:, 1, :], in1=xr, op=ALU.mult), reads=extra_r)
                    SV(lambda: dv.tensor_tensor(out=q2, in0=q2, in1=R_[8], op=ALU.add))
                    SV(lambda: dv.tensor_tensor(out=outi, in0=q2, in1=slli, op=ALU.add), writes=extra_w)
                    SV(lambda: dv.tensor_copy(out=outr, in_=q1), writes=extra_w)

                sout_from(sprev[:, 0, :], sprev[:, 1, :], spr, spi, extra_r=[sprevB])
                ccinB, ccoutB = Buf("ccin"), Buf("ccout")
                T.dma(SP, ccin_d[blk_i].ap(), ssms[:, 2:4, :].rearrange("p a b -> p (a b)"), ccinB, reads=[ssmsB], writes=[ccinB])
                T.custom(POOL, lambda: gp.collective_compute("AllGather", ALU.bypass, replica_groups=[[0, 1], [2, 3], [4, 5], [6, 7]],
                                                             ins=[ccin_d[blk_i].ap().opt()], outs=[ccout_d[blk_i].ap().opt()]),
                         f"cc{blk_i}", reads=[ccinB], writes=[ccoutB])
                T.dma(SP, gbuf[:], ccout_d[blk_i].ap().rearrange("(r p) n -> p r n", p=128), gB, reads=[ccoutB], writes=[gB])
                fl = flags_t[:, 3 * blk_i:3 * blk_i + 3]
                sin64 = ssms[:, 4:6, :].rearrange("p a b -> p (a b)")
                sp64 = sprev[:].rearrange("p a b -> p (a b)")
                SV(lambda: dv.tensor_scalar(out=sin64, in0=sp64, scalar1=fl[:, 0:1], scalar2=None, op0=ALU.mult), reads=[sprevB])
                SV(lambda: dv.scalar_tensor_tensor(out=sin64, in0=gbuf[:, 0, :], scalar=fl[:, 1:2], in1=sin64, op0=ALU.mult, op1=ALU.add), reads=[gB])
                SV(lambda: dv.scalar_tensor_tensor(out=sin64, in0=gbuf[:, 1, :], scalar=fl[:, 2:3], in1=sin64, op0=ALU.mult, op1=ALU.add), reads=[gB])
                sout_from(sinr, sini, sprev[:, 0, :], sprev[:, 1, :], extra_w=[sprevB])
                sinr_b = sinr.unsqueeze(1).to_broadcast([128, NSEG, 32])
                sini_b = sini.unsqueeze(1).to_broadcast([128, NSEG, 32])
                kq = [a32(OFF_TMP + 256 * i, 256).rearrange("p (s g) -> p s g", s=NSEG) for i in range(2)]
                SV(lambda: dv.tensor_tensor(out=kq[0], in0=ttab[:, 0], in1=sinr_b, op=ALU.mult), writes=[tB[0]])
                SV(lambda: dv.tensor_tensor(out=kq[1], in0=ttab[:, 1], in1=sini_b, op=ALU.mult), writes=[tB[0]])
                SV(lambda: dv.tensor_tensor(out=kbuf[:, 0], in0=kq[0], in1=kq[1], op=ALU.subtract), reads=[tB[0]], writes=[kB])
                SV(lambda: dv.tensor_tensor(out=kq[0], in0=ttab[:, 0], in1=sini_b, op=ALU.mult), writes=[tB[0]])
                SV(lambda: dv.tensor_tensor(out=kq[1], in0=ttab[:, 1], in1=sinr_b, op=ALU.mult), writes=[tB[0]])
                SV(lambda: dv.tensor_tensor(out=kbuf[:, 1], in0=kq[0], in1=kq[1], op=ALU.add), reads=[tB[0]], writes=[kB])
                w4 = [a32(OFF_E + 1024 * i, 1024).rearrange("p (j s c) -> p j s c", j=4, s=NSEG) for i in range(4)]
                w4B = [Buf(f"w4_{i}") for i in range(4)]
                inh = {}
                for b_ in EBf:
                    for k_, v_ in ([b_.w] if b_.w else []) + list(b_.r.items()):
                        if inh.get(k_, 0) < v_:
                            inh[k_] = v_
                for b_ in w4B:
                    b_.r = dict(inh)
                arena_live.extend(w4B)
                for bt_ in range(8):
                    wv = wbuf[:, bt_ % 2]
                    wb_ = wB[bt_ % 2]
                    CR = c32b[:, 4 * bt_:4 * bt_ + 4, 0, :].unsqueeze(2).to_broadcast([128, 4, NSEG, 32])
                    CN = c32b[:, 4 * bt_:4 * bt_ + 4, 1, :].unsqueeze(2).to_broadcast([128, 4, NSEG, 32])
                    Kr = kbuf[:, 0, :, 4 * bt_:4 * bt_ + 4].rearrange("p s j -> p j s").unsqueeze(3).to_broadcast([128, 4, NSEG, 32])
                    Ki = kbuf[:, 1, :, 4 * bt_:4 * bt_ + 4].rearrange("p s j -> p j s").unsqueeze(3).to_broadcast([128, 4, NSEG, 32])
                    T.op(DVE, lambda: dv.tensor_tensor(out=w4[0], in0=CR, in1=Kr, op=ALU.mult), reads=[kB, constB], writes=[w4B[0]])
                    T.op(DVE, lambda: dv.tensor_tensor(out=w4[1], in0=CN, in1=Ki, op=ALU.mult), reads=[kB, constB], writes=[w4B[1]])
                    T.op(DVE, lambda: dv.tensor_tensor(out=wv[:, 0], in0=w4[0], in1=w4[1], op=ALU.add), reads=[w4B[0]], writes=[wb_])
                    T.op(POOL, lambda: gp.tensor_tensor(out=w4[2], in0=CN, in1=Kr, op=ALU.mult), reads=[kB, constB], writes=[w4B[2]])
                    T.op(POOL, lambda: gp.tensor_tensor(out=w4[3], in0=CR, in1=Ki, op=ALU.mult), reads=[kB, constB], writes=[w4B[3]])
                    T.op(POOL, lambda: gp.tensor_tensor(out=wv[:, 1], in0=w4[2], in1=w4[3], op=ALU.subtract), reads=[w4B[2]], writes=[wb_])
                    cps, cpb = nextps()
                    fns = []
                    for j in range(4):
                        for s_ in range(NSEG):
                            for r in range(2):
                                fns.append(lambda j=j, s_=s_, r=r: pe.matmul(cps[32 * j:32 * j + 32, SEG * s_:SEG * (s_ + 1)], lhsT=wv[:, r, j, s_, :],
                                                                            rhs=ptab[:, 4 * bt_ + j, r, :], start=(r == 0), stop=(r == 1),
                                                                            tile_position=(0, 32 * j)))
                    T.group(PE, fns, reads=[wb_, constB], writes=[cpb])
                    T.op(DVE, lambda: dv.tensor_tensor(out=tmp[2 + bt_ % 2], in0=ymain[:, bt_, :], in1=cps[:], op=ALU.add),
                         reads=ymB(bt_) + [cpb], writes=[tB[2 + bt_ % 2]])
                    T.op(ACT, lambda: ac.activation(out=y1t[bt_], in_=tmp[2 + bt_ % 2], func=AF.Gelu_apprx_tanh),
                         reads=[tB[2 + bt_ % 2]], writes=y1B[bt_])
                scope(f"b{blk_i}_glu_out")
                wgluv = wglu_d.rearrange("(k p) c -> p k c", p=128)
                for c in range(4):
                    w_ap, wb = load_w(wview_k(8, 256), wgluv[:, :, c * 256:(c + 1) * 256])
                    for jj in range(2):
                        i = 2 * c + jj
                        pst, psb = nextps()
                        T.group(PE, [(lambda k=k: pe.matmul(pst[:], lhsT=w_ap[:, k, jj * 128:(jj + 1) * 128], rhs=y1t[k],
                                                           start=(k == 0), stop=(k == 7))) for k in range(8)],
                                reads=[wb] + [x_ for l_ in y1B for x_ in l_], writes=[psb])
                        T.op(ACT, lambda: ac.activation(out=tmp[4 + i % 2], in_=pst[:], func=AF.Sigmoid), reads=[psb], writes=[tB[4 + i % 2]])
                        T.op(DVE, lambda: dv.tensor_tensor(out=zb[:, i, :], in0=y1t[i], in1=tmp[4 + i % 2], op=ALU.mult),
                             reads=y1B[i] + [tB[4 + i % 2]], writes=[zbB[i]])
                norm([(zb[:, i, :], zbB[i]) for i in range(8)], cols_t[:, C_GS:C_GS + 8], xnT[0:8], DSSM, sqB, rtB)
                woutv = wout_d.rearrange("(k p) c -> p k c", p=128)
                for c in range(8):
                    w_ap, wb = load_w(wview_k(16, 256), woutv[:, :, c * 256:(c + 1) * 256])
                    for jj in range(2):
                        i = 2 * c + jj
                        pst, psb = nextps()
                        T.group(PE, [(lambda k=k: pe.matmul(pst[:], lhsT=w_ap[:, k, jj * 128:(jj + 1) * 128], rhs=xn_t[:, k, :],
                                                           start=(k == 0), stop=(k == KT - 1))) for k in range(KT)],
                                reads=[wb] + xnB, writes=[psb])
                        T.op(DVE, lambda: dv.tensor_tensor(out=h_t[:, i, :], in0=pst[:], in1=h_t[:, i, :], op=ALU.add), reads=[psb], writes=[hB[i]])
            if stage >= 3:
                scope(f"b{blk_i}_ffn2")
                norm(hT, cols_t[:, C_G2:C_G2 + 16], xnT, D, sqB, rtB)
                fb = ffn(w2g_d, w2u_d, w2d_d)
            if stage >= 4:
                scope(f"b{blk_i}_ple")
                norm(hT, cols_t[:, C_GP:C_GP + 16], xnT, D, sqB, rtB)
                wpgv = wpg_d.rearrange("(k p) c -> p k c", p=128)
                wppb = fb[12]
                wpp_ap = a16(18432, 4096).rearrange("p (k c) -> p k c", k=2)
                T.dma(POOL, wpp_ap, wpp_d.rearrange("(k p) c -> p k c", p=128), wppb, writes=[wppb])
                siluB = fb[10:12]
                for c in range(8):
                    w_ap, wb = load_w(wview_k(16, 256), wpgv[:, :, c * 256:(c + 1) * 256])
                    for jj in range(2):
                        i = 2 * c + jj
                        gps, gpb = nextps()
                        pps, ppb = nextps()
                        T.group(PE, [(lambda k=k: pe.matmul(gps[:], lhsT=w_ap[:, k, jj * 128:(jj + 1) * 128], rhs=xn_t[:, k, :],
                                                           start=(k == 0), stop=(k == KT - 1))) for k in range(KT)],
                                reads=[wb] + xnB, writes=[gpb])
                        T.group(PE, [(lambda k=k: pe.matmul(pps[:], lhsT=wpp_ap[:, k, i * 128:(i + 1) * 128], rhs=pT[:, k, :],
                                                           start=(k == 0), stop=(k == 1))) for k in range(2)],
                                reads=[wppb, ptB], writes=[ppb])
                        sl = a32(OFF_SILU + 512 * jj, 512)
                        T.op(ACT, lambda: ac.activation(out=sl, in_=gps[:], func=AF.Sigmoid), reads=[gpb], writes=[siluB[jj]])
                        T.op(DVE, lambda: dv.tensor_tensor(out=sl, in0=sl, in1=pps[:], op=ALU.mult), reads=[ppb], writes=[siluB[jj]])
                        T.op(DVE, lambda: dv.tensor_tensor(out=h_t[:, i, :], in0=sl, in1=h_t[:, i, :], op=ALU.add), reads=[siluB[jj]], writes=[hB[i]])
            scope(f"b{blk_i}_store")
            ob = new_phase_bufs(["ost0", "ost1"])
            ostB = ob[0:2]
            norm(hT, cols_t[:, C_GF:C_GF + 16], hT, D, sqB, rtB)
            for tt in range(4):
                ost = a32(OFF_OST + 2048 * (tt % 2), 2048)
                for q in range(4):
                    pst, psb = nextps()
                    T.group(PE, [(lambda kk=kk: pe.transpose(pst[:, kk * 128:(kk + 1) * 128],
                                                            h_t[:, 4 * q + kk, tt * 128:(tt + 1) * 128], ident_t[:]))
                                 for kk in range(4)], reads=hB[4 * q:4 * q + 4] + [constB], writes=[psb])
                    T.op(ACT, lambda: ac.activation(out=ost[:, q * 512:(q + 1) * 512], in_=pst[:], func=AF.Copy),
                         reads=[psb], writes=[ostB[tt % 2]])
                T.dma(SP, out_d[t0 + tt * 128:t0 + (tt + 1) * 128, :], ost, ostB[tt % 2], reads=[ostB[tt % 2]])
        scope(None)
        for nm in ("ost0", "ost1"):
            ent = T.dsem_by_name[nm]
            SP.e.wait_ge(T.sems[ent[0]], ent[1])
        nc.n_sems_used = len(T.sems)
    return nc


def _host_consts(inp):
    f32 = np.float32
    L = 0

    def col16(v):
        return np.ascontiguousarray(v.reshape(-1, 128).T)

    cols = np.zeros((128, NCOL), f32)
    cols[:, C_G1:C_G1 + 16] = col16(inp["norm_ffn1"][L])
    cols[:, C_GM:C_GM + 16] = col16(inp["norm_mix"][L])
    cols[:, C_G2:C_G2 + 16] = col16(inp["norm_ffn2"][L])
    cols[:, C_GP:C_GP + 16] = col16(inp["norm_ple"][L])
    cols[:, C_GF:C_GF + 16] = col16(inp["norm_final"])
    cols[:, C_GS:C_GS + 8] = col16(inp["norm_ssm_out"][L])
    cols[:, C_GG:C_GG + 8] = col16(inp["norm_gmlp_out"][L])
    cols[:, C_DD:C_DD + 8] = col16(inp["ssm_d"][L])
    nvbc = np.ascontiguousarray(np.broadcast_to(inp["gmlp_norm_v"][L][None, :], (128, 1024))).astype(f32)
    bsbc = np.ascontiguousarray(np.broadcast_to(inp["gmlp_b_s"][L].reshape(1, 1024), (128, 1024))).astype(f32)
    wst = np.ascontiguousarray(inp["gmlp_w_s"][L].transpose(2, 0, 1).reshape(128, 1024)).astype(f32)
    mask = np.triu(np.ones((128, 128), f32))
    ident = np.eye(128, dtype=f32)
    tau = np.ascontiguousarray(np.broadcast_to(np.arange(SEG, dtype=f32)[None, :], (128, SEG)))

    def alay(a):
        return a.reshape(32, 2, 64).transpose(1, 2, 0).reshape(128, 32)

    ldt = np.broadcast_to(inp["ssm_log_dt"][L][:, None], (64, 64))
    ssa = np.concatenate([alay(ldt), alay(inp["ssm_a_re"][L]), alay(inp["ssm_a_im"][L])], axis=1).astype(f32)

    def blay(b):
        return b.reshape(32, 2, 64, 16).transpose(1, 2, 0, 3).reshape(128, 512)

    ssb = np.stack([blay(inp["ssm_b_re"][L]), blay(inp["ssm_b_im"][L])], axis=1).astype(f32)

    def clay(c):
        c4 = c.reshape(32, 2, 16, 64)
        o = np.zeros((2, 64, 32, 2, 16), f32)
        for g2 in range(2):
            o[g2, :, :, g2, :] = c4[:, g2, :, :].transpose(2, 0, 1)
        return o.reshape(128, 1024)

    ssc = np.stack([clay(inp["ssm_c_re"][L]), clay(inp["ssm_c_im"][L])], axis=1).astype(f32)
    return dict(cols=cols, nvbc=nvbc, bsbc=bsbc, wst=wst, mask=mask, ident=ident, tau=tau, ssa=ssa, ssb=ssb, ssc=ssc)


_NC_CACHE = {}
BLOCKS_OF = [[0, 3, 4, 7], [1, 2, 5, 6]]
FIRST_RANK = [0, 1, 0, 1]


def _flags(rank):
    f = np.zeros((128, 12), np.float32)
    for k in range(4):
        if FIRST_RANK[k] == rank:
            f[:, 3 * k + 0] = 1.0
        elif rank == 0:
            f[:, 3 * k + 2] = 1.0
        else:
            f[:, 3 * k + 1] = 1.0
    return f


def kernel(**inp):
    L = 0
    consts = _host_consts(inp)
    shared = dict(
        w1g=np.ascontiguousarray(inp["w1_gate"][L]), w1u=np.ascontiguousarray(inp["w1_up"][L]), w1d=np.ascontiguousarray(inp["w1_down"][L]),
        w2g=np.ascontiguousarray(inp["w2_gate"][L]), w2u=np.ascontiguousarray(inp["w2_up"][L]), w2d=np.ascontiguousarray(inp["w2_down"][L]),
        win=np.ascontiguousarray(inp["w_in"][L]), wglu=np.ascontiguousarray(inp["ssm_w_glu"][L]), wout=np.ascontiguousarray(inp["w_out"][L]),
        wpg=np.ascontiguousarray(inp["w_ple_gate"][L]), wpp=np.ascontiguousarray(inp["w_ple_proj"][L]), **consts)
    nblk = 4
    key = ("v2", nblk)
    if key not in _NC_CACHE:
        _NC_CACHE[key] = build_nc(nblk, use_cc=True)
    nc = _NC_CACHE[key]
    x = inp["x"]
    p = inp["p"][L]
    in_maps = []
    for c in range(N_CORES):
        b, r = c // 2, c % 2
        m = dict(shared)
        m["x"] = np.ascontiguousarray(np.concatenate([x[b, g * TB:(g + 1) * TB] for g in BLOCKS_OF[r]], axis=0))
        m["p"] = np.ascontiguousarray(np.concatenate([p[b, g * TB:(g + 1) * TB] for g in BLOCKS_OF[r]], axis=0))
        m["flags"] = _flags(r)
        in_maps.append(m)
    res = run_bass_kernel_spmd(nc, in_maps, core_ids=list(range(N_CORES)))
    out = np.empty((4, SEQ, D), np.float32)
    for c in range(N_CORES):
        b, r = c // 2, c % 2
        o = res.results[c]["out"]
        for k, g in enumerate(BLOCKS_OF[r]):
            out[b, g * TB:(g + 1) * TB] = o[k * TB:(k + 1) * TB]
    return out
```

```python
import contextlib
import math
import numpy as np
import concourse.bass as bass
import concourse.mybir as mybir
from concourse.bass_utils import run_bass_kernel_spmd

F32 = mybir.dt.float32
BF16 = mybir.dt.bfloat16
I32 = mybir.dt.int32
AF = mybir.ActivationFunctionType
ALU = mybir.AluOpType

D = 2048
DFF = 5632
DSSM = 1024
TB = 512
KT = D // 128
SEG = 64
NSEG = TB // SEG
EPS = 1e-6
N_CORES = 8
SEQ = 4096

C_G1, C_GM, C_G2, C_GP, C_GF = 0, 16, 32, 48, 64
C_GS, C_GG, C_DD = 80, 88, 96
NCOL = 104


class Buf:
    __slots__ = ("name", "w", "r", "dsem", "dcnt")

    def __init__(self, name):
        self.name = name
        self.w = None
        self.r = {}
        self.dsem = None
        self.dcnt = 0


class Eng:
    def __init__(self, e, semidx):
        self.e = e
        self.semidx = semidx
        self.n = 0
        self.known = {}


class Tracker:
    def __init__(self, nc, es):
        self.nc = nc
        self.es = es
        self.sems = []
        self.dsem_by_name = {}

    def new_sem(self, name):
        s = self.es.enter_context(self.nc.semaphore(name))
        self.sems.append(s)
        return len(self.sems) - 1

    def eng(self, e, name):
        return Eng(e, self.new_sem("e_" + name))

    def _deps(self, reads, writes):
        deps = {}

        def add(k, v):
            if deps.get(k, 0) < v:
                deps[k] = v
        for b in reads:
            if b.w is not None:
                add(*b.w)
        for b in writes:
            if b.w is not None:
                add(*b.w)
            for k, v in b.r.items():
                add(k, v)
        return deps

    def _wait(self, E, deps):
        for k, v in deps.items():
            if E.known.get(k, 0) < v:
                E.e.wait_ge(self.sems[k], v)
                E.known[k] = v

    def _mark(self, tok, reads, writes):
        k, v = tok
        for b in reads:
            if b.r.get(k, 0) < v:
                b.r[k] = v
        for b in writes:
            b.w = tok
            b.r = {}

    def op(self, E, fn, reads=(), writes=()):
        self._wait(E, self._deps(reads, writes))
        ins = fn()
        E.n += 1
        ins.then_inc(self.sems[E.semidx], 1)
        tok = (E.semidx, E.n)
        self._mark(tok, reads, writes)
        return tok

    def group(self, E, fns, reads=(), writes=()):
        self._wait(E, self._deps(reads, writes))
        ins = None
        for fn in fns:
            ins = fn()
        E.n += 1
        ins.then_inc(self.sems[E.semidx], 1)
        tok = (E.semidx, E.n)
        self._mark(tok, reads, writes)
        return tok

    def custom(self, E, fn, name, reads=(), writes=()):
        self._wait(E, self._deps(reads, writes))
        ins = fn()
        k = self.new_sem(name)
        ins.then_inc(self.sems[k], 1)
        tok = (k, 1)
        self._mark(tok, reads, writes)
        return tok

    def dma(self, E, out, in_, dbuf, reads=(), writes=(), **kw):
        if dbuf.name not in self.dsem_by_name:
            self.dsem_by_name[dbuf.name] = [self.new_sem("d_" + dbuf.name), 0]
        ent = self.dsem_by_name[dbuf.name]
        self._wait(E, self._deps(reads, writes))
        ins = E.e.dma_start(out=out, in_=in_, **kw)
        ent[1] += 16
        ins.then_inc(self.sems[ent[0]], 16)
        tok = (ent[0], ent[1])
        self._mark(tok, reads, writes)
        return tok


def build_nc(nblk, stage=99, use_cc=False):
    ntok = nblk * TB
    nc = bass.Bass("TRN2", target_bir_lowering=False)

    def din(name, shape):
        return nc.dram_tensor(name, list(shape), F32, kind="ExternalInput").ap()

    x_d = din("x", [ntok, D])
    p_d = din("p", [ntok, 256])
    out_d = nc.dram_tensor("out", [ntok, D], F32, kind="ExternalOutput").ap()
    w1g_d, w1u_d, w1d_d = din("w1g", [D, DFF]), din("w1u", [D, DFF]), din("w1d", [DFF, D])
    w2g_d, w2u_d, w2d_d = din("w2g", [D, DFF]), din("w2u", [D, DFF]), din("w2d", [DFF, D])
    win_d = din("win", [D, 3072])
    wglu_d = din("wglu", [DSSM, DSSM])
    wout_d = din("wout", [D, D])
    wpg_d = din("wpg", [D, D])
    wpp_d = din("wpp", [256, D])
    cols_d = din("cols", [128, NCOL])
    nvbc_d = din("nvbc", [128, 1024])
    bsbc_d = din("bsbc", [128, 1024])
    wst_d = din("wst", [128, 1024])
    mask_d = din("mask", [128, 128])
    ident_d = din("ident", [128, 128])
    tau_d = din("tau", [128, SEG])
    ssa_d = din("ssa", [128, 96])
    ssb_d = din("ssb", [128, 2, 512])
    ssc_d = din("ssc", [128, 2, 1024])
    if use_cc:
        flags_d = din("flags", [128, 3 * nblk])
        ccin_d = [nc.dram_tensor(f"cc_in{k}", [128, 64], F32) for k in range(nblk)]
        ccout_d = [nc.dram_tensor(f"cc_out{k}", [256, 64], F32) for k in range(nblk)]

    es = contextlib.ExitStack()
    with es:
        T = Tracker(nc, es)
        PE = T.eng(nc.tensor, "pe")
        ACT = T.eng(nc.scalar, "act")
        DVE = T.eng(nc.vector, "dve")
        POOL = T.eng(nc.gpsimd, "pool")
        SP = T.eng(nc.sync, "sp")

        def sb(name, shape, dt):
            return es.enter_context(nc.sbuf_tensor("sb_" + name, list(shape), dt))

        h_t = sb("h", [128, KT, TB], F32)
        xn_t = sb("xn", [128, KT, TB], BF16)
        hB = [Buf(f"h{i}") for i in range(KT)]
        xnB = [Buf(f"xn{i}") for i in range(KT)]
        cols_t = sb("cols", [128, NCOL], F32)
        ident_t = sb("ident", [128, 128], F32)
        ones_t = sb("ones", [128, 128], BF16)
        nvbc_t = sb("nvbc", [128, 1024], F32)
        bsbc_t = sb("bsbc", [128, 1024], F32)
        wct_t = sb("wct", [128, 1024], BF16)
        cosT = sb("cosT", [128, 32, SEG], F32)
        sinT = sb("sinT", [128, 32, SEG], F32)
        mtab = sb("mtab", [128, 32, SEG], F32)
        bbT = sb("bbT", [128, 8, 2, 128], BF16)
        c32b = sb("c32b", [128, 32, 2, 32], BF16)
        rotm = sb("rotm", [128, 2, 2, 32], F32)
        carry = sb("carry", [128, 2, 32], F32)
        small = sb("small", [128, 64], F32)
        ptab = sb("ptab", [128, 32, 2, SEG], BF16)
        ttab = sb("ttab", [128, 2, NSEG, 32], F32)
        a512 = sb("a512", [128, 2, 32], F32)
        sprev = sb("sprev", [128, 2, 32], F32)
        ssms = sb("ssms", [128, 12, 32], F32)
        gbuf = sb("gbuf", [128, 2, 64], F32)
        kbuf = sb("kbuf", [128, 2, NSEG, 32], F32)
        wbuf = sb("wbuf", [128, 2, 2, 4, NSEG, 32], BF16)
        flags_t = sb("flags", [128, 3 * nblk], F32)
        ssmsB, gB, kB = Buf("ssms"), Buf("gbuf"), Buf("kbuf")
        wB = [Buf("wbuf0"), Buf("wbuf1")]
        sprevB = Buf("sprev")
        constB = Buf("const")
        carryB = [Buf(f"carry{i}") for i in range(8)]
        smallB = Buf("small")

        ARENA_W = 22528
        arena = sb("arena", [128, ARENA_W], F32)
        SLOT_W = 2048

        def a32(off, n):
            return arena[:, off:off + n]

        def a16(off, n_bf):
            return arena[:, off:off + n_bf // 2].bitcast(BF16)

        NSLOT_FFN = 8
        slot_off = [i * SLOT_W for i in range(NSLOT_FFN)]
        OFF_ACT = 16384
        OFF_SILU = 17408
        OFF_XIN = 0
        OFF_PIN = 4096
        OFF_OST = 4096
        OFF_ZB = 4096
        OFF_GU = OFF_ZB + 2048
        OFF_V = OFF_GU + 2048
        OFF_E = OFF_V + 2048
        OFF_S = OFF_E + 4096
        OFF_TMP = OFF_S + 1024
        assert OFF_TMP + 3072 == 18432
        OFF_SQ = 20480
        OFF_RT = 20992
        OFF_PT = 22016
        sqB = [Buf("sq0"), Buf("sq1")]
        rtB = [Buf("rt"), Buf("rstd")]
        ptB = Buf("pt")

        ps_t = [es.enter_context(nc.psum_tensor(f"ps{i}", [128, 512], F32)) for i in range(8)]
        psB = [Buf(f"ps{i}") for i in range(8)]
        ps_ctr = [0]

        def nextps():
            i = ps_ctr[0] % 8
            ps_ctr[0] += 1
            return ps_t[i], psB[i]

        arena_live = []

        def new_phase_bufs(names):
            inherit = {}
            for b in arena_live:
                if b.w is not None:
                    k, v = b.w
                    if inherit.get(k, 0) < v:
                        inherit[k] = v
                for k, v in b.r.items():
                    if inherit.get(k, 0) < v:
                        inherit[k] = v
            out = []
            for n in names:
                b = Buf(n)
                b.r = dict(inherit)
                out.append(b)
            arena_live.clear()
            arena_live.extend(out)
            return out

        scope_state = {"cm": None}

        def scope(name):
            if scope_state["cm"] is not None:
                scope_state["cm"].__exit__(None, None, None)
                scope_state["cm"] = None
            if name is not None:
                cm = nc.named_scope(name)
                cm.__enter__()
                scope_state["cm"] = cm

        scope("setup")

        def cload(dst, src):
            T.dma(SP, dst, src, constB, writes=[constB])

        cload(cols_t[:], cols_d)
        cload(ident_t[:], ident_d)
        cload(nvbc_t[:], nvbc_d)
        cload(bsbc_t[:], bsbc_d)
        if use_cc:
            cload(flags_t[:], flags_d)
        setupB = new_phase_bufs(["setup"])[0]
        wst_s = a32(0, 1024)
        mask_s = a32(1024, 128)
        tau_s = a32(1152, SEG)
        ssa_s = a32(1216, 96)
        ssb_s = a32(1312, 1024)
        ssc_s = a32(2336, 2048)
        cload(wst_s, wst_d)
        cload(mask_s, mask_d)
        cload(tau_s, tau_d)
        cload(ssa_s, ssa_d)
        cload(ssb_s, ssb_d.rearrange("p a b -> p (a b)"))
        cload(ssc_s, ssc_d.rearrange("p a b -> p (a b)"))
        W0 = 4384

        def V(fn, reads=(), writes=()):
            return T.op(DVE, fn, reads=list(reads) + [constB], writes=list(writes) + [setupB])

        def A(fn, reads=(), writes=()):
            return T.op(ACT, fn, reads=list(reads) + [constB], writes=list(writes) + [setupB])

        dv, ac = nc.vector, nc.scalar
        V(lambda: dv.memset(ones_t[:], 1.0))
        V(lambda: dv.memset(carry[:], 0.0))
        V(lambda: dv.tensor_tensor(out=wct_t[:].rearrange("p (h t) -> p h t", h=8),
                                   in0=wst_s.rearrange("p (h t) -> p h t", h=8),
                                   in1=mask_s.unsqueeze(1).to_broadcast([128, 8, 128]), op=ALU.mult))
        ldt, are, aim = ssa_s[:, 0:32], ssa_s[:, 32:64], ssa_s[:, 64:96]
        sc = [a32(W0 + 32 * i, 32) for i in range(24)]
        dtv, lr, mag, ang, cs, sn, abr, abi, den, xr, zr, zi, u1, u2, u3, u4 = sc[:16]
        A(lambda: ac.activation(out=dtv, in_=ldt, func=AF.Exp))
        V(lambda: dv.tensor_scalar(out=lr, in0=are, scalar1=-1e-4, scalar2=None, op0=ALU.min))
        lrdt = sc[16]
        V(lambda: dv.tensor_tensor(out=lrdt, in0=lr, in1=dtv, op=ALU.mult))
        A(lambda: ac.activation(out=mag, in_=lrdt, func=AF.Exp))
        V(lambda: dv.tensor_tensor(out=ang, in0=aim, in1=dtv, op=ALU.mult))

        BIGW = W0 + 1024

        def sincos(src, n, out_sin, out_cos, scale=1.0):
            y = a32(BIGW, n)
            yi = a32(BIGW + n, n).bitcast(I32)
            yf = a32(BIGW + 2 * n, n)
            m1 = a32(BIGW + 3 * n, n)
            for off, dst in ((0.0, out_sin), (0.25, out_cos)):
                V(lambda: dv.tensor_scalar(out=y, in0=src, scalar1=scale / (2 * math.pi), scalar2=off,
                                           op0=ALU.mult, op1=ALU.add))
                V(lambda: dv.tensor_copy(out=yi, in_=y))
                V(lambda: dv.tensor_copy(out=yf, in_=yi))
                V(lambda: dv.tensor_tensor(out=y, in0=y, in1=yf, op=ALU.subtract))
                V(lambda: dv.tensor_scalar(out=m1, in0=y, scalar1=0.5, scalar2=None, op0=ALU.is_gt))
                V(lambda: dv.tensor_tensor(out=y, in0=y, in1=m1, op=ALU.subtract))
                V(lambda: dv.tensor_scalar(out=m1, in0=y, scalar1=-0.5, scalar2=None, op0=ALU.is_lt))
                V(lambda: dv.tensor_tensor(out=y, in0=y, in1=m1, op=ALU.add))
                A(lambda: ac.activation(out=dst, in_=y, func=AF.Sin, scale=2 * math.pi))

        sincos(ang, 32, sn, cs)
        V(lambda: dv.tensor_tensor(out=abr, in0=mag, in1=cs, op=ALU.mult))
        V(lambda: dv.tensor_tensor(out=abi, in0=mag, in1=sn, op=ALU.mult))
        V(lambda: dv.tensor_tensor(out=u1, in0=lr, in1=lr, op=ALU.mult))
        V(lambda: dv.tensor_tensor(out=u2, in0=aim, in1=aim, op=ALU.mult))
        V(lambda: dv.tensor_tensor(out=den, in0=u1, in1=u2, op=ALU.add))
        V(lambda: dv.reciprocal(out=den, in_=den))
        V(lambda: dv.tensor_scalar(out=xr, in0=abr, scalar1=-1.0, scalar2=None, op0=ALU.add))
        V(lambda: dv.tensor_tensor(out=u1, in0=xr, in1=lr, op=ALU.mult))
        V(lambda: dv.tensor_tensor(out=u2, in0=abi, in1=aim, op=ALU.mult))
        V(lambda: dv.tensor_tensor(out=u1, in0=u1, in1=u2, op=ALU.add))
        V(lambda: dv.tensor_tensor(out=zr, in0=u1, in1=den, op=ALU.mult))
        V(lambda: dv.tensor_tensor(out=u1, in0=abi, in1=lr, op=ALU.mult))
        V(lambda: dv.tensor_tensor(out=u2, in0=xr, in1=aim, op=ALU.mult))
        V(lambda: dv.tensor_tensor(out=u1, in0=u1, in1=u2, op=ALU.subtract))
        V(lambda: dv.tensor_tensor(out=zi, in0=u1, in1=den, op=ALU.mult))
        bre = ssb_s[:, 0:512].rearrange("p (g q) -> p g q", q=16)
        bim = ssb_s[:, 512:1024].rearrange("p (g q) -> p g q", q=16)
        bw = BIGW + 8192
        bt1 = a32(bw, 512).rearrange("p (g q) -> p g q", q=16)
        bt2 = a32(bw + 512, 512).rearrange("p (g q) -> p g q", q=16)
        bbr = a32(bw + 1024, 512).rearrange("p (g q) -> p g q", q=16)
        bbi = a32(bw + 1536, 512).rearrange("p (g q) -> p g q", q=16)
        blk = [a32(bw + 2048, 1024), a32(bw + 3072, 1024)]
        zrb = zr.unsqueeze(2).to_broadcast([128, 32, 16])
        zib = zi.unsqueeze(2).to_broadcast([128, 32, 16])
        V(lambda: dv.tensor_tensor(out=bt1, in0=bre, in1=zrb, op=ALU.mult))
        V(lambda: dv.tensor_tensor(out=bt2, in0=bim, in1=zib, op=ALU.mult))
        V(lambda: dv.tensor_tensor(out=bbr, in0=bt1, in1=bt2, op=ALU.subtract))
        V(lambda: dv.tensor_tensor(out=bt1, in0=bim, in1=zrb, op=ALU.mult))
        V(lambda: dv.tensor_tensor(out=bt2, in0=bre, in1=zib, op=ALU.mult))
        V(lambda: dv.tensor_tensor(out=bbi, in0=bt1, in1=bt2, op=ALU.add))
        for ri, src in ((0, bbr), (1, bbi)):
            b3 = blk[ri].rearrange("p (g c) -> p g c", c=32)
            V(lambda: dv.memset(blk[ri], 0.0))
            V(lambda: dv.tensor_copy(out=b3[0:64, :, 0:16], in_=src[0:64]))
            V(lambda: dv.tensor_copy(out=b3[64:128, :, 16:32], in_=src[64:128]))
            for bt_ in range(8):
                pst, psb = nextps()
                T.op(PE, lambda: nc.tensor.transpose(pst[:, 0:128], blk[ri][:, bt_ * 128:(bt_ + 1) * 128], ident_t[:]),
                     reads=[setupB, constB], writes=[psb])
                T.op(DVE, lambda: dv.tensor_copy(out=bbT[:, bt_, ri, :], in_=pst[:, 0:128]), reads=[psb], writes=[setupB])
        cre = ssc_s[:, 0:1024].rearrange("p (g c) -> p g c", c=32)
        cim = ssc_s[:, 1024:2048].rearrange("p (g c) -> p g c", c=32)
        V(lambda: dv.tensor_copy(out=c32b[:, :, 0, :], in_=cre))
        V(lambda: dv.tensor_scalar(out=c32b[:, :, 1, :], in0=cim, scalar1=-1.0, scalar2=None, op0=ALU.mult))
        ph = a32(BIGW + 8192 + 4096, 2048)
        V(lambda: dv.tensor_tensor(out=ph.rearrange("p (g t) -> p g t", t=SEG),
                                   in0=ang.unsqueeze(2).to_broadcast([128, 32, SEG]),
                                   in1=tau_s.unsqueeze(1).to_broadcast([128, 32, SEG]), op=ALU.mult))
        sincos(ph, 2048, sinT[:].rearrange("p g t -> p (g t)"), cosT[:].rearrange("p g t -> p (g t)"))
        tm = a32(BIGW + 8192 + 4096 + 2048, SEG)
        V(lambda: dv.tensor_scalar(out=tm, in0=tau_s, scalar1=0.5, scalar2=None, op0=ALU.is_gt))
        V(lambda: dv.tensor_tensor(out=mtab[:], in0=mag.unsqueeze(2).to_broadcast([128, 32, SEG]),
                                   in1=tm.unsqueeze(1).to_broadcast([128, 32, SEG]), op=ALU.mult))
        sincos(ang, 32, u3, u4, scale=float(SEG))
        V(lambda: dv.tensor_tensor(out=rotm[:, 0, 0, :], in0=u4, in1=mag, op=ALU.mult))
        V(lambda: dv.tensor_copy(out=rotm[:, 0, 1, :], in_=rotm[:, 0, 0, :]))
        V(lambda: dv.tensor_tensor(out=rotm[:, 1, 1, :], in0=u3, in1=mag, op=ALU.mult))
        V(lambda: dv.tensor_scalar(out=rotm[:, 1, 0, :], in0=rotm[:, 1, 1, :], scalar1=-1.0, scalar2=None, op0=ALU.mult))
        V(lambda: dv.memset(sprev[:], 0.0))
        phm = a32(BIGW + 8192 + 4096, 2048)
        V(lambda: dv.tensor_tensor(out=phm.rearrange("p (g t) -> p g t", t=SEG),
                                   in0=lrdt.unsqueeze(2).to_broadcast([128, 32, SEG]),
                                   in1=tau_s.unsqueeze(1).to_broadcast([128, 32, SEG]), op=ALU.mult))
        A(lambda: ac.activation(out=phm, in_=phm, func=AF.Exp))
        V(lambda: dv.tensor_tensor(out=ptab[:, :, 0, :], in0=phm.rearrange("p (g t) -> p g t", t=SEG), in1=cosT[:], op=ALU.mult))
        V(lambda: dv.tensor_tensor(out=ptab[:, :, 1, :], in0=phm.rearrange("p (g t) -> p g t", t=SEG), in1=sinT[:], op=ALU.mult))
        pw = [(sc[17], sc[18]), (sc[19], sc[20])]
        V(lambda: dv.tensor_copy(out=pw[0][0], in_=abr))
        V(lambda: dv.tensor_copy(out=pw[0][1], in_=abi))
        a64 = (sc[21], sc[22])
        cur = 0
        for it in range(9):
            r_, i_ = pw[cur]
            nr, ni = pw[1 - cur]
            V(lambda: dv.tensor_tensor(out=u1, in0=r_, in1=r_, op=ALU.mult))
            V(lambda: dv.tensor_tensor(out=u2, in0=i_, in1=i_, op=ALU.mult))
            V(lambda: dv.tensor_tensor(out=nr, in0=u1, in1=u2, op=ALU.subtract))
            V(lambda: dv.tensor_tensor(out=u1, in0=r_, in1=i_, op=ALU.mult))
            V(lambda: dv.tensor_scalar(out=ni, in0=u1, scalar1=2.0, scalar2=None, op0=ALU.mult))
            cur = 1 - cur
            if it == 5:
                V(lambda: dv.tensor_copy(out=a64[0], in_=nr))
                V(lambda: dv.tensor_copy(out=a64[1], in_=ni))
        V(lambda: dv.tensor_copy(out=a512[:, 0, :], in_=pw[cur][0]))
        V(lambda: dv.tensor_copy(out=a512[:, 1, :], in_=pw[cur][1]))
        V(lambda: dv.tensor_copy(out=ttab[:, 0, 0, :], in_=abr))
        V(lambda: dv.tensor_copy(out=ttab[:, 1, 0, :], in_=abi))
        for s_ in range(1, NSEG):
            pr, pi = ttab[:, 0, s_ - 1, :], ttab[:, 1, s_ - 1, :]
            V(lambda: dv.tensor_tensor(out=u1, in0=pr, in1=a64[0], op=ALU.mult))
            V(lambda: dv.tensor_tensor(out=u2, in0=pi, in1=a64[1], op=ALU.mult))
            V(lambda: dv.tensor_tensor(out=ttab[:, 0, s_, :], in0=u1, in1=u2, op=ALU.subtract))
            V(lambda: dv.tensor_tensor(out=u1, in0=pr, in1=a64[1], op=ALU.mult))
            V(lambda: dv.tensor_tensor(out=u2, in0=pi, in1=a64[0], op=ALU.mult))
            V(lambda: dv.tensor_tensor(out=ttab[:, 1, s_, :], in0=u1, in1=u2, op=ALU.add))
        T.op(DVE, lambda: dv.memset(small[:], 0.0), reads=[setupB, constB], writes=[smallB, constB])

        ring = {"slots": [], "i": 0, "offs": slot_off}

        def set_ring(nslots):
            bufs = [Buf(f"slot{j}") for j in range(nslots)]
            ring["slots"] = bufs
            ring["i"] = 0
            return bufs

        def load_w(view_fn, src):
            j = ring["i"] % len(ring["slots"])
            ring["i"] += 1
            b = ring["slots"][j]
            sl = a16(ring["offs"][j], 4096)
            dst = view_fn(sl)
            T.dma(POOL, dst, src, b, writes=[b])
            return dst, b

        def wview_k(nk, ncols):
            return lambda sl: sl[:, 0:nk * ncols].rearrange("p (k c) -> p k c", k=nk)

        pe, gp = nc.tensor, nc.gpsimd

        def norm(srcs, gcol, dsts, dim, sqB, rtB, in_place_fp32=False):
            n = len(srcs)
            pst, psb = nextps()
            for i, (sap, sbuf) in enumerate(srcs):
                sq = a16(OFF_SQ + 256 * (i % 2), 512)
                T.op(ACT, lambda: ac.activation(out=sq, in_=sap, func=AF.Square), reads=[sbuf], writes=[sqB[i % 2]])
                T.op(PE, lambda: pe.matmul(pst[:], lhsT=ones_t[:], rhs=sq, start=(i == 0), stop=(i == n - 1)),
                     reads=[sqB[i % 2], constB], writes=[psb])
            rt = a32(OFF_RT, 512)
            rstd = a32(OFF_RT + 512, 512)
            T.op(ACT, lambda: ac.activation(out=rt, in_=pst[:], func=AF.Sqrt, scale=1.0 / dim, bias=small[:, 1:2]),
                 reads=[psb, smallB], writes=[rtB[0]])
            T.op(DVE, lambda: dv.reciprocal(out=rstd, in_=rt), reads=[rtB[0]], writes=[rtB[1]])
            for i, (sap, sbuf) in enumerate(srcs):
                dap, dbuf = dsts[i]
                T.op(DVE, lambda: dv.scalar_tensor_tensor(out=dap, in0=sap, scalar=gcol[:, i:i + 1], in1=rstd,
                                                          op0=ALU.mult, op1=ALU.mult),
                     reads=[sbuf, rtB[1], constB], writes=[dbuf])

        T.op(DVE, lambda: dv.memset(small[:, 1:2], EPS), reads=[constB], writes=[smallB])

        hT = [(h_t[:, i, :], hB[i]) for i in range(KT)]
        xnT = [(xn_t[:, i, :], xnB[i]) for i in range(KT)]

        def ffn(wg_d, wu_d, wd_d):
            set_ring(NSLOT_FFN)
            fb = new_phase_bufs([f"slot{j}" for j in range(NSLOT_FFN)] + ["act0", "act1", "silu0", "silu1", "wpp"])
            ring["slots"] = fb[:NSLOT_FFN]
            ring["offs"] = slot_off
            actB, siluB = fb[8:10], fb[10:12]
            wgv = wg_d.rearrange("(k p) c -> p k c", p=128)
            wuv = wu_d.rearrange("(k p) c -> p k c", p=128)
            wdv = wd_d.rearrange("(j p) f -> p j f", p=128)
            NCH = DFF // 256
            pend = None

            def down(c, wd_ap, wdb, actv, ab):
                for i in range(KT):
                    pst, psb = nextps()
                    T.group(PE, [(lambda jj=jj: pe.matmul(pst[:], lhsT=wd_ap[:, jj, i * 128:(i + 1) * 128], rhs=actv[:, jj, :],
                                                         start=(jj == 0), stop=(jj == 1))) for jj in range(2)],
                            reads=[wdb, ab], writes=[psb])
                    T.op(DVE, lambda: dv.scalar_tensor_tensor(out=h_t[:, i, :], in0=pst[:], scalar=0.5, in1=h_t[:, i, :],
                                                              op0=ALU.mult, op1=ALU.add),
                         reads=[psb], writes=[hB[i]])

            for c in range(NCH):
                wg_ap, wgb = load_w(wview_k(16, 256), wgv[:, :, c * 256:(c + 1) * 256])
                wu_ap, wub = load_w(wview_k(16, 256), wuv[:, :, c * 256:(c + 1) * 256])
                wd_ap, wdb = load_w(wview_k(2, 2048), wdv[:, 2 * c:2 * c + 2, :])
                actv = a16(OFF_ACT + 512 * (c % 2), 1024).rearrange("p (j t) -> p j t", j=2)
                ab = actB[c % 2]
                for jj in range(2):
                    gps, gpb = nextps()
                    ups, upb = nextps()
                    T.group(PE, [(lambda k=k: pe.matmul(gps[:], lhsT=wg_ap[:, k, jj * 128:(jj + 1) * 128], rhs=xn_t[:, k, :],
                                                       start=(k == 0), stop=(k == KT - 1))) for k in range(KT)],
                            reads=[wgb] + xnB, writes=[gpb])
                    T.group(PE, [(lambda k=k: pe.matmul(ups[:], lhsT=wu_ap[:, k, jj * 128:(jj + 1) * 128], rhs=xn_t[:, k, :],
                                                       start=(k == 0), stop=(k == KT - 1))) for k in range(KT)],
                            reads=[wub] + xnB, writes=[upb])
                    sl = a32(OFF_SILU + 512 * jj, 512)
                    T.op(ACT, lambda: ac.activation(out=sl, in_=gps[:], func=AF.Silu), reads=[gpb], writes=[siluB[jj]])
                    T.op(DVE, lambda: dv.tensor_tensor(out=actv[:, jj, :], in0=sl, in1=ups[:], op=ALU.mult),
                         reads=[siluB[jj], upb], writes=[ab])
                if pend is not None:
                    down(*pend)
                pend = (c, wd_ap, wdb, actv, ab)
            down(*pend)
            return fb

        for blk_i in range(nblk):
            t0 = blk_i * TB
            scope(f"b{blk_i}_load")
            lb = new_phase_bufs(["xin0", "xin1", "pin"])
            xinB, pinB = lb[0:2], lb[2]
            for tt in range(4):
                xin = a32(OFF_XIN + 2048 * (tt % 2), 2048)
                T.dma(SP, xin, x_d[t0 + tt * 128:t0 + (tt + 1) * 128, :], xinB[tt % 2], writes=[xinB[tt % 2]])
                for q in range(4):
                    pst, psb = nextps()
                    T.group(PE, [(lambda kk=kk: pe.transpose(pst[:, kk * 128:(kk + 1) * 128],
                                                            xin[:, (4 * q + kk) * 128:(4 * q + kk + 1) * 128], ident_t[:]))
                                 for kk in range(4)], reads=[xinB[tt % 2], constB], writes=[psb])
                    T.op(ACT, lambda: ac.activation(out=h_t[:, 4 * q:4 * q + 4, tt * 128:(tt + 1) * 128],
                                                    in_=pst[:].rearrange("p (a b) -> p a b", a=4), func=AF.Copy),
                         reads=[psb], writes=hB[4 * q:4 * q + 4])
            pin = a32(OFF_PIN, 1024).rearrange("p (a b) -> p a b", a=4)
            T.dma(SP, pin, p_d[t0:t0 + TB, :].rearrange("(a p) c -> p a c", p=128), pinB, writes=[pinB])
            pT = a16(OFF_PT, 1024).rearrange("p (k t) -> p k t", k=2)
            for k2 in range(2):
                pst, psb = nextps()
                T.group(PE, [(lambda a=a: pe.transpose(pst[:, a * 128:(a + 1) * 128], pin[:, a, k2 * 128:(k2 + 1) * 128], ident_t[:]))
                             for a in range(4)], reads=[pinB, constB], writes=[psb])
                T.op(ACT, lambda: ac.activation(out=pT[:, k2, :], in_=pst[:], func=AF.Copy), reads=[psb], writes=[ptB])

            if stage >= 1:
                scope(f"b{blk_i}_ffn1")
                norm(hT, cols_t[:, C_G1:C_G1 + 16], xnT, D, sqB, rtB)
                fb = ffn(w1g_d, w1u_d, w1d_d)
            if stage >= 2:
                mb = new_phase_bufs(["slot0", "slot1", "slot2"] + [f"zb{i}" for i in range(8)] + [f"gu{i}" for i in range(8)]
                                    + [f"v{i}" for i in range(4)] + [f"E{i}" for i in range(8)] + ["S0", "S1"] + [f"t{i}" for i in range(6)])
                ring["slots"] = mb[0:3]
                ring["i"] = 0
                ring["offs"] = [0, 2048, 18432]
                zbB, guB, vB = mb[3:11], mb[11:19], mb[19:23]
                EBf, SB_, tB = mb[23:31], mb[31:33], mb[33:39]

                def EBc(buf, r, jj):
                    return EBf[buf * 4 + r * 2 + jj]

                def EBall(buf):
                    return EBf[buf * 4:buf * 4 + 4]
                zb = a16(OFF_ZB, 4096).rearrange("p (k t) -> p k t", k=8)
                gu = a16(OFF_GU, 4096).rearrange("p (k t) -> p k t", k=8)
                vv = a16(OFF_V, 4096).rearrange("p (a f) -> p a f", a=4)
                Ev2 = [a32(OFF_E + 2048 * i, 2048).rearrange("p (r s j t) -> p r s j t", r=2, s=NSEG, j=2) for i in range(2)]
                tmp = [a32(OFF_TMP + 512 * i, 512) for i in range(6)]
                y1t = [a16(OFF_GU + 512 * i, 512) for i in range(8)]

                def ymB(b_):
                    return [guB[2 * b_], guB[2 * b_ + 1]] if b_ < 4 else [vB[b_ - 4]]
                y1B = [ymB(i) for i in range(8)]
                scope(f"b{blk_i}_inproj")
                norm(hT, cols_t[:, C_GM:C_GM + 16], xnT, D, sqB, rtB)
                winv = win_d.rearrange("(k p) c -> p k c", p=128)
                for c in range(8):
                    w_ap, wb = load_w(wview_k(16, 256), winv[:, :, c * 256:(c + 1) * 256])
                    for jj in range(2):
                        ft = 2 * c + jj
                        pst, psb = nextps()
                        T.group(PE, [(lambda k=k: pe.matmul(pst[:], lhsT=w_ap[:, k, jj * 128:(jj + 1) * 128], rhs=xn_t[:, k, :],
                                                           start=(k == 0), stop=(k == KT - 1))) for k in range(KT)],
                                reads=[wb] + xnB, writes=[psb])
                        if ft < 8:
                            T.op(ACT, lambda: ac.activation(out=zb[:, ft, :], in_=pst[:], func=AF.Copy), reads=[psb], writes=[zbB[ft]])
                        else:
                            T.op(ACT, lambda: ac.activation(out=gu[:, ft - 8, :], in_=pst[:], func=AF.Gelu_apprx_tanh),
                                 reads=[psb], writes=[guB[ft - 8]])
                for c in range(4):
                    w_ap, wb = load_w(wview_k(16, 256), winv[:, :, 2048 + c * 256:2048 + (c + 1) * 256])
                    for tt in range(4):
                        pst, psb = nextps()
                        T.group(PE, [(lambda k=k: pe.matmul(pst[:, 0:256], lhsT=xn_t[:, k, tt * 128:(tt + 1) * 128], rhs=w_ap[:, k, :],
                                                           start=(k == 0), stop=(k == KT - 1))) for k in range(KT)],
                                reads=[wb] + xnB, writes=[psb])
                        T.op(ACT, lambda: ac.activation(out=vv[:, tt, c * 256:(c + 1) * 256], in_=pst[:, 0:256], func=AF.Gelu_apprx_tanh),
                             reads=[psb], writes=[vB[tt]])
                scope(f"b{blk_i}_gmlp")
                t1024 = a32(OFF_TMP, 1024)
                for tt in range(4):
                    st = small[:, 8:20]
                    T.op(DVE, lambda: dv.bn_stats(out=small[:, 8:14], in_=vv[:, tt, 0:512]), reads=[vB[tt]], writes=[smallB])
                    T.op(DVE, lambda: dv.bn_stats(out=small[:, 14:20], in_=vv[:, tt, 512:1024]), reads=[vB[tt]], writes=[smallB])
                    T.op(DVE, lambda: dv.bn_aggr(out=small[:, 20:22], in_=st), reads=[smallB], writes=[smallB])
                    T.op(ACT, lambda: ac.activation(out=small[:, 22:23], in_=small[:, 21:22], func=AF.Sqrt, bias=small[:, 1:2]),
                         reads=[smallB], writes=[smallB])
                    T.op(DVE, lambda: dv.reciprocal(out=small[:, 23:24], in_=small[:, 22:23]), reads=[smallB], writes=[smallB])
                    T.op(DVE, lambda: dv.tensor_scalar(out=t1024, in0=vv[:, tt, :], scalar1=small[:, 20:21], scalar2=small[:, 23:24],
                                                       op0=ALU.subtract, op1=ALU.mult),
                         reads=[vB[tt], smallB], writes=[tB[0], tB[1]])
                    T.op(DVE, lambda: dv.tensor_tensor(out=vv[:, tt, :], in0=t1024, in1=nvbc_t[:], op=ALU.mult),
                         reads=[tB[0], tB[1], constB], writes=[vB[tt]])
                for hd in range(8):
                    pst, psb = nextps()
                    T.group(PE, [(lambda tt=tt: pe.matmul(pst[:, tt * 128:(tt + 1) * 128], lhsT=vv[:, tt, hd * 128:(hd + 1) * 128],
                                                         rhs=wct_t[:, hd * 128:(hd + 1) * 128], start=True, stop=True)) for tt in range(4)],
                            reads=vB + [constB], writes=[psb])
                    tq = tmp[2 + hd % 2]
                    T.op(DVE, lambda: dv.tensor_tensor(out=tq.rearrange("p (a t) -> p a t", a=4), in0=pst[:].rearrange("p (a t) -> p a t", a=4),
                                                       in1=bsbc_t[:, hd * 128:(hd + 1) * 128].unsqueeze(1).to_broadcast([128, 4, 128]), op=ALU.add),
                         reads=[psb, constB], writes=[tB[2 + hd % 2]])
                    T.op(DVE, lambda: dv.tensor_tensor(out=gu[:, hd, :], in0=gu[:, hd, :], in1=tq, op=ALU.mult),
                         reads=[tB[2 + hd % 2]], writes=[guB[hd]])
                norm([(gu[:, i, :], guB[i]) for i in range(8)], cols_t[:, C_GG:C_GG + 8], xnT[8:16], DSSM, sqB, rtB)
                scope(f"b{blk_i}_ssm")
                ymain = a32(OFF_GU, 4096).rearrange("p (k t) -> p k t", k=8)
                ypsd = {}

                def st_A(b2):
                    Eb, eb = Ev2[b2 % 2], b2 % 2
                    bt_ = b2 // 2
                    for jj in range(2):
                        j = 2 * (b2 % 2) + jj
                        g_ = 2 * b2 + jj
                        dre, dreb = nextps()
                        dim_, dimb = nextps()
                        T.op(PE, lambda: pe.matmul(dre[:], lhsT=bbT[32 * j:32 * j + 32, bt_, 0, :], rhs=zb[32 * j:32 * j + 32, bt_, :],
                                                   start=True, stop=True, tile_position=(32 * j, 0)), reads=[zbB[bt_], constB], writes=[dreb])
                        T.op(PE, lambda: pe.matmul(dim_[:], lhsT=bbT[32 * j:32 * j + 32, bt_, 1, :], rhs=zb[32 * j:32 * j + 32, bt_, :],
                                                   start=True, stop=True, tile_position=(32 * j, 0)), reads=[zbB[bt_], constB], writes=[dimb])
                        cb = cosT[:, g_, :].unsqueeze(1).to_broadcast([128, NSEG, SEG])
                        sbb = sinT[:, g_, :].unsqueeze(1).to_broadcast([128, NSEG, SEG])
                        d3r = dre[:].rearrange("p (s t) -> p s t", s=NSEG)
                        d3i = dim_[:].rearrange("p (s t) -> p s t", s=NSEG)
                        ta, tb_ = 0, 1
                        r1 = tmp[ta].rearrange("p (s t) -> p s t", s=NSEG)
                        r3 = tmp[tb_].rearrange("p (s t) -> p s t", s=NSEG)
                        ere, eim = Eb[:, 0, :, jj, :], Eb[:, 1, :, jj, :]
                        T.op(DVE, lambda: dv.tensor_tensor(out=ere, in0=d3r, in1=cb, op=ALU.mult), reads=[dreb, constB], writes=[EBc(eb, 0, jj)])
                        T.op(DVE, lambda: dv.tensor_tensor(out=r1, in0=d3i, in1=sbb, op=ALU.mult), reads=[dimb, constB], writes=[tB[ta]])
                        T.op(DVE, lambda: dv.tensor_tensor(out=eim, in0=d3i, in1=cb, op=ALU.mult), reads=[dimb, constB], writes=[EBc(eb, 1, jj)])
                        T.op(DVE, lambda: dv.tensor_tensor(out=r3, in0=d3r, in1=sbb, op=ALU.mult), reads=[dreb, constB], writes=[tB[tb_]])
                        T.op(POOL, lambda: gp.tensor_tensor(out=ere, in0=ere, in1=r1, op=ALU.add), reads=[tB[ta]], writes=[EBc(eb, 0, jj)])
                        T.op(POOL, lambda: gp.tensor_tensor(out=eim, in0=eim, in1=r3, op=ALU.subtract), reads=[tB[tb_]], writes=[EBc(eb, 1, jj)])

                def st_B(b2):
                    Eb, eb = Ev2[b2 % 2], b2 % 2
                    all4 = EBall(eb)
                    mt2 = mtab[:, 2 * b2:2 * b2 + 2, :].rearrange("p g t -> p (g t)")
                    rc = rotm[:, 0, :, 2 * b2:2 * b2 + 2]
                    rs = rotm[:, 1, :, 2 * b2:2 * b2 + 2]
                    for s_ in range(NSEG):
                        if s_ > 0:
                            last = Eb[:, :, s_ - 1, :, SEG - 1]
                            i1 = small[:, 24:28].rearrange("p (r j) -> p r j", r=2)
                            i2 = small[:, 32:36].rearrange("p (r j) -> p r j", r=2)
                            T.op(DVE, lambda: dv.tensor_tensor(out=i1, in0=last, in1=rc, op=ALU.mult), reads=all4 + [constB], writes=[smallB])
                            T.op(DVE, lambda: dv.tensor_tensor(out=i2[:, 0, :], in0=last[:, 1, :], in1=rs[:, 0, :], op=ALU.mult),
                                 reads=all4 + [constB], writes=[smallB])
                            T.op(DVE, lambda: dv.tensor_tensor(out=i2[:, 1, :], in0=last[:, 0, :], in1=rs[:, 1, :], op=ALU.mult),
                                 reads=all4 + [constB], writes=[smallB])
                            T.op(DVE, lambda: dv.tensor_tensor(out=i1, in0=i1, in1=i2, op=ALU.add), reads=[smallB], writes=[smallB])
                            T.op(DVE, lambda: dv.tensor_tensor(out=Eb[:, :, s_, :, 0], in0=Eb[:, :, s_, :, 0], in1=i1, op=ALU.add),
                                 reads=[smallB], writes=all4)
                        for r in range(2):
                            er = Eb[:, r, s_, :, :].rearrange("p j t -> p (j t)")
                            T.op(DVE, lambda: dv.tensor_tensor_scan(out=er, data0=mt2, data1=er, initial=0.0, op0=ALU.mult, op1=ALU.add),
                                 reads=[constB], writes=[EBc(eb, r, 0), EBc(eb, r, 1)])
                    T.op(DVE, lambda: dv.tensor_copy(out=carry[:, :, 2 * b2:2 * b2 + 2], in_=Eb[:, :, NSEG - 1, :, SEG - 1]),
                         reads=all4, writes=[carryB[b2 // 2]])

                def st_C(b2):
                    Eb, eb = Ev2[b2 % 2], b2 % 2
                    for jj in range(2):
                        g_ = 2 * b2 + jj
                        cb = cosT[:, g_, :].unsqueeze(1).to_broadcast([128, NSEG, SEG])
                        sbb = sinT[:, g_, :].unsqueeze(1).to_broadcast([128, NSEG, SEG])
                        rre, rim = Eb[:, 0, :, jj, :], Eb[:, 1, :, jj, :]
                        u0 = tmp[2 + 2 * jj].rearrange("p (s t) -> p s t", s=NSEG)
                        u1 = tmp[3 + 2 * jj].rearrange("p (s t) -> p s t", s=NSEG)
                        T.op(POOL, lambda: gp.tensor_tensor(out=u0, in0=rre, in1=cb, op=ALU.mult), reads=[EBc(eb, 0, jj), constB], writes=[tB[2 + 2 * jj]])
                        T.op(POOL, lambda: gp.tensor_tensor(out=u1, in0=rim, in1=sbb, op=ALU.mult), reads=[EBc(eb, 1, jj), constB], writes=[tB[3 + 2 * jj]])
                        T.op(POOL, lambda: gp.tensor_tensor(out=rre, in0=rre, in1=sbb, op=ALU.mult), reads=[constB], writes=[EBc(eb, 0, jj)])
                        T.op(POOL, lambda: gp.tensor_tensor(out=rim, in0=rim, in1=cb, op=ALU.mult), reads=[constB], writes=[EBc(eb, 1, jj)])
                        sv = a16(OFF_S + 512 * jj, 1024).rearrange("p (r t) -> p r t", r=2)
                        s3i = sv[:, 1, :].rearrange("p (s t) -> p s t", s=NSEG)
                        T.op(POOL, lambda: gp.tensor_tensor(out=s3i, in0=rre, in1=rim, op=ALU.add),
                             reads=[EBc(eb, 0, jj), EBc(eb, 1, jj)], writes=[SB_[jj]])

                def st_D(b2):
                    Eb, eb = Ev2[b2 % 2], b2 % 2
                    bt_ = b2 // 2
                    if b2 % 2 == 0:
                        ypsd[bt_] = nextps()
                    yps, ypb = ypsd[bt_]
                    for jj in range(2):
                        j = 2 * (b2 % 2) + jj
                        g_ = 2 * b2 + jj
                        rre, rim = Eb[:, 0, :, jj, :], Eb[:, 1, :, jj, :]
                        u0 = tmp[2 + 2 * jj].rearrange("p (s t) -> p s t", s=NSEG)
                        u1 = tmp[3 + 2 * jj].rearrange("p (s t) -> p s t", s=NSEG)
                        sv = a16(OFF_S + 512 * jj, 1024).rearrange("p (r t) -> p r t", r=2)
                        s3 = [sv[:, r, :].rearrange("p (s t) -> p s t", s=NSEG) for r in range(2)]
                        T.op(DVE, lambda: dv.tensor_tensor(out=s3[0], in0=u0, in1=u1, op=ALU.subtract),
                             reads=[tB[2 + 2 * jj], tB[3 + 2 * jj]], writes=[SB_[jj]])
                        T.group(PE, [(lambda r=r: pe.matmul(yps[32 * j:32 * j + 32, :], lhsT=c32b[:, g_, r, :], rhs=sv[:, r, :],
                                                           start=(r == 0), stop=(r == 1), tile_position=(0, 32 * j))) for r in range(2)],
                                reads=[SB_[jj], constB], writes=[ypb])
                    if b2 % 2 == 1:
                        T.op(DVE, lambda: dv.scalar_tensor_tensor(out=ymain[:, bt_, :], in0=zb[:, bt_, :], scalar=cols_t[:, C_DD + bt_:C_DD + bt_ + 1],
                                                                  in1=yps[:], op0=ALU.mult, op1=ALU.add),
                             reads=[zbB[bt_], ypb, constB], writes=ymB(bt_))

                NB2 = 16
                for it in range(NB2 + 2):
                    if it < NB2:
                        st_A(it)
                    if 0 <= it - 1 < NB2:
                        st_B(it - 1)
                    if 0 <= it - 2 < NB2:
                        st_D(it - 2)
                    if 0 <= it - 1 < NB2:
                        st_C(it - 1)

                def SV(fn, reads=(), writes=()):
                    return T.op(DVE, fn, reads=list(reads) + [ssmsB, constB], writes=list(writes) + [ssmsB])
                R_ = [ssms[:, i, :] for i in range(12)]
                c63, s63 = cosT[:, :, SEG - 1], sinT[:, :, SEG - 1]
                lfr, lfi = carry[:, 0, :], carry[:, 1, :]
                sllr, slli, spr, spi, sinr, sini, q1, q2 = R_[0], R_[1], R_[2], R_[3], R_[4], R_[5], R_[6], R_[7]
                SV(lambda: dv.tensor_tensor(out=q1, in0=lfr, in1=c63, op=ALU.mult), reads=carryB)
                SV(lambda: dv.tensor_tensor(out=q2, in0=lfi, in1=s63, op=ALU.mult), reads=carryB)
                SV(lambda: dv.tensor_tensor(out=sllr, in0=q1, in1=q2, op=ALU.subtract))
                SV(lambda: dv.tensor_tensor(out=q1, in0=lfr, in1=s63, op=ALU.mult), reads=carryB)
                SV(lambda: dv.tensor_tensor(out=q2, in0=lfi, in1=c63, op=ALU.mult), reads=carryB)
                SV(lambda: dv.tensor_tensor(out=slli, in0=q1, in1=q2, op=ALU.add))

                def sout_from(xr, xi, outr, outi, extra_r=(), extra_w=()):
                    SV(lambda: dv.tensor_tensor(out=q1, in0=a512[:, 0, :], in1=xr, op=ALU.mult), reads=extra_r)
                    SV(lambda: dv.tensor_tensor(out=q2, in0=a512[:, 1, :], in1=xi, op=ALU.mult), reads=extra_r)
                    SV(lambda: dv.tensor_tensor(out=q1, in0=q1, in1=q2, op=ALU.subtract))
                    SV(lambda: dv.tensor_tensor(out=q1, in0=q1, in1=sllr, op=ALU.add))
                    SV(lambda: dv.tensor_tensor(out=q2, in0=a512[:, 0, :], in1=xi, op=ALU.mult), reads=extra_r)
                    SV(lambda: dv.tensor_tensor(out=R_[8], in0=a512[:, 1, :], in1=xr, op=ALU.mult), reads=extra_r)
                    SV(lambda: dv.tensor_tensor(out=q2, in0=q2, in1=R_[8], op=ALU.add))
                    SV(lambda: dv.tensor_tensor(out=outi, in0=q2, in1=slli, op=ALU.add), writes=extra_w)
                    SV(lambda: dv.tensor_copy(out=outr, in_=q1), writes=extra_w)

                sout_from(sprev[:, 0, :], sprev[:, 1, :], spr, spi, extra_r=[sprevB])
                ccinB, ccoutB = Buf("ccin"), Buf("ccout")
                T.dma(SP, ccin_d[blk_i].ap(), ssms[:, 2:4, :].rearrange("p a b -> p (a b)"), ccinB, reads=[ssmsB], writes=[ccinB])
                T.custom(POOL, lambda: gp.collective_compute("AllGather", ALU.bypass, replica_groups=[[0, 1], [2, 3], [4, 5], [6, 7]],
                                                             ins=[ccin_d[blk_i].ap().opt()], outs=[ccout_d[blk_i].ap().opt()]),
                         f"cc{blk_i}", reads=[ccinB], writes=[ccoutB])
                T.dma(SP, gbuf[:], ccout_d[blk_i].ap().rearrange("(r p) n -> p r n", p=128), gB, reads=[ccoutB], writes=[gB])
                fl = flags_t[:, 3 * blk_i:3 * blk_i + 3]
                sin64 = ssms[:, 4:6, :].rearrange("p a b -> p (a b)")
                sp64 = sprev[:].rearrange("p a b -> p (a b)")
                SV(lambda: dv.tensor_scalar(out=sin64, in0=sp64, scalar1=fl[:, 0:1], scalar2=None, op0=ALU.mult), reads=[sprevB])
                SV(lambda: dv.scalar_tensor_tensor(out=sin64, in0=gbuf[:, 0, :], scalar=fl[:, 1:2], in1=sin64, op0=ALU.mult, op1=ALU.add), reads=[gB])
                SV(lambda: dv.scalar_tensor_tensor(out=sin64, in0=gbuf[:, 1, :], scalar=fl[:, 2:3], in1=sin64, op0=ALU.mult, op1=ALU.add), reads=[gB])
                sout_from(sinr, sini, sprev[:, 0, :], sprev[:, 1, :], extra_w=[sprevB])
                sinr_b = sinr.unsqueeze(1).to_broadcast([128, NSEG, 32])
                sini_b = sini.unsqueeze(1).to_broadcast([128, NSEG, 32])
                kq = [a32(OFF_TMP + 256 * i, 256).rearrange("p (s g) -> p s g", s=NSEG) for i in range(2)]
                SV(lambda: dv.tensor_tensor(out=kq[0], in0=ttab[:, 0], in1=sinr_b, op=ALU.mult), writes=[tB[0]])
                SV(lambda: dv.tensor_tensor(out=kq[1], in0=ttab[:, 1], in1=sini_b, op=ALU.mult), writes=[tB[0]])
                SV(lambda: dv.tensor_tensor(out=kbuf[:, 0], in0=kq[0], in1=kq[1], op=ALU.subtract), reads=[tB[0]], writes=[kB])
                SV(lambda: dv.tensor_tensor(out=kq[0], in0=ttab[:, 0], in1=sini_b, op=ALU.mult), writes=[tB[0]])
                SV(lambda: dv.tensor_tensor(out=kq[1], in0=ttab[:, 1], in1=sinr_b, op=ALU.mult), writes=[tB[0]])
                SV(lambda: dv.tensor_tensor(out=kbuf[:, 1], in0=kq[0], in1=kq[1], op=ALU.add), reads=[tB[0]], writes=[kB])
                w4 = [a32(OFF_E + 1024 * i, 1024).rearrange("p (j s c) -> p j s c", j=4, s=NSEG) for i in range(4)]
                w4B = [Buf(f"w4_{i}") for i in range(4)]
                inh = {}
                for b_ in EBf:
                    for k_, v_ in ([b_.w] if b_.w else []) + list(b_.r.items()):
                        if inh.get(k_, 0) < v_:
                            inh[k_] = v_
                for b_ in w4B:
                    b_.r = dict(inh)
                arena_live.extend(w4B)
                for bt_ in range(8):
                    wv = wbuf[:, bt_ % 2]
                    wb_ = wB[bt_ % 2]
                    CR = c32b[:, 4 * bt_:4 * bt_ + 4, 0, :].unsqueeze(2).to_broadcast([128, 4, NSEG, 32])
                    CN = c32b[:, 4 * bt_:4 * bt_ + 4, 1, :].unsqueeze(2).to_broadcast([128, 4, NSEG, 32])
                    Kr = kbuf[:, 0, :, 4 * bt_:4 * bt_ + 4].rearrange("p s j -> p j s").unsqueeze(3).to_broadcast([128, 4, NSEG, 32])
                    Ki = kbuf[:, 1, :, 4 * bt_:4 * bt_ + 4].rearrange("p s j -> p j s").unsqueeze(3).to_broadcast([128, 4, NSEG, 32])
                    T.op(DVE, lambda: dv.tensor_tensor(out=w4[0], in0=CR, in1=Kr, op=ALU.mult), reads=[kB, constB], writes=[w4B[0]])
                    T.op(DVE, lambda: dv.tensor_tensor(out=w4[1], in0=CN, in1=Ki, op=ALU.mult), reads=[kB, constB], writes=[w4B[1]])
                    T.op(DVE, lambda: dv.tensor_tensor(out=wv[:, 0], in0=w4[0], in1=w4[1], op=ALU.add), reads=[w4B[0]], writes=[wb_])
                    T.op(POOL, lambda: gp.tensor_tensor(out=w4[2], in0=CN, in1=Kr, op=ALU.mult), reads=[kB, constB], writes=[w4B[2]])
                    T.op(POOL, lambda: gp.tensor_tensor(out=w4[3], in0=CR, in1=Ki, op=ALU.mult), reads=[kB, constB], writes=[w4B[3]])
                    T.op(POOL, lambda: gp.tensor_tensor(out=wv[:, 1], in0=w4[2], in1=w4[3], op=ALU.subtract), reads=[w4B[2]], writes=[wb_])
                    cps, cpb = nextps()
                    fns = []
                    for j in range(4):
                        for s_ in range(NSEG):
                            for r in range(2):
                                fns.append(lambda j=j, s_=s_, r=r: pe.matmul(cps[32 * j:32 * j + 32, SEG * s_:SEG * (s_ + 1)], lhsT=wv[:, r, j, s_, :],
                                                                            rhs=ptab[:, 4 * bt_ + j, r, :], start=(r == 0), stop=(r == 1),
                                                                            tile_position=(0, 32 * j)))
                    T.group(PE, fns, reads=[wb_, constB], writes=[cpb])
                    T.op(DVE, lambda: dv.tensor_tensor(out=tmp[2 + bt_ % 2], in0=ymain[:, bt_, :], in1=cps[:], op=ALU.add),
                         reads=ymB(bt_) + [cpb], writes=[tB[2 + bt_ % 2]])
                    T.op(ACT, lambda: ac.activation(out=y1t[bt_], in_=tmp[2 + bt_ % 2], func=AF.Gelu_apprx_tanh),
                         reads=[tB[2 + bt_ % 2]], writes=y1B[bt_])
                scope(f"b{blk_i}_glu_out")
                wgluv = wglu_d.rearrange("(k p) c -> p k c", p=128)
                for c in range(4):
                    w_ap, wb = load_w(wview_k(8, 256), wgluv[:, :, c * 256:(c + 1) * 256])
                    for jj in range(2):
                        i = 2 * c + jj
                        pst, psb = nextps()
                        T.group(PE, [(lambda k=k: pe.matmul(pst[:], lhsT=w_ap[:, k, jj * 128:(jj + 1) * 128], rhs=y1t[k],
                                                           start=(k == 0), stop=(k == 7))) for k in range(8)],
                                reads=[wb] + [x_ for l_ in y1B for x_ in l_], writes=[psb])
                        T.op(ACT, lambda: ac.activation(out=tmp[4 + i % 2], in_=pst[:], func=AF.Sigmoid), reads=[psb], writes=[tB[4 + i % 2]])
                        T.op(DVE, lambda: dv.tensor_tensor(out=zb[:, i, :], in0=y1t[i], in1=tmp[4 + i % 2], op=ALU.mult),
                             reads=y1B[i] + [tB[4 + i % 2]], writes=[zbB[i]])
                norm([(zb[:, i, :], zbB[i]) for i in range(8)], cols_t[:, C_GS:C_GS + 8], xnT[0:8], DSSM, sqB, rtB)
                woutv = wout_d.rearrange("(k p) c -> p k c", p=128)
                for c in range(8):
                    w_ap, wb = load_w(wview_k(16, 256), woutv[:, :, c * 256:(c + 1) * 256])
                    for jj in range(2):
                        i = 2 * c + jj
                        pst, psb = nextps()
                        T.group(PE, [(lambda k=k: pe.matmul(pst[:], lhsT=w_ap[:, k, jj * 128:(jj + 1) * 128], rhs=xn_t[:, k, :],
                                                           start=(k == 0), stop=(k == KT - 1))) for k in range(KT)],
                                reads=[wb] + xnB, writes=[psb])
                        T.op(DVE, lambda: dv.tensor_tensor(out=h_t[:, i, :], in0=pst[:], in1=h_t[:, i, :], op=ALU.add), reads=[psb], writes=[hB[i]])
            if stage >= 3:
                scope(f"b{blk_i}_ffn2")
                norm(hT, cols_t[:, C_G2:C_G2 + 16], xnT, D, sqB, rtB)
                fb = ffn(w2g_d, w2u_d, w2d_d)
            if stage >= 4:
                scope(f"b{blk_i}_ple")
                norm(hT, cols_t[:, C_GP:C_GP + 16], xnT, D, sqB, rtB)
                wpgv = wpg_d.rearrange("(k p) c -> p k c", p=128)
                wppb = fb[12]
                wpp_ap = a16(18432, 4096).rearrange("p (k c) -> p k c", k=2)
                T.dma(POOL, wpp_ap, wpp_d.rearrange("(k p) c -> p k c", p=128), wppb, writes=[wppb])
                siluB = fb[10:12]
                for c in range(8):
                    w_ap, wb = load_w(wview_k(16, 256), wpgv[:, :, c * 256:(c + 1) * 256])
                    for jj in range(2):
                        i = 2 * c + jj
                        gps, gpb = nextps()
                        pps, ppb = nextps()
                        T.group(PE, [(lambda k=k: pe.matmul(gps[:], lhsT=w_ap[:, k, jj * 128:(jj + 1) * 128], rhs=xn_t[:, k, :],
                                                           start=(k == 0), stop=(k == KT - 1))) for k in range(KT)],
                                reads=[wb] + xnB, writes=[gpb])
                        T.group(PE, [(lambda k=k: pe.matmul(pps[:], lhsT=wpp_ap[:, k, i * 128:(i + 1) * 128], rhs=pT[:, k, :],
                                                           start=(k == 0), stop=(k == 1))) for k in range(2)],
                                reads=[wppb, ptB], writes=[ppb])
                        sl = a32(OFF_SILU + 512 * jj, 512)
                        T.op(ACT, lambda: ac.activation(out=sl, in_=gps[:], func=AF.Sigmoid), reads=[gpb], writes=[siluB[jj]])
                        T.op(DVE, lambda: dv.tensor_tensor(out=sl, in0=sl, in1=pps[:], op=ALU.mult), reads=[ppb], writes=[siluB[jj]])
                        T.op(DVE, lambda: dv.tensor_tensor(out=h_t[:, i, :], in0=sl, in1=h_t[:, i, :], op=ALU.add), reads=[siluB[jj]], writes=[hB[i]])
            scope(f"b{blk_i}_store")
            ob = new_phase_bufs(["ost0", "ost1"])
            ostB = ob[0:2]
            norm(hT, cols_t[:, C_GF:C_GF + 16], hT, D, sqB, rtB)
            for tt in range(4):
                ost = a32(OFF_OST + 2048 * (tt % 2), 2048)
                for q in range(4):
                    pst, psb = nextps()
                    T.group(PE, [(lambda kk=kk: pe.transpose(pst[:, kk * 128:(kk + 1) * 128],
                                                            h_t[:, 4 * q + kk, tt * 128:(tt + 1) * 128], ident_t[:]))
                                 for kk in range(4)], reads=hB[4 * q:4 * q + 4] + [constB], writes=[psb])
                    T.op(ACT, lambda: ac.activation(out=ost[:, q * 512:(q + 1) * 512], in_=pst[:], func=AF.Copy),
                         reads=[psb], writes=[ostB[tt % 2]])
                T.dma(SP, out_d[t0 + tt * 128:t0 + (tt + 1) * 128, :], ost, ostB[tt % 2], reads=[ostB[tt % 2]])
        scope(None)
        for nm in ("ost0", "ost1"):
            ent = T.dsem_by_name[nm]
            SP.e.wait_ge(T.sems[ent[0]], ent[1])
        nc.n_sems_used = len(T.sems)
    return nc


def _host_consts(inp):
    f32 = np.float32
    L = 0

    def col16(v):
        return np.ascontiguousarray(v.reshape(-1, 128).T)

    cols = np.zeros((128, NCOL), f32)
    cols[:, C_G1:C_G1 + 16] = col16(inp["norm_ffn1"][L])
    cols[:, C_GM:C_GM + 16] = col16(inp["norm_mix"][L])
    cols[:, C_G2:C_G2 + 16] = col16(inp["norm_ffn2"][L])
    cols[:, C_GP:C_GP + 16] = col16(inp["norm_ple"][L])
    cols[:, C_GF:C_GF + 16] = col16(inp["norm_final"])
    cols[:, C_GS:C_GS + 8] = col16(inp["norm_ssm_out"][L])
    cols[:, C_GG:C_GG + 8] = col16(inp["norm_gmlp_out"][L])
    cols[:, C_DD:C_DD + 8] = col16(inp["ssm_d"][L])
    nvbc = np.ascontiguousarray(np.broadcast_to(inp["gmlp_norm_v"][L][None, :], (128, 1024))).astype(f32)
    bsbc = np.ascontiguousarray(np.broadcast_to(inp["gmlp_b_s"][L].reshape(1, 1024), (128, 1024))).astype(f32)
    wst = np.ascontiguousarray(inp["gmlp_w_s"][L].transpose(2, 0, 1).reshape(128, 1024)).astype(f32)
    mask = np.triu(np.ones((128, 128), f32))
    ident = np.eye(128, dtype=f32)
    tau = np.ascontiguousarray(np.broadcast_to(np.arange(SEG, dtype=f32)[None, :], (128, SEG)))

    def alay(a):
        return a.reshape(32, 2, 64).transpose(1, 2, 0).reshape(128, 32)

    ldt = np.broadcast_to(inp["ssm_log_dt"][L][:, None], (64, 64))
    ssa = np.concatenate([alay(ldt), alay(inp["ssm_a_re"][L]), alay(inp["ssm_a_im"][L])], axis=1).astype(f32)

    def blay(b):
        return b.reshape(32, 2, 64, 16).transpose(1, 2, 0, 3).reshape(128, 512)

    ssb = np.stack([blay(inp["ssm_b_re"][L]), blay(inp["ssm_b_im"][L])], axis=1).astype(f32)

    def clay(c):
        c4 = c.reshape(32, 2, 16, 64)
        o = np.zeros((2, 64, 32, 2, 16), f32)
        for g2 in range(2):
            o[g2, :, :, g2, :] = c4[:, g2, :, :].transpose(2, 0, 1)
        return o.reshape(128, 1024)

    ssc = np.stack([clay(inp["ssm_c_re"][L]), clay(inp["ssm_c_im"][L])], axis=1).astype(f32)
    return dict(cols=cols, nvbc=nvbc, bsbc=bsbc, wst=wst, mask=mask, ident=ident, tau=tau, ssa=ssa, ssb=ssb, ssc=ssc)


_NC_CACHE = {}
BLOCKS_OF = [[0, 3, 4, 7], [1, 2, 5, 6]]
FIRST_RANK = [0, 1, 0, 1]


def _flags(rank):
    f = np.zeros((128, 12), np.float32)
    for k in range(4):
        if FIRST_RANK[k] == rank:
            f[:, 3 * k + 0] = 1.0
        elif rank == 0:
            f[:, 3 * k + 2] = 1.0
        else:
            f[:, 3 * k + 1] = 1.0
    return f


def kernel(**inp):
    L = 0
    consts = _host_consts(inp)
    shared = dict(
        w1g=np.ascontiguousarray(inp["w1_gate"][L]), w1u=np.ascontiguousarray(inp["w1_up"][L]), w1d=np.ascontiguousarray(inp["w1_down"][L]),
        w2g=np.ascontiguousarray(inp["w2_gate"][L]), w2u=np.ascontiguousarray(inp["w2_up"][L]), w2d=np.ascontiguousarray(inp["w2_down"][L]),
        win=np.ascontiguousarray(inp["w_in"][L]), wglu=np.ascontiguousarray(inp["ssm_w_glu"][L]), wout=np.ascontiguousarray(inp["w_out"][L]),
        wpg=np.ascontiguousarray(inp["w_ple_gate"][L]), wpp=np.ascontiguousarray(inp["w_ple_proj"][L]), **consts)
    nblk = 4
    key = ("v2", nblk)
    if key not in _NC_CACHE:
        _NC_CACHE[key] = build_nc(nblk, use_cc=True)
    nc = _NC_CACHE[key]
    x = inp["x"]
    p = inp["p"][L]
    in_maps = []
    for c in range(N_CORES):
        b, r = c // 2, c % 2
        m = dict(shared)
        m["x"] = np.ascontiguousarray(np.concatenate([x[b, g * TB:(g + 1) * TB] for g in BLOCKS_OF[r]], axis=0))
        m["p"] = np.ascontiguousarray(np.concatenate([p[b, g * TB:(g + 1) * TB] for g in BLOCKS_OF[r]], axis=0))
        m["flags"] = _flags(r)
        in_maps.append(m)
    res = run_bass_kernel_spmd(nc, in_maps, core_ids=list(range(N_CORES)))
    out = np.empty((4, SEQ, D), np.float32)
    for c in range(N_CORES):
        b, r = c // 2, c % 2
        o = res.results[c]["out"]
        for k, g in enumerate(BLOCKS_OF[r]):
            out[b, g * TB:(g + 1) * TB] = o[k * TB:(k + 1) * TB]
    return out
```

```python
import contextlib
import math
import numpy as np
import concourse.bass as bass
import concourse.mybir as mybir
from concourse.bass_utils import run_bass_kernel_spmd

F32 = mybir.dt.float32
BF16 = mybir.dt.bfloat16
I32 = mybir.dt.int32
AF = mybir.ActivationFunctionType
ALU = mybir.AluOpType

D = 2048
DFF = 5632
DSSM = 1024
TB = 512
KT = D // 128
SEG = 64
NSEG = TB // SEG
EPS = 1e-6
N_CORES = 8
SEQ = 4096

C_G1, C_GM, C_G2, C_GP, C_GF = 0, 16, 32, 48, 64
C_GS, C_GG, C_DD = 80, 88, 96
NCOL = 104


class Buf:
    __slots__ = ("name", "w", "r", "dsem", "dcnt")

    def __init__(self, name):
        self.name = name
        self.w = None
        self.r = {}
        self.dsem = None
        self.dcnt = 0


class Eng:
    def __init__(self, e, semidx):
        self.e = e
        self.semidx = semidx
        self.n = 0
        self.known = {}


class Tracker:
    def __init__(self, nc, es):
        self.nc = nc
        self.es = es
        self.sems = []
        self.dsem_by_name = {}

    def new_sem(self, name):
        s = self.es.enter_context(self.nc.semaphore(name))
        self.sems.append(s)
        return len(self.sems) - 1

    def eng(self, e, name):
        return Eng(e, self.new_sem("e_" + name))

    def _deps(self, reads, writes):
        deps = {}

        def add(k, v):
            if deps.get(k, 0) < v:
                deps[k] = v
        for b in reads:
            if b.w is not None:
                add(*b.w)
        for b in writes:
            if b.w is not None:
                add(*b.w)
            for k, v in b.r.items():
                add(k, v)
        return deps

    def _wait(self, E, deps):
        for k, v in deps.items():
            if E.known.get(k, 0) < v:
                E.e.wait_ge(self.sems[k], v)
                E.known[k] = v

    def _mark(self, tok, reads, writes):
        k, v = tok
        for b in reads:
            if b.r.get(k, 0) < v:
                b.r[k] = v
        for b in writes:
            b.w = tok
            b.r = {}

    def op(self, E, fn, reads=(), writes=()):
        self._wait(E, self._deps(reads, writes))
        ins = fn()
        E.n += 1
        ins.then_inc(self.sems[E.semidx], 1)
        tok = (E.semidx, E.n)
        self._mark(tok, reads, writes)
        return tok

    def group(self, E, fns, reads=(), writes=(), stagger=None):
        self._wait(E, self._deps(reads, writes))
        ins = None
        for i, fn in enumerate(fns):
            if stagger is not None:
                self._wait(E, self._deps([stagger[i]], ()))
            ins = fn()
        E.n += 1
        ins.then_inc(self.sems[E.semidx], 1)
        tok = (E.semidx, E.n)
        self._mark(tok, list(reads) + (list(stagger) if stagger is not None else []), writes)
        return tok

    def custom(self, E, fn, name, reads=(), writes=()):
        self._wait(E, self._deps(reads, writes))
        ins = fn()
        k = self.new_sem(name)
        ins.then_inc(self.sems[k], 1)
        tok = (k, 1)
        self._mark(tok, reads, writes)
        return tok

    def dma(self, E, out, in_, dbuf, reads=(), writes=(), **kw):
        if dbuf.name not in self.dsem_by_name:
            self.dsem_by_name[dbuf.name] = [self.new_sem("d_" + dbuf.name), 0]
        ent = self.dsem_by_name[dbuf.name]
        self._wait(E, self._deps(reads, writes))
        ins = E.e.dma_start(out=out, in_=in_, **kw)
        ent[1] += 16
        ins.then_inc(self.sems[ent[0]], 16)
        tok = (ent[0], ent[1])
        self._mark(tok, reads, writes)
        return tok


def build_nc(nblk, stage=99, use_cc=False):
    ntok = nblk * TB
    nc = bass.Bass("TRN2", target_bir_lowering=False)

    def din(name, shape):
        return nc.dram_tensor(name, list(shape), F32, kind="ExternalInput").ap()

    x_d = din("x", [ntok, D])
    p_d = din("p", [ntok, 256])
    out_d = nc.dram_tensor("out", [ntok, D], F32, kind="ExternalOutput").ap()
    w1g_d, w1u_d, w1d_d = din("w1g", [D, DFF]), din("w1u", [D, DFF]), din("w1d", [DFF, D])
    w2g_d, w2u_d, w2d_d = din("w2g", [D, DFF]), din("w2u", [D, DFF]), din("w2d", [DFF, D])
    win_d = din("win", [D, 3072])
    wglu_d = din("wglu", [DSSM, DSSM])
    wout_d = din("wout", [D, D])
    wpg_d = din("wpg", [D, D])
    wpp_d = din("wpp", [256, D])
    cols_d = din("cols", [128, NCOL])
    nvbc_d = din("nvbc", [128, 1024])
    bsbc_d = din("bsbc", [128, 1024])
    wst_d = din("wst", [128, 1024])
    mask_d = din("mask", [128, 128])
    ident_d = din("ident", [128, 128])
    tau_d = din("tau", [128, SEG])
    ssa_d = din("ssa", [128, 96])
    ssb_d = din("ssb", [128, 2, 512])
    ssc_d = din("ssc", [128, 2, 1024])
    if use_cc:
        flags_d = din("flags", [128, 3 * nblk])
        ccin_d = [nc.dram_tensor(f"cc_in{k}", [128, 64], F32) for k in range(nblk)]
        ccout_d = [nc.dram_tensor(f"cc_out{k}", [256, 64], F32) for k in range(nblk)]

    es = contextlib.ExitStack()
    with es:
        T = Tracker(nc, es)
        PE = T.eng(nc.tensor, "pe")
        ACT = T.eng(nc.scalar, "act")
        DVE = T.eng(nc.vector, "dve")
        POOL = T.eng(nc.gpsimd, "pool")
        SP = T.eng(nc.sync, "sp")

        def sb(name, shape, dt):
            return es.enter_context(nc.sbuf_tensor("sb_" + name, list(shape), dt))

        h_t = sb("h", [128, KT, TB], F32)
        xn_t = sb("xn", [128, KT, TB], BF16)
        hB = [Buf(f"h{i}") for i in range(KT)]
        xnB = [Buf(f"xn{i}") for i in range(KT)]
        cols_t = sb("cols", [128, NCOL], F32)
        ident_t = sb("ident", [128, 128], F32)
        ones_t = sb("ones", [128, 128], BF16)
        nvbc_t = sb("nvbc", [128, 1024], F32)
        bsbc_t = sb("bsbc", [128, 1024], F32)
        wct_t = sb("wct", [128, 1024], BF16)
        cosT = sb("cosT", [128, 32, SEG], F32)
        sinT = sb("sinT", [128, 32, SEG], F32)
        mtab = sb("mtab", [128, 32, SEG], F32)
        bbT = sb("bbT", [128, 8, 2, 128], BF16)
        c32b = sb("c32b", [128, 32, 2, 32], BF16)
        rotm = sb("rotm", [128, 2, 2, 32], F32)
        carry = sb("carry", [128, 2, 32], F32)
        small = sb("small", [128, 64], F32)
        ptab = sb("ptab", [128, 32, 2, SEG], BF16)
        ttab = sb("ttab", [128, 2, NSEG, 32], F32)
        a512 = sb("a512", [128, 2, 32], F32)
        sprev = sb("sprev", [128, 2, 32], F32)
        ssms = sb("ssms", [128, 12, 32], F32)
        gbuf = sb("gbuf", [128, 2, 64], F32)
        kbuf = sb("kbuf", [128, 2, NSEG, 32], F32)
        wbuf = sb("wbuf", [128, 1, 2, 4, NSEG, 32], BF16)
        flags_t = sb("flags", [128, 3 * nblk], F32)
        ssmsB, gB, kB = Buf("ssms"), Buf("gbuf"), Buf("kbuf")
        wB = [Buf("wbuf0"), Buf("wbuf1")]
        sprevB = Buf("sprev")
        constB = Buf("const")
        carryB = [Buf(f"carry{i}") for i in range(8)]
        smallB = Buf("small")

        ARENA_W = 24576
        arena = sb("arena", [128, ARENA_W], F32)
        SLOT_W = 2048

        def a32(off, n):
            return arena[:, off:off + n]

        def a16(off, n_bf):
            return arena[:, off:off + n_bf // 2].bitcast(BF16)

        NSLOT_FFN = 8
        slot_off = [i * SLOT_W for i in range(NSLOT_FFN)]
        OFF_ACT = 16384
        OFF_SILU = 17408
        OFF_XIN = 0
        OFF_PIN = 4096
        OFF_OST = 4096
        OFF_ZB = 4096
        OFF_GU = OFF_ZB + 2048
        OFF_V = OFF_GU + 2048
        OFF_E = OFF_V + 2048
        OFF_S = OFF_E + 4096
        OFF_TMP = OFF_S + 1024
        assert OFF_TMP + 3072 == 18432
        OFF_SQ = 22528
        OFF_RT = 23040
        OFF_PT = 24064
        sqB = [Buf("sq0"), Buf("sq1")]
        rtB = [Buf("rt"), Buf("rstd")]
        ptB = Buf("pt")

        ps_t = [es.enter_context(nc.psum_tensor(f"ps{i}", [128, 512], F32)) for i in range(8)]
        psB = [Buf(f"ps{i}") for i in range(8)]
        ps_ctr = [0]

        def nextps():
            i = ps_ctr[0] % 8
            ps_ctr[0] += 1
            return ps_t[i], psB[i]

        arena_live = []

        def new_phase_bufs(names):
            inherit = {}
            for b in arena_live:
                if b.w is not None:
                    k, v = b.w
                    if inherit.get(k, 0) < v:
                        inherit[k] = v
                for k, v in b.r.items():
                    if inherit.get(k, 0) < v:
                        inherit[k] = v
            out = []
            for n in names:
                b = Buf(n)
                b.r = dict(inherit)
                out.append(b)
            arena_live.clear()
            arena_live.extend(out)
            return out

        scope_state = {"cm": None}

        def scope(name):
            if scope_state["cm"] is not None:
                scope_state["cm"].__exit__(None, None, None)
                scope_state["cm"] = None
            if name is not None:
                cm = nc.named_scope(name)
                cm.__enter__()
                scope_state["cm"] = cm

        scope("setup")

        def cload(dst, src):
            T.dma(SP, dst, src, constB, writes=[constB])

        cload(cols_t[:], cols_d)
        cload(ident_t[:], ident_d)
        cload(nvbc_t[:], nvbc_d)
        cload(bsbc_t[:], bsbc_d)
        if use_cc:
            cload(flags_t[:], flags_d)
        setupB = new_phase_bufs(["setup"])[0]
        wst_s = a32(0, 1024)
        mask_s = a32(1024, 128)
        tau_s = a32(1152, SEG)
        ssa_s = a32(1216, 96)
        ssb_s = a32(1312, 1024)
        ssc_s = a32(2336, 2048)
        cload(wst_s, wst_d)
        cload(mask_s, mask_d)
        cload(tau_s, tau_d)
        cload(ssa_s, ssa_d)
        cload(ssb_s, ssb_d.rearrange("p a b -> p (a b)"))
        cload(ssc_s, ssc_d.rearrange("p a b -> p (a b)"))
        W0 = 4384

        def V(fn, reads=(), writes=()):
            return T.op(DVE, fn, reads=list(reads) + [constB], writes=list(writes) + [setupB])

        def A(fn, reads=(), writes=()):
            return T.op(ACT, fn, reads=list(reads) + [constB], writes=list(writes) + [setupB])

        dv, ac = nc.vector, nc.scalar
        V(lambda: dv.memset(ones_t[:], 1.0))
        V(lambda: dv.memset(carry[:], 0.0))
        V(lambda: dv.tensor_tensor(out=wct_t[:].rearrange("p (h t) -> p h t", h=8),
                                   in0=wst_s.rearrange("p (h t) -> p h t", h=8),
                                   in1=mask_s.unsqueeze(1).to_broadcast([128, 8, 128]), op=ALU.mult))
        ldt, are, aim = ssa_s[:, 0:32], ssa_s[:, 32:64], ssa_s[:, 64:96]
        sc = [a32(W0 + 32 * i, 32) for i in range(24)]
        dtv, lr, mag, ang, cs, sn, abr, abi, den, xr, zr, zi, u1, u2, u3, u4 = sc[:16]
        A(lambda: ac.activation(out=dtv, in_=ldt, func=AF.Exp))
        V(lambda: dv.tensor_scalar(out=lr, in0=are, scalar1=-1e-4, scalar2=None, op0=ALU.min))
        lrdt = sc[16]
        V(lambda: dv.tensor_tensor(out=lrdt, in0=lr, in1=dtv, op=ALU.mult))
        A(lambda: ac.activation(out=mag, in_=lrdt, func=AF.Exp))
        V(lambda: dv.tensor_tensor(out=ang, in0=aim, in1=dtv, op=ALU.mult))

        BIGW = W0 + 1024

        def sincos(src, n, out_sin, out_cos, scale=1.0):
            y = a32(BIGW, n)
            yi = a32(BIGW + n, n).bitcast(I32)
            yf = a32(BIGW + 2 * n, n)
            m1 = a32(BIGW + 3 * n, n)
            for off, dst in ((0.0, out_sin), (0.25, out_cos)):
                V(lambda: dv.tensor_scalar(out=y, in0=src, scalar1=scale / (2 * math.pi), scalar2=off,
                                           op0=ALU.mult, op1=ALU.add))
                V(lambda: dv.tensor_copy(out=yi, in_=y))
                V(lambda: dv.tensor_copy(out=yf, in_=yi))
                V(lambda: dv.tensor_tensor(out=y, in0=y, in1=yf, op=ALU.subtract))
                V(lambda: dv.tensor_scalar(out=m1, in0=y, scalar1=0.5, scalar2=None, op0=ALU.is_gt))
                V(lambda: dv.tensor_tensor(out=y, in0=y, in1=m1, op=ALU.subtract))
                V(lambda: dv.tensor_scalar(out=m1, in0=y, scalar1=-0.5, scalar2=None, op0=ALU.is_lt))
                V(lambda: dv.tensor_tensor(out=y, in0=y, in1=m1, op=ALU.add))
                A(lambda: ac.activation(out=dst, in_=y, func=AF.Sin, scale=2 * math.pi))

        sincos(ang, 32, sn, cs)
        V(lambda: dv.tensor_tensor(out=abr, in0=mag, in1=cs, op=ALU.mult))
        V(lambda: dv.tensor_tensor(out=abi, in0=mag, in1=sn, op=ALU.mult))
        V(lambda: dv.tensor_tensor(out=u1, in0=lr, in1=lr, op=ALU.mult))
        V(lambda: dv.tensor_tensor(out=u2, in0=aim, in1=aim, op=ALU.mult))
        V(lambda: dv.tensor_tensor(out=den, in0=u1, in1=u2, op=ALU.add))
        V(lambda: dv.reciprocal(out=den, in_=den))
        V(lambda: dv.tensor_scalar(out=xr, in0=abr, scalar1=-1.0, scalar2=None, op0=ALU.add))
        V(lambda: dv.tensor_tensor(out=u1, in0=xr, in1=lr, op=ALU.mult))
        V(lambda: dv.tensor_tensor(out=u2, in0=abi, in1=aim, op=ALU.mult))
        V(lambda: dv.tensor_tensor(out=u1, in0=u1, in1=u2, op=ALU.add))
        V(lambda: dv.tensor_tensor(out=zr, in0=u1, in1=den, op=ALU.mult))
        V(lambda: dv.tensor_tensor(out=u1, in0=abi, in1=lr, op=ALU.mult))
        V(lambda: dv.tensor_tensor(out=u2, in0=xr, in1=aim, op=ALU.mult))
        V(lambda: dv.tensor_tensor(out=u1, in0=u1, in1=u2, op=ALU.subtract))
        V(lambda: dv.tensor_tensor(out=zi, in0=u1, in1=den, op=ALU.mult))
        bre = ssb_s[:, 0:512].rearrange("p (g q) -> p g q", q=16)
        bim = ssb_s[:, 512:1024].rearrange("p (g q) -> p g q", q=16)
        bw = BIGW + 8192
        bt1 = a32(bw, 512).rearrange("p (g q) -> p g q", q=16)
        bt2 = a32(bw + 512, 512).rearrange("p (g q) -> p g q", q=16)
        bbr = a32(bw + 1024, 512).rearrange("p (g q) -> p g q", q=16)
        bbi = a32(bw + 1536, 512).rearrange("p (g q) -> p g q", q=16)
        blk = [a32(bw + 2048, 1024), a32(bw + 3072, 1024)]
        zrb = zr.unsqueeze(2).to_broadcast([128, 32, 16])
        zib = zi.unsqueeze(2).to_broadcast([128, 32, 16])
        V(lambda: dv.tensor_tensor(out=bt1, in0=bre, in1=zrb, op=ALU.mult))
        V(lambda: dv.tensor_tensor(out=bt2, in0=bim, in1=zib, op=ALU.mult))
        V(lambda: dv.tensor_tensor(out=bbr, in0=bt1, in1=bt2, op=ALU.subtract))
        V(lambda: dv.tensor_tensor(out=bt1, in0=bim, in1=zrb, op=ALU.mult))
        V(lambda: dv.tensor_tensor(out=bt2, in0=bre, in1=zib, op=ALU.mult))
        V(lambda: dv.tensor_tensor(out=bbi, in0=bt1, in1=bt2, op=ALU.add))
        for ri, src in ((0, bbr), (1, bbi)):
            b3 = blk[ri].rearrange("p (g c) -> p g c", c=32)
            V(lambda: dv.memset(blk[ri], 0.0))
            V(lambda: dv.tensor_copy(out=b3[0:64, :, 0:16], in_=src[0:64]))
            V(lambda: dv.tensor_copy(out=b3[64:128, :, 16:32], in_=src[64:128]))
            for bt_ in range(8):
                pst, psb = nextps()
                T.op(PE, lambda: nc.tensor.transpose(pst[:, 0:128], blk[ri][:, bt_ * 128:(bt_ + 1) * 128], ident_t[:]),
                     reads=[setupB, constB], writes=[psb])
                T.op(DVE, lambda: dv.tensor_copy(out=bbT[:, bt_, ri, :], in_=pst[:, 0:128]), reads=[psb], writes=[setupB])
        cre = ssc_s[:, 0:1024].rearrange("p (g c) -> p g c", c=32)
        cim = ssc_s[:, 1024:2048].rearrange("p (g c) -> p g c", c=32)
        V(lambda: dv.tensor_copy(out=c32b[:, :, 0, :], in_=cre))
        V(lambda: dv.tensor_scalar(out=c32b[:, :, 1, :], in0=cim, scalar1=-1.0, scalar2=None, op0=ALU.mult))
        ph = a32(BIGW + 8192 + 4096, 2048)
        V(lambda: dv.tensor_tensor(out=ph.rearrange("p (g t) -> p g t", t=SEG),
                                   in0=ang.unsqueeze(2).to_broadcast([128, 32, SEG]),
                                   in1=tau_s.unsqueeze(1).to_broadcast([128, 32, SEG]), op=ALU.mult))
        sincos(ph, 2048, sinT[:].rearrange("p g t -> p (g t)"), cosT[:].rearrange("p g t -> p (g t)"))
        tm = a32(BIGW + 8192 + 4096 + 2048, SEG)
        V(lambda: dv.tensor_scalar(out=tm, in0=tau_s, scalar1=0.5, scalar2=None, op0=ALU.is_gt))
        V(lambda: dv.tensor_tensor(out=mtab[:], in0=mag.unsqueeze(2).to_broadcast([128, 32, SEG]),
                                   in1=tm.unsqueeze(1).to_broadcast([128, 32, SEG]), op=ALU.mult))
        sincos(ang, 32, u3, u4, scale=float(SEG))
        V(lambda: dv.tensor_tensor(out=rotm[:, 0, 0, :], in0=u4, in1=mag, op=ALU.mult))
        V(lambda: dv.tensor_copy(out=rotm[:, 0, 1, :], in_=rotm[:, 0, 0, :]))
        V(lambda: dv.tensor_tensor(out=rotm[:, 1, 1, :], in0=u3, in1=mag, op=ALU.mult))
        V(lambda: dv.tensor_scalar(out=rotm[:, 1, 0, :], in0=rotm[:, 1, 1, :], scalar1=-1.0, scalar2=None, op0=ALU.mult))
        V(lambda: dv.memset(sprev[:], 0.0))
        phm = a32(BIGW + 8192 + 4096, 2048)
        V(lambda: dv.tensor_tensor(out=phm.rearrange("p (g t) -> p g t", t=SEG),
                                   in0=lrdt.unsqueeze(2).to_broadcast([128, 32, SEG]),
                                   in1=tau_s.unsqueeze(1).to_broadcast([128, 32, SEG]), op=ALU.mult))
        A(lambda: ac.activation(out=phm, in_=phm, func=AF.Exp))
        V(lambda: dv.tensor_tensor(out=ptab[:, :, 0, :], in0=phm.rearrange("p (g t) -> p g t", t=SEG), in1=cosT[:], op=ALU.mult))
        V(lambda: dv.tensor_tensor(out=ptab[:, :, 1, :], in0=phm.rearrange("p (g t) -> p g t", t=SEG), in1=sinT[:], op=ALU.mult))
        pw = [(sc[17], sc[18]), (sc[19], sc[20])]
        V(lambda: dv.tensor_copy(out=pw[0][0], in_=abr))
        V(lambda: dv.tensor_copy(out=pw[0][1], in_=abi))
        a64 = (sc[21], sc[22])
        cur = 0
        for it in range(9):
            r_, i_ = pw[cur]
            nr, ni = pw[1 - cur]
            V(lambda: dv.tensor_tensor(out=u1, in0=r_, in1=r_, op=ALU.mult))
            V(lambda: dv.tensor_tensor(out=u2, in0=i_, in1=i_, op=ALU.mult))
            V(lambda: dv.tensor_tensor(out=nr, in0=u1, in1=u2, op=ALU.subtract))
            V(lambda: dv.tensor_tensor(out=u1, in0=r_, in1=i_, op=ALU.mult))
            V(lambda: dv.tensor_scalar(out=ni, in0=u1, scalar1=2.0, scalar2=None, op0=ALU.mult))
            cur = 1 - cur
            if it == 5:
                V(lambda: dv.tensor_copy(out=a64[0], in_=nr))
                V(lambda: dv.tensor_copy(out=a64[1], in_=ni))
        V(lambda: dv.tensor_copy(out=a512[:, 0, :], in_=pw[cur][0]))
        V(lambda: dv.tensor_copy(out=a512[:, 1, :], in_=pw[cur][1]))
        V(lambda: dv.tensor_copy(out=ttab[:, 0, 0, :], in_=abr))
        V(lambda: dv.tensor_copy(out=ttab[:, 1, 0, :], in_=abi))
        for s_ in range(1, NSEG):
            pr, pi = ttab[:, 0, s_ - 1, :], ttab[:, 1, s_ - 1, :]
            V(lambda: dv.tensor_tensor(out=u1, in0=pr, in1=a64[0], op=ALU.mult))
            V(lambda: dv.tensor_tensor(out=u2, in0=pi, in1=a64[1], op=ALU.mult))
            V(lambda: dv.tensor_tensor(out=ttab[:, 0, s_, :], in0=u1, in1=u2, op=ALU.subtract))
            V(lambda: dv.tensor_tensor(out=u1, in0=pr, in1=a64[1], op=ALU.mult))
            V(lambda: dv.tensor_tensor(out=u2, in0=pi, in1=a64[0], op=ALU.mult))
            V(lambda: dv.tensor_tensor(out=ttab[:, 1, s_, :], in0=u1, in1=u2, op=ALU.add))
        T.op(DVE, lambda: dv.memset(small[:], 0.0), reads=[setupB, constB], writes=[smallB, constB])

        ring = {"slots": [], "i": 0, "offs": slot_off}

        def set_ring(nslots):
            bufs = [Buf(f"slot{j}") for j in range(nslots)]
            ring["slots"] = bufs
            ring["i"] = 0
            return bufs

        def load_w(view_fn, src):
            j = ring["i"] % len(ring["slots"])
            ring["i"] += 1
            b = ring["slots"][j]
            sl = a16(ring["offs"][j], 4096)
            dst = view_fn(sl)
            T.dma(POOL, dst, src, b, writes=[b])
            return dst, b

        def wview_k(nk, ncols):
            return lambda sl: sl[:, 0:nk * ncols].rearrange("p (k c) -> p k c", k=nk)

        pe, gp = nc.tensor, nc.gpsimd

        def norm(srcs, gcol, dsts, dim, sqB, rtB, in_place_fp32=False):
            n = len(srcs)
            pst, psb = nextps()
            for i, (sap, sbuf) in enumerate(srcs):
                sq = a16(OFF_SQ + 256 * (i % 2), 512)
                T.op(ACT, lambda: ac.activation(out=sq, in_=sap, func=AF.Square), reads=[sbuf], writes=[sqB[i % 2]])
                T.op(PE, lambda: pe.matmul(pst[:], lhsT=ones_t[:], rhs=sq, start=(i == 0), stop=(i == n - 1)),
                     reads=[sqB[i % 2], constB], writes=[psb])
            rt = a32(OFF_RT, 512)
            rstd = a32(OFF_RT + 512, 512)
            T.op(ACT, lambda: ac.activation(out=rt, in_=pst[:], func=AF.Sqrt, scale=1.0 / dim, bias=small[:, 1:2]),
                 reads=[psb, smallB], writes=[rtB[0]])
            T.op(DVE, lambda: dv.reciprocal(out=rstd, in_=rt), reads=[rtB[0]], writes=[rtB[1]])
            for i, (sap, sbuf) in enumerate(srcs):
                dap, dbuf = dsts[i]
                T.op(DVE, lambda: dv.scalar_tensor_tensor(out=dap, in0=sap, scalar=gcol[:, i:i + 1], in1=rstd,
                                                          op0=ALU.mult, op1=ALU.mult),
                     reads=[sbuf, rtB[1], constB], writes=[dbuf])

        T.op(DVE, lambda: dv.memset(small[:, 1:2], EPS), reads=[constB], writes=[smallB])

        hT = [(h_t[:, i, :], hB[i]) for i in range(KT)]
        xnT = [(xn_t[:, i, :], xnB[i]) for i in range(KT)]

        def ffn(wg_d, wu_d, wd_d):
            set_ring(NSLOT_FFN)
            fb = new_phase_bufs([f"slot{j}" for j in range(NSLOT_FFN)] + ["act0", "act1", "silu0", "silu1", "wpp"])
            ring["slots"] = fb[:NSLOT_FFN]
            ring["offs"] = slot_off
            actB, siluB = fb[8:10], fb[10:12]
            wgv = wg_d.rearrange("(k p) c -> p k c", p=128)
            wuv = wu_d.rearrange("(k p) c -> p k c", p=128)
            wdv = wd_d.rearrange("(j p) f -> p j f", p=128)
            NCH = DFF // 256
            pend = None

            def down(c, wd_ap, wdb, actv, ab):
                for i in range(KT):
                    pst, psb = nextps()
                    T.group(PE, [(lambda jj=jj: pe.matmul(pst[:], lhsT=wd_ap[:, jj, i * 128:(i + 1) * 128], rhs=actv[:, jj, :],
                                                         start=(jj == 0), stop=(jj == 1))) for jj in range(2)],
                            reads=[wdb, ab], writes=[psb])
                    T.op(DVE, lambda: dv.scalar_tensor_tensor(out=h_t[:, i, :], in0=pst[:], scalar=0.5, in1=h_t[:, i, :],
                                                              op0=ALU.mult, op1=ALU.add),
                         reads=[psb], writes=[hB[i]])

            for c in range(NCH):
                wg_ap, wgb = load_w(wview_k(16, 256), wgv[:, :, c * 256:(c + 1) * 256])
                wu_ap, wub = load_w(wview_k(16, 256), wuv[:, :, c * 256:(c + 1) * 256])
                wd_ap, wdb = load_w(wview_k(2, 2048), wdv[:, 2 * c:2 * c + 2, :])
                actv = a16(OFF_ACT + 512 * (c % 2), 1024).rearrange("p (j t) -> p j t", j=2)
                ab = actB[c % 2]
                for jj in range(2):
                    gps, gpb = nextps()
                    ups, upb = nextps()
                    T.group(PE, [(lambda k=k: pe.matmul(gps[:], lhsT=wg_ap[:, k, jj * 128:(jj + 1) * 128], rhs=xn_t[:, k, :],
                                                       start=(k == 0), stop=(k == KT - 1))) for k in range(KT)],
                            reads=[wgb], stagger=xnB, writes=[gpb])
                    T.group(PE, [(lambda k=k: pe.matmul(ups[:], lhsT=wu_ap[:, k, jj * 128:(jj + 1) * 128], rhs=xn_t[:, k, :],
                                                       start=(k == 0), stop=(k == KT - 1))) for k in range(KT)],
                            reads=[wub], stagger=xnB, writes=[upb])
                    sl = a32(OFF_SILU + 512 * jj, 512)
                    T.op(ACT, lambda: ac.activation(out=sl, in_=gps[:], func=AF.Silu), reads=[gpb], writes=[siluB[jj]])
                    T.op(DVE, lambda: dv.tensor_tensor(out=actv[:, jj, :], in0=sl, in1=ups[:], op=ALU.mult),
                         reads=[siluB[jj], upb], writes=[ab])
                if pend is not None:
                    down(*pend)
                pend = (c, wd_ap, wdb, actv, ab)
            down(*pend)
            return fb

        for blk_i in range(nblk):
            t0 = blk_i * TB
            scope(f"b{blk_i}_load")
            lb = new_phase_bufs(["xin0", "xin1", "pin"])
            xinB, pinB = lb[0:2], lb[2]
            for tt in range(4):
                xin = a32(OFF_XIN + 2048 * (tt % 2), 2048)
                T.dma(SP, xin, x_d[t0 + tt * 128:t0 + (tt + 1) * 128, :], xinB[tt % 2], writes=[xinB[tt % 2]])
                for q in range(4):
                    pst, psb = nextps()
                    T.group(PE, [(lambda kk=kk: pe.transpose(pst[:, kk * 128:(kk + 1) * 128],
                                                            xin[:, (4 * q + kk) * 128:(4 * q + kk + 1) * 128], ident_t[:]))
                                 for kk in range(4)], reads=[xinB[tt % 2], constB], writes=[psb])
                    T.op(ACT, lambda: ac.activation(out=h_t[:, 4 * q:4 * q + 4, tt * 128:(tt + 1) * 128],
                                                    in_=pst[:].rearrange("p (a b) -> p a b", a=4), func=AF.Copy),
                         reads=[psb], writes=hB[4 * q:4 * q + 4])
            pin = a32(OFF_PIN, 1024).rearrange("p (a b) -> p a b", a=4)
            T.dma(SP, pin, p_d[t0:t0 + TB, :].rearrange("(a p) c -> p a c", p=128), pinB, writes=[pinB])
            pT = a16(OFF_PT, 1024).rearrange("p (k t) -> p k t", k=2)
            for k2 in range(2):
                pst, psb = nextps()
                T.group(PE, [(lambda a=a: pe.transpose(pst[:, a * 128:(a + 1) * 128], pin[:, a, k2 * 128:(k2 + 1) * 128], ident_t[:]))
                             for a in range(4)], reads=[pinB, constB], writes=[psb])
                T.op(ACT, lambda: ac.activation(out=pT[:, k2, :], in_=pst[:], func=AF.Copy), reads=[psb], writes=[ptB])

            if stage >= 1:
                scope(f"b{blk_i}_ffn1")
                norm(hT, cols_t[:, C_G1:C_G1 + 16], xnT, D, sqB, rtB)
                fb = ffn(w1g_d, w1u_d, w1d_d)
            if stage >= 2:
                mb = new_phase_bufs(["slot0", "slot1", "slot2"] + [f"zb{i}" for i in range(8)] + [f"gu{i}" for i in range(8)]
                                    + [f"v{i}" for i in range(4)] + [f"E{i}" for i in range(12)] + ["S0", "S1"] + [f"t{i}" for i in range(6)])
                ring["slots"] = mb[0:3]
                ring["i"] = 0
                ring["offs"] = [0, 2048, 18432]
                zbB, guB, vB = mb[3:11], mb[11:19], mb[19:23]
                EBf, SB_, tB = mb[23:35], mb[35:37], mb[37:43]

                def EBc(buf, r, jj):
                    return EBf[buf * 4 + r * 2 + jj]

                def EBall(buf):
                    return EBf[buf * 4:buf * 4 + 4]
                zb = a16(OFF_ZB, 4096).rearrange("p (k t) -> p k t", k=8)
                gu = a16(OFF_GU, 4096).rearrange("p (k t) -> p k t", k=8)
                vv = a16(OFF_V, 4096).rearrange("p (a f) -> p a f", a=4)
                Ev2 = [a32(o_, 2048).rearrange("p (r s j t) -> p r s j t", r=2, s=NSEG, j=2) for o_ in (OFF_E, OFF_E + 2048, 20480)]
                tmp = [a32(OFF_TMP + 512 * i, 512) for i in range(6)]
                y1t = [a16(OFF_GU + 512 * i, 512) for i in range(8)]

                def ymB(b_):
                    return [guB[2 * b_], guB[2 * b_ + 1]] if b_ < 4 else [vB[b_ - 4]]
                y1B = [ymB(i) for i in range(8)]
                scope(f"b{blk_i}_inproj")
                norm(hT, cols_t[:, C_GM:C_GM + 16], xnT, D, sqB, rtB)
                winv = win_d.rearrange("(k p) c -> p k c", p=128)
                for c in range(8):
                    w_ap, wb = load_w(wview_k(16, 256), winv[:, :, c * 256:(c + 1) * 256])
                    for jj in range(2):
                        ft = 2 * c + jj
                        pst, psb = nextps()
                        T.group(PE, [(lambda k=k: pe.matmul(pst[:], lhsT=w_ap[:, k, jj * 128:(jj + 1) * 128], rhs=xn_t[:, k, :],
                                                           start=(k == 0), stop=(k == KT - 1))) for k in range(KT)],
                                reads=[wb], stagger=xnB, writes=[psb])
                        if ft < 8:
                            T.op(ACT, lambda: ac.activation(out=zb[:, ft, :], in_=pst[:], func=AF.Copy), reads=[psb], writes=[zbB[ft]])
                        else:
                            T.op(ACT, lambda: ac.activation(out=gu[:, ft - 8, :], in_=pst[:], func=AF.Gelu_apprx_tanh),
                                 reads=[psb], writes=[guB[ft - 8]])
                for c in range(4):
                    w_ap, wb = load_w(wview_k(16, 256), winv[:, :, 2048 + c * 256:2048 + (c + 1) * 256])
                    for tt in range(4):
                        pst, psb = nextps()
                        T.group(PE, [(lambda k=k: pe.matmul(pst[:, 0:256], lhsT=xn_t[:, k, tt * 128:(tt + 1) * 128], rhs=w_ap[:, k, :],
                                                           start=(k == 0), stop=(k == KT - 1))) for k in range(KT)],
                                reads=[wb], stagger=xnB, writes=[psb])
                        T.op(ACT, lambda: ac.activation(out=vv[:, tt, c * 256:(c + 1) * 256], in_=pst[:, 0:256], func=AF.Gelu_apprx_tanh),
                             reads=[psb], writes=[vB[tt]])
                scope(f"b{blk_i}_gmlp")
                t1024 = a32(OFF_TMP, 1024)
                for tt in range(4):
                    st = small[:, 8:20]
                    T.op(DVE, lambda: dv.bn_stats(out=small[:, 8:14], in_=vv[:, tt, 0:512]), reads=[vB[tt]], writes=[smallB])
                    T.op(DVE, lambda: dv.bn_stats(out=small[:, 14:20], in_=vv[:, tt, 512:1024]), reads=[vB[tt]], writes=[smallB])
                    T.op(DVE, lambda: dv.bn_aggr(out=small[:, 20:22], in_=st), reads=[smallB], writes=[smallB])
                    T.op(ACT, lambda: ac.activation(out=small[:, 22:23], in_=small[:, 21:22], func=AF.Sqrt, bias=small[:, 1:2]),
                         reads=[smallB], writes=[smallB])
                    T.op(DVE, lambda: dv.reciprocal(out=small[:, 23:24], in_=small[:, 22:23]), reads=[smallB], writes=[smallB])
                    T.op(DVE, lambda: dv.tensor_scalar(out=t1024, in0=vv[:, tt, :], scalar1=small[:, 20:21], scalar2=small[:, 23:24],
                                                       op0=ALU.subtract, op1=ALU.mult),
                         reads=[vB[tt], smallB], writes=[tB[0], tB[1]])
                    T.op(DVE, lambda: dv.tensor_tensor(out=vv[:, tt, :], in0=t1024, in1=nvbc_t[:], op=ALU.mult),
                         reads=[tB[0], tB[1], constB], writes=[vB[tt]])
                for hd in range(8):
                    pst, psb = nextps()
                    T.group(PE, [(lambda tt=tt: pe.matmul(pst[:, tt * 128:(tt + 1) * 128], lhsT=vv[:, tt, hd * 128:(hd + 1) * 128],
                                                         rhs=wct_t[:, hd * 128:(hd + 1) * 128], start=True, stop=True)) for tt in range(4)],
                            reads=vB + [constB], writes=[psb])
                    tq = tmp[2 + hd % 2]
                    T.op(DVE, lambda: dv.tensor_tensor(out=tq.rearrange("p (a t) -> p a t", a=4), in0=pst[:].rearrange("p (a t) -> p a t", a=4),
                                                       in1=bsbc_t[:, hd * 128:(hd + 1) * 128].unsqueeze(1).to_broadcast([128, 4, 128]), op=ALU.add),
                         reads=[psb, constB], writes=[tB[2 + hd % 2]])
                    T.op(DVE, lambda: dv.tensor_tensor(out=gu[:, hd, :], in0=gu[:, hd, :], in1=tq, op=ALU.mult),
                         reads=[tB[2 + hd % 2]], writes=[guB[hd]])
                norm([(gu[:, i, :], guB[i]) for i in range(8)], cols_t[:, C_GG:C_GG + 8], xnT[8:16], DSSM, sqB, rtB)
                scope(f"b{blk_i}_ssm")
                ymain = a32(OFF_GU, 4096).rearrange("p (k t) -> p k t", k=8)
                ypsd = {}

                def st_A(b2):
                    Eb, eb = Ev2[b2 % 3], b2 % 3
                    bt_ = b2 // 2
                    for jj in range(2):
                        j = 2 * (b2 % 2) + jj
                        g_ = 2 * b2 + jj
                        dre, dreb = nextps()
                        dim_, dimb = nextps()
                        T.op(PE, lambda: pe.matmul(dre[:], lhsT=bbT[32 * j:32 * j + 32, bt_, 0, :], rhs=zb[32 * j:32 * j + 32, bt_, :],
                                                   start=True, stop=True, tile_position=(32 * j, 0)), reads=[zbB[bt_], constB], writes=[dreb])
                        T.op(PE, lambda: pe.matmul(dim_[:], lhsT=bbT[32 * j:32 * j + 32, bt_, 1, :], rhs=zb[32 * j:32 * j + 32, bt_, :],
                                                   start=True, stop=True, tile_position=(32 * j, 0)), reads=[zbB[bt_], constB], writes=[dimb])
                        cb = cosT[:, g_, :].unsqueeze(1).to_broadcast([128, NSEG, SEG])
                        sbb = sinT[:, g_, :].unsqueeze(1).to_broadcast([128, NSEG, SEG])
                        d3r = dre[:].rearrange("p (s t) -> p s t", s=NSEG)
                        d3i = dim_[:].rearrange("p (s t) -> p s t", s=NSEG)
                        ta, tb_ = 0, 1
                        r1 = tmp[ta].rearrange("p (s t) -> p s t", s=NSEG)
                        r3 = tmp[tb_].rearrange("p (s t) -> p s t", s=NSEG)
                        ere, eim = Eb[:, 0, :, jj, :], Eb[:, 1, :, jj, :]
                        T.op(DVE, lambda: dv.tensor_tensor(out=ere, in0=d3r, in1=cb, op=ALU.mult), reads=[dreb, constB], writes=[EBc(eb, 0, jj)])
                        T.op(DVE, lambda: dv.tensor_tensor(out=r1, in0=d3i, in1=sbb, op=ALU.mult), reads=[dimb, constB], writes=[tB[ta]])
                        T.op(DVE, lambda: dv.tensor_tensor(out=eim, in0=d3i, in1=cb, op=ALU.mult), reads=[dimb, constB], writes=[EBc(eb, 1, jj)])
                        T.op(DVE, lambda: dv.tensor_tensor(out=r3, in0=d3r, in1=sbb, op=ALU.mult), reads=[dreb, constB], writes=[tB[tb_]])
                        T.op(POOL, lambda: gp.tensor_tensor(out=ere, in0=ere, in1=r1, op=ALU.add), reads=[tB[ta]], writes=[EBc(eb, 0, jj)])
                        T.op(POOL, lambda: gp.tensor_tensor(out=eim, in0=eim, in1=r3, op=ALU.subtract), reads=[tB[tb_]], writes=[EBc(eb, 1, jj)])

                def st_B(b2):
                    Eb, eb = Ev2[b2 % 3], b2 % 3
                    all4 = EBall(eb)
                    mt2 = mtab[:, 2 * b2:2 * b2 + 2, :].rearrange("p g t -> p (g t)")
                    rc = rotm[:, 0, :, 2 * b2:2 * b2 + 2]
                    rs = rotm[:, 1, :, 2 * b2:2 * b2 + 2]
                    for s_ in range(NSEG):
                        if s_ > 0:
                            last = Eb[:, :, s_ - 1, :, SEG - 1]
                            i1 = small[:, 24:28].rearrange("p (r j) -> p r j", r=2)
                            i2 = small[:, 32:36].rearrange("p (r j) -> p r j", r=2)
                            T.op(DVE, lambda: dv.tensor_tensor(out=i1, in0=last, in1=rc, op=ALU.mult), reads=all4 + [constB], writes=[smallB])
                            T.op(DVE, lambda: dv.tensor_tensor(out=i2[:, 0, :], in0=last[:, 1, :], in1=rs[:, 0, :], op=ALU.mult),
                                 reads=all4 + [constB], writes=[smallB])
                            T.op(DVE, lambda: dv.tensor_tensor(out=i2[:, 1, :], in0=last[:, 0, :], in1=rs[:, 1, :], op=ALU.mult),
                                 reads=all4 + [constB], writes=[smallB])
                            T.op(DVE, lambda: dv.tensor_tensor(out=i1, in0=i1, in1=i2, op=ALU.add), reads=[smallB], writes=[smallB])
                            T.op(DVE, lambda: dv.tensor_tensor(out=Eb[:, :, s_, :, 0], in0=Eb[:, :, s_, :, 0], in1=i1, op=ALU.add),
                                 reads=[smallB], writes=all4)
                        for r in range(2):
                            er = Eb[:, r, s_, :, :].rearrange("p j t -> p (j t)")
                            T.op(DVE, lambda: dv.tensor_tensor_scan(out=er, data0=mt2, data1=er, initial=0.0, op0=ALU.mult, op1=ALU.add),
                                 reads=[constB], writes=[EBc(eb, r, 0), EBc(eb, r, 1)])
                    T.op(DVE, lambda: dv.tensor_copy(out=carry[:, :, 2 * b2:2 * b2 + 2], in_=Eb[:, :, NSEG - 1, :, SEG - 1]),
                         reads=all4, writes=[carryB[b2 // 2]])

                def st_C(b2):
                    Eb, eb = Ev2[b2 % 3], b2 % 3
                    for jj in range(2):
                        g_ = 2 * b2 + jj
                        cb = cosT[:, g_, :].unsqueeze(1).to_broadcast([128, NSEG, SEG])
                        sbb = sinT[:, g_, :].unsqueeze(1).to_broadcast([128, NSEG, SEG])
                        rre, rim = Eb[:, 0, :, jj, :], Eb[:, 1, :, jj, :]
                        u0 = tmp[2 + 2 * jj].rearrange("p (s t) -> p s t", s=NSEG)
                        u1 = tmp[3 + 2 * jj].rearrange("p (s t) -> p s t", s=NSEG)
                        T.op(POOL, lambda: gp.tensor_tensor(out=u0, in0=rre, in1=cb, op=ALU.mult), reads=[EBc(eb, 0, jj), constB], writes=[tB[2 + 2 * jj]])
                        T.op(POOL, lambda: gp.tensor_tensor(out=u1, in0=rim, in1=sbb, op=ALU.mult), reads=[EBc(eb, 1, jj), constB], writes=[tB[3 + 2 * jj]])
                        T.op(POOL, lambda: gp.tensor_tensor(out=rre, in0=rre, in1=sbb, op=ALU.mult), reads=[constB], writes=[EBc(eb, 0, jj)])
                        T.op(POOL, lambda: gp.tensor_tensor(out=rim, in0=rim, in1=cb, op=ALU.mult), reads=[constB], writes=[EBc(eb, 1, jj)])
                        sv = a16(OFF_S + 512 * jj, 1024).rearrange("p (r t) -> p r t", r=2)
                        s3i = sv[:, 1, :].rearrange("p (s t) -> p s t", s=NSEG)
                        T.op(POOL, lambda: gp.tensor_tensor(out=s3i, in0=rre, in1=rim, op=ALU.add),
                             reads=[EBc(eb, 0, jj), EBc(eb, 1, jj)], writes=[SB_[jj]])

                def st_D(b2):
                    Eb, eb = Ev2[b2 % 3], b2 % 3
                    bt_ = b2 // 2
                    if b2 % 2 == 0:
                        ypsd[bt_] = nextps()
                    yps, ypb = ypsd[bt_]
                    for jj in range(2):
                        j = 2 * (b2 % 2) + jj
                        g_ = 2 * b2 + jj
                        rre, rim = Eb[:, 0, :, jj, :], Eb[:, 1, :, jj, :]
                        u0 = tmp[2 + 2 * jj].rearrange("p (s t) -> p s t", s=NSEG)
                        u1 = tmp[3 + 2 * jj].rearrange("p (s t) -> p s t", s=NSEG)
                        sv = a16(OFF_S + 512 * jj, 1024).rearrange("p (r t) -> p r t", r=2)
                        s3 = [sv[:, r, :].rearrange("p (s t) -> p s t", s=NSEG) for r in range(2)]
                        T.op(DVE, lambda: dv.tensor_tensor(out=s3[0], in0=u0, in1=u1, op=ALU.subtract),
                             reads=[tB[2 + 2 * jj], tB[3 + 2 * jj]], writes=[SB_[jj]])
                        T.group(PE, [(lambda r=r: pe.matmul(yps[32 * j:32 * j + 32, :], lhsT=c32b[:, g_, r, :], rhs=sv[:, r, :],
                                                           start=(r == 0), stop=(r == 1), tile_position=(0, 32 * j))) for r in range(2)],
                                reads=[SB_[jj], constB], writes=[ypb])
                    if b2 % 2 == 1:
                        T.op(DVE, lambda: dv.scalar_tensor_tensor(out=ymain[:, bt_, :], in0=zb[:, bt_, :], scalar=cols_t[:, C_DD + bt_:C_DD + bt_ + 1],
                                                                  in1=yps[:], op0=ALU.mult, op1=ALU.add),
                             reads=[zbB[bt_], ypb, constB], writes=ymB(bt_))

                NB2 = 16
                for it in range(NB2 + 3):
                    if 0 <= it - 3 < NB2:
                        st_D(it - 3)
                    if it < NB2:
                        st_A(it)
                    if 0 <= it - 1 < NB2:
                        st_B(it - 1)
                    if 0 <= it - 2 < NB2:
                        st_C(it - 2)

                def SV(fn, reads=(), writes=()):
                    return T.op(DVE, fn, reads=list(reads) + [ssmsB, constB], writes=list(writes) + [ssmsB])
                R_ = [ssms[:, i, :] for i in range(12)]
                c63, s63 = cosT[:, :, SEG - 1], sinT[:, :, SEG - 1]
                lfr, lfi = carry[:, 0, :], carry[:, 1, :]
                sllr, slli, spr, spi, sinr, sini, q1, q2 = R_[0], R_[1], R_[2], R_[3], R_[4], R_[5], R_[6], R_[7]
                SV(lambda: dv.tensor_tensor(out=q1, in0=lfr, in1=c63, op=ALU.mult), reads=carryB)
                SV(lambda: dv.tensor_tensor(out=q2, in0=lfi, in1=s63, op=ALU.mult), reads=carryB)
                SV(lambda: dv.tensor_tensor(out=sllr, in0=q1, in1=q2, op=ALU.subtract))
                SV(lambda: dv.tensor_tensor(out=q1, in0=lfr, in1=s63, op=ALU.mult), reads=carryB)
                SV(lambda: dv.tensor_tensor(out=q2, in0=lfi, in1=c63, op=ALU.mult), reads=carryB)
                SV(lambda: dv.tensor_tensor(out=slli, in0=q1, in1=q2, op=ALU.add))

                def sout_from(xr, xi, outr, outi, extra_r=(), extra_w=()):
                    SV(lambda: dv.tensor_tensor(out=q1, in0=a512[:, 0, :], in1=xr, op=ALU.mult), reads=extra_r)
                    SV(lambda: dv.tensor_tensor(out=q2, in0=a512[:, 1, :], in1=xi, op=ALU.mult), reads=extra_r)
                    SV(lambda: dv.tensor_tensor(out=q1, in0=q1, in1=q2, op=ALU.subtract))
                    SV(lambda: dv.tensor_tensor(out=q1, in0=q1, in1=sllr, op=ALU.add))
                    SV(lambda: dv.tensor_tensor(out=q2, in0=a512[:, 0, :], in1=xi, op=ALU.mult), reads=extra_r)
                    SV(lambda: dv.tensor_tensor(out=R_[8], in0=a512[:, 1, :], in1=xr, op=ALU.mult), reads=extra_r)
                    SV(lambda: dv.tensor_tensor(out=q2, in0=q2, in1=R_[8], op=ALU.add))
                    SV(lambda: dv.tensor_tensor(out=outi, in0=q2, in1=slli, op=ALU.add), writes=extra_w)
                    SV(lambda: dv.tensor_copy(out=outr, in_=q1), writes=extra_w)

                sout_from(sprev[:, 0, :], sprev[:, 1, :], spr, spi, extra_r=[sprevB])
                ccinB, ccoutB = Buf("ccin"), Buf("ccout")
                T.dma(SP, ccin_d[blk_i].ap(), ssms[:, 2:4, :].rearrange("p a b -> p (a b)"), ccinB, reads=[ssmsB], writes=[ccinB])
                T.custom(POOL, lambda: gp.collective_compute("AllGather", ALU.bypass, replica_groups=[[0, 1], [2, 3], [4, 5], [6, 7]],
                                                             ins=[ccin_d[blk_i].ap().opt()], outs=[ccout_d[blk_i].ap().opt()]),
                         f"cc{blk_i}", reads=[ccinB], writes=[ccoutB])
                T.dma(SP, gbuf[:], ccout_d[blk_i].ap().rearrange("(r p) n -> p r n", p=128), gB, reads=[ccoutB], writes=[gB])
                fl = flags_t[:, 3 * blk_i:3 * blk_i + 3]
                sin64 = ssms[:, 4:6, :].rearrange("p a b -> p (a b)")
                sp64 = sprev[:].rearrange("p a b -> p (a b)")
                SV(lambda: dv.tensor_scalar(out=sin64, in0=sp64, scalar1=fl[:, 0:1], scalar2=None, op0=ALU.mult), reads=[sprevB])
                SV(lambda: dv.scalar_tensor_tensor(out=sin64, in0=gbuf[:, 0, :], scalar=fl[:, 1:2], in1=sin64, op0=ALU.mult, op1=ALU.add), reads=[gB])
                SV(lambda: dv.scalar_tensor_tensor(out=sin64, in0=gbuf[:, 1, :], scalar=fl[:, 2:3], in1=sin64, op0=ALU.mult, op1=ALU.add), reads=[gB])
                sout_from(sinr, sini, sprev[:, 0, :], sprev[:, 1, :], extra_w=[sprevB])
                sinr_b = sinr.unsqueeze(1).to_broadcast([128, NSEG, 32])
                sini_b = sini.unsqueeze(1).to_broadcast([128, NSEG, 32])
                kq = [a32(OFF_TMP + 256 * i, 256).rearrange("p (s g) -> p s g", s=NSEG) for i in range(2)]
                SV(lambda: dv.tensor_tensor(out=kq[0], in0=ttab[:, 0], in1=sinr_b, op=ALU.mult), writes=[tB[0]])
                SV(lambda: dv.tensor_tensor(out=kq[1], in0=ttab[:, 1], in1=sini_b, op=ALU.mult), writes=[tB[0]])
                SV(lambda: dv.tensor_tensor(out=kbuf[:, 0], in0=kq[0], in1=kq[1], op=ALU.subtract), reads=[tB[0]], writes=[kB])
                SV(lambda: dv.tensor_tensor(out=kq[0], in0=ttab[:, 0], in1=sini_b, op=ALU.mult), writes=[tB[0]])
                SV(lambda: dv.tensor_tensor(out=kq[1], in0=ttab[:, 1], in1=sinr_b, op=ALU.mult), writes=[tB[0]])
                SV(lambda: dv.tensor_tensor(out=kbuf[:, 1], in0=kq[0], in1=kq[1], op=ALU.add), reads=[tB[0]], writes=[kB])
                w4 = [a32(OFF_E + 1024 * i, 1024).rearrange("p (j s c) -> p j s c", j=4, s=NSEG) for i in range(4)]
                w4B = [Buf(f"w4_{i}") for i in range(4)]
                inh = {}
                for b_ in EBf:
                    for k_, v_ in ([b_.w] if b_.w else []) + list(b_.r.items()):
                        if inh.get(k_, 0) < v_:
                            inh[k_] = v_
                for b_ in w4B:
                    b_.r = dict(inh)
                arena_live.extend(w4B)
                for bt_ in range(8):
                    wv = wbuf[:, 0]
                    wb_ = wB[0]
                    CR = c32b[:, 4 * bt_:4 * bt_ + 4, 0, :].unsqueeze(2).to_broadcast([128, 4, NSEG, 32])
                    CN = c32b[:, 4 * bt_:4 * bt_ + 4, 1, :].unsqueeze(2).to_broadcast([128, 4, NSEG, 32])
                    Kr = kbuf[:, 0, :, 4 * bt_:4 * bt_ + 4].rearrange("p s j -> p j s").unsqueeze(3).to_broadcast([128, 4, NSEG, 32])
                    Ki = kbuf[:, 1, :, 4 * bt_:4 * bt_ + 4].rearrange("p s j -> p j s").unsqueeze(3).to_broadcast([128, 4, NSEG, 32])
                    T.op(DVE, lambda: dv.tensor_tensor(out=w4[0], in0=CR, in1=Kr, op=ALU.mult), reads=[kB, constB], writes=[w4B[0]])
                    T.op(DVE, lambda: dv.tensor_tensor(out=w4[1], in0=CN, in1=Ki, op=ALU.mult), reads=[kB, constB], writes=[w4B[1]])
                    T.op(DVE, lambda: dv.tensor_tensor(out=wv[:, 0], in0=w4[0], in1=w4[1], op=ALU.add), reads=[w4B[0]], writes=[wb_])
                    T.op(POOL, lambda: gp.tensor_tensor(out=w4[2], in0=CN, in1=Kr, op=ALU.mult), reads=[kB, constB], writes=[w4B[2]])
                    T.op(POOL, lambda: gp.tensor_tensor(out=w4[3], in0=CR, in1=Ki, op=ALU.mult), reads=[kB, constB], writes=[w4B[3]])
                    T.op(POOL, lambda: gp.tensor_tensor(out=wv[:, 1], in0=w4[2], in1=w4[3], op=ALU.subtract), reads=[w4B[2]], writes=[wb_])
                    cps, cpb = nextps()
                    fns = []
                    for j in range(4):
                        for s_ in range(NSEG):
                            for r in range(2):
                                fns.append(lambda j=j, s_=s_, r=r: pe.matmul(cps[32 * j:32 * j + 32, SEG * s_:SEG * (s_ + 1)], lhsT=wv[:, r, j, s_, :],
                                                                            rhs=ptab[:, 4 * bt_ + j, r, :], start=(r == 0), stop=(r == 1),
                                                                            tile_position=(0, 32 * j)))
                    T.group(PE, fns, reads=[wb_, constB], writes=[cpb])
                    T.op(DVE, lambda: dv.tensor_tensor(out=tmp[2 + bt_ % 2], in0=ymain[:, bt_, :], in1=cps[:], op=ALU.add),
                         reads=ymB(bt_) + [cpb], writes=[tB[2 + bt_ % 2]])
                    T.op(ACT, lambda: ac.activation(out=y1t[bt_], in_=tmp[2 + bt_ % 2], func=AF.Gelu_apprx_tanh),
                         reads=[tB[2 + bt_ % 2]], writes=y1B[bt_])
                scope(f"b{blk_i}_glu_out")
                wgluv = wglu_d.rearrange("(k p) c -> p k c", p=128)
                for c in range(4):
                    w_ap, wb = load_w(wview_k(8, 256), wgluv[:, :, c * 256:(c + 1) * 256])
                    for jj in range(2):
                        i = 2 * c + jj
                        pst, psb = nextps()
                        T.group(PE, [(lambda k=k: pe.matmul(pst[:], lhsT=w_ap[:, k, jj * 128:(jj + 1) * 128], rhs=y1t[k],
                                                           start=(k == 0), stop=(k == 7))) for k in range(8)],
                                reads=[wb] + [x_ for l_ in y1B for x_ in l_], writes=[psb])
                        T.op(ACT, lambda: ac.activation(out=tmp[4 + i % 2], in_=pst[:], func=AF.Sigmoid), reads=[psb], writes=[tB[4 + i % 2]])
                        T.op(DVE, lambda: dv.tensor_tensor(out=zb[:, i, :], in0=y1t[i], in1=tmp[4 + i % 2], op=ALU.mult),
                             reads=y1B[i] + [tB[4 + i % 2]], writes=[zbB[i]])
                norm([(zb[:, i, :], zbB[i]) for i in range(8)], cols_t[:, C_GS:C_GS + 8], xnT[0:8], DSSM, sqB, rtB)
                woutv = wout_d.rearrange("(k p) c -> p k c", p=128)
                for c in range(8):
                    w_ap, wb = load_w(wview_k(16, 256), woutv[:, :, c * 256:(c + 1) * 256])
                    for jj in range(2):
                        i = 2 * c + jj
                        pst, psb = nextps()
                        T.group(PE, [(lambda k=k: pe.matmul(pst[:], lhsT=w_ap[:, k, jj * 128:(jj + 1) * 128], rhs=xn_t[:, k, :],
                                                           start=(k == 0), stop=(k == KT - 1))) for k in range(KT)],
                                reads=[wb], stagger=xnB, writes=[psb])
                        T.op(DVE, lambda: dv.tensor_tensor(out=h_t[:, i, :], in0=pst[:], in1=h_t[:, i, :], op=ALU.add), reads=[psb], writes=[hB[i]])
            if stage >= 3:
                scope(f"b{blk_i}_ffn2")
                norm(hT, cols_t[:, C_G2:C_G2 + 16], xnT, D, sqB, rtB)
                fb = ffn(w2g_d, w2u_d, w2d_d)
            if stage >= 4:
                scope(f"b{blk_i}_ple")
                norm(hT, cols_t[:, C_GP:C_GP + 16], xnT, D, sqB, rtB)
                wpgv = wpg_d.rearrange("(k p) c -> p k c", p=128)
                wppb = fb[12]
                wpp_ap = a16(18432, 4096).rearrange("p (k c) -> p k c", k=2)
                T.dma(POOL, wpp_ap, wpp_d.rearrange("(k p) c -> p k c", p=128), wppb, writes=[wppb])
                siluB = fb[10:12]
                for c in range(8):
                    w_ap, wb = load_w(wview_k(16, 256), wpgv[:, :, c * 256:(c + 1) * 256])
                    for jj in range(2):
                        i = 2 * c + jj
                        gps, gpb = nextps()
                        pps, ppb = nextps()
                        T.group(PE, [(lambda k=k: pe.matmul(gps[:], lhsT=w_ap[:, k, jj * 128:(jj + 1) * 128], rhs=xn_t[:, k, :],
                                                           start=(k == 0), stop=(k == KT - 1))) for k in range(KT)],
                                reads=[wb], stagger=xnB, writes=[gpb])
                        T.group(PE, [(lambda k=k: pe.matmul(pps[:], lhsT=wpp_ap[:, k, i * 128:(i + 1) * 128], rhs=pT[:, k, :],
                                                           start=(k == 0), stop=(k == 1))) for k in range(2)],
                                reads=[wppb, ptB], writes=[ppb])
                        sl = a32(OFF_SILU + 512 * jj, 512)
                        T.op(ACT, lambda: ac.activation(out=sl, in_=gps[:], func=AF.Sigmoid), reads=[gpb], writes=[siluB[jj]])
                        T.op(DVE, lambda: dv.tensor_tensor(out=sl, in0=sl, in1=pps[:], op=ALU.mult), reads=[ppb], writes=[siluB[jj]])
                        T.op(DVE, lambda: dv.tensor_tensor(out=h_t[:, i, :], in0=sl, in1=h_t[:, i, :], op=ALU.add), reads=[siluB[jj]], writes=[hB[i]])
            scope(f"b{blk_i}_store")
            ob = new_phase_bufs(["ost0", "ost1"])
            ostB = ob[0:2]
            norm(hT, cols_t[:, C_GF:C_GF + 16], hT, D, sqB, rtB)
            for tt in range(4):
                ost = a32(OFF_OST + 2048 * (tt % 2), 2048)
                for q in range(4):
                    pst, psb = nextps()
                    T.group(PE, [(lambda kk=kk: pe.transpose(pst[:, kk * 128:(kk + 1) * 128],
                                                            h_t[:, 4 * q + kk, tt * 128:(tt + 1) * 128], ident_t[:]))
                                 for kk in range(4)], reads=hB[4 * q:4 * q + 4] + [constB], writes=[psb])
                    T.op(ACT, lambda: ac.activation(out=ost[:, q * 512:(q + 1) * 512], in_=pst[:], func=AF.Copy),
                         reads=[psb], writes=[ostB[tt % 2]])
                T.dma(SP, out_d[t0 + tt * 128:t0 + (tt + 1) * 128, :], ost, ostB[tt % 2], reads=[ostB[tt % 2]])
        scope(None)
        for nm in ("ost0", "ost1"):
            ent = T.dsem_by_name[nm]
            SP.e.wait_ge(T.sems[ent[0]], ent[1])
        nc.n_sems_used = len(T.sems)
    return nc


def _host_consts(inp):
    f32 = np.float32
    L = 0

    def col16(v):
        return np.ascontiguousarray(v.reshape(-1, 128).T)

    cols = np.zeros((128, NCOL), f32)
    cols[:, C_G1:C_G1 + 16] = col16(inp["norm_ffn1"][L])
    cols[:, C_GM:C_GM + 16] = col16(inp["norm_mix"][L])
    cols[:, C_G2:C_G2 + 16] = col16(inp["norm_ffn2"][L])
    cols[:, C_GP:C_GP + 16] = col16(inp["norm_ple"][L])
    cols[:, C_GF:C_GF + 16] = col16(inp["norm_final"])
    cols[:, C_GS:C_GS + 8] = col16(inp["norm_ssm_out"][L])
    cols[:, C_GG:C_GG + 8] = col16(inp["norm_gmlp_out"][L])
    cols[:, C_DD:C_DD + 8] = col16(inp["ssm_d"][L])
    nvbc = np.ascontiguousarray(np.broadcast_to(inp["gmlp_norm_v"][L][None, :], (128, 1024))).astype(f32)
    bsbc = np.ascontiguousarray(np.broadcast_to(inp["gmlp_b_s"][L].reshape(1, 1024), (128, 1024))).astype(f32)
    wst = np.ascontiguousarray(inp["gmlp_w_s"][L].transpose(2, 0, 1).reshape(128, 1024)).astype(f32)
    mask = np.triu(np.ones((128, 128), f32))
    ident = np.eye(128, dtype=f32)
    tau = np.ascontiguousarray(np.broadcast_to(np.arange(SEG, dtype=f32)[None, :], (128, SEG)))

    def alay(a):
        return a.reshape(32, 2, 64).transpose(1, 2, 0).reshape(128, 32)

    ldt = np.broadcast_to(inp["ssm_log_dt"][L][:, None], (64, 64))
    ssa = np.concatenate([alay(ldt), alay(inp["ssm_a_re"][L]), alay(inp["ssm_a_im"][L])], axis=1).astype(f32)

    def blay(b):
        return b.reshape(32, 2, 64, 16).transpose(1, 2, 0, 3).reshape(128, 512)

    ssb = np.stack([blay(inp["ssm_b_re"][L]), blay(inp["ssm_b_im"][L])], axis=1).astype(f32)

    def clay(c):
        c4 = c.reshape(32, 2, 16, 64)
        o = np.zeros((2, 64, 32, 2, 16), f32)
        for g2 in range(2):
            o[g2, :, :, g2, :] = c4[:, g2, :, :].transpose(2, 0, 1)
        return o.reshape(128, 1024)

    ssc = np.stack([clay(inp["ssm_c_re"][L]), clay(inp["ssm_c_im"][L])], axis=1).astype(f32)
    return dict(cols=cols, nvbc=nvbc, bsbc=bsbc, wst=wst, mask=mask, ident=ident, tau=tau, ssa=ssa, ssb=ssb, ssc=ssc)


_NC_CACHE = {}
BLOCKS_OF = [[0, 3, 4, 7], [1, 2, 5, 6]]
FIRST_RANK = [0, 1, 0, 1]


def _flags(rank):
    f = np.zeros((128, 12), np.float32)
    for k in range(4):
        if FIRST_RANK[k] == rank:
            f[:, 3 * k + 0] = 1.0
        elif rank == 0:
            f[:, 3 * k + 2] = 1.0
        else:
            f[:, 3 * k + 1] = 1.0
    return f


def kernel(**inp):
    L = 0
    consts = _host_consts(inp)
    shared = dict(
        w1g=np.ascontiguousarray(inp["w1_gate"][L]), w1u=np.ascontiguousarray(inp["w1_up"][L]), w1d=np.ascontiguousarray(inp["w1_down"][L]),
        w2g=np.ascontiguousarray(inp["w2_gate"][L]), w2u=np.ascontiguousarray(inp["w2_up"][L]), w2d=np.ascontiguousarray(inp["w2_down"][L]),
        win=np.ascontiguousarray(inp["w_in"][L]), wglu=np.ascontiguousarray(inp["ssm_w_glu"][L]), wout=np.ascontiguousarray(inp["w_out"][L]),
        wpg=np.ascontiguousarray(inp["w_ple_gate"][L]), wpp=np.ascontiguousarray(inp["w_ple_proj"][L]), **consts)
    nblk = 4
    key = ("v2", nblk)
    if key not in _NC_CACHE:
        _NC_CACHE[key] = build_nc(nblk, use_cc=True)
    nc = _NC_CACHE[key]
    x = inp["x"]
    p = inp["p"][L]
    in_maps = []
    for c in range(N_CORES):
        b, r = c // 2, c % 2
        m = dict(shared)
        m["x"] = np.ascontiguousarray(np.concatenate([x[b, g * TB:(g + 1) * TB] for g in BLOCKS_OF[r]], axis=0))
        m["p"] = np.ascontiguousarray(np.concatenate([p[b, g * TB:(g + 1) * TB] for g in BLOCKS_OF[r]], axis=0))
        m["flags"] = _flags(r)
        in_maps.append(m)
    res = run_bass_kernel_spmd(nc, in_maps, core_ids=list(range(N_CORES)))
    out = np.empty((4, SEQ, D), np.float32)
    for c in range(N_CORES):
        b, r = c // 2, c % 2
        o = res.results[c]["out"]
        for k, g in enumerate(BLOCKS_OF[r]):
            out[b, g * TB:(g + 1) * TB] = o[k * TB:(k + 1) * TB]
    return out
```

```python
import contextlib
import math
import numpy as np
import concourse.bass as bass
import concourse.mybir as mybir
from concourse.bass_utils import run_bass_kernel_spmd

F32 = mybir.dt.float32
BF16 = mybir.dt.bfloat16
I32 = mybir.dt.int32
AF = mybir.ActivationFunctionType
ALU = mybir.AluOpType

D = 2048
DFF = 5632
DSSM = 1024
TB = 512
KT = D // 128
SEG = 64
NSEG = TB // SEG
EPS = 1e-6
N_CORES = 8
SEQ = 4096

C_G1, C_GM, C_G2, C_GP, C_GF = 0, 16, 32, 48, 64
C_GS, C_GG, C_DD = 80, 88, 96
NCOL = 104


class Buf:
    __slots__ = ("name", "w", "r", "dsem", "dcnt")

    def __init__(self, name):
        self.name = name
        self.w = None
        self.r = {}
        self.dsem = None
        self.dcnt = 0


class Eng:
    def __init__(self, e, semidx):
        self.e = e
        self.semidx = semidx
        self.n = 0
        self.known = {}


class Tracker:
    def __init__(self, nc, es):
        self.nc = nc
        self.es = es
        self.sems = []
        self.dsem_by_name = {}

    def new_sem(self, name):
        s = self.es.enter_context(self.nc.semaphore(name))
        self.sems.append(s)
        return len(self.sems) - 1

    def eng(self, e, name):
        return Eng(e, self.new_sem("e_" + name))

    def _deps(self, reads, writes):
        deps = {}

        def add(k, v):
            if deps.get(k, 0) < v:
                deps[k] = v
        for b in reads:
            if b.w is not None:
                add(*b.w)
        for b in writes:
            if b.w is not None:
                add(*b.w)
            for k, v in b.r.items():
                add(k, v)
        return deps

    def _wait(self, E, deps):
        for k, v in deps.items():
            if E.known.get(k, 0) < v:
                E.e.wait_ge(self.sems[k], v)
                E.known[k] = v

    def _mark(self, tok, reads, writes):
        k, v = tok
        for b in reads:
            if b.r.get(k, 0) < v:
                b.r[k] = v
        for b in writes:
            b.w = tok
            b.r = {}

    def op(self, E, fn, reads=(), writes=()):
        self._wait(E, self._deps(reads, writes))
        ins = fn()
        E.n += 1
        ins.then_inc(self.sems[E.semidx], 1)
        tok = (E.semidx, E.n)
        self._mark(tok, reads, writes)
        return tok

    def group(self, E, fns, reads=(), writes=(), stagger=None):
        self._wait(E, self._deps(reads, writes))
        ins = None
        for i, fn in enumerate(fns):
            if stagger is not None:
                self._wait(E, self._deps([stagger[i]], ()))
            ins = fn()
        E.n += 1
        ins.then_inc(self.sems[E.semidx], 1)
        tok = (E.semidx, E.n)
        self._mark(tok, list(reads) + (list(stagger) if stagger is not None else []), writes)
        return tok

    def custom(self, E, fn, name, reads=(), writes=()):
        self._wait(E, self._deps(reads, writes))
        ins = fn()
        k = self.new_sem(name)
        ins.then_inc(self.sems[k], 1)
        tok = (k, 1)
        self._mark(tok, reads, writes)
        return tok

    def dma(self, E, out, in_, dbuf, reads=(), writes=(), **kw):
        if dbuf.name not in self.dsem_by_name:
            self.dsem_by_name[dbuf.name] = [self.new_sem("d_" + dbuf.name), 0]
        ent = self.dsem_by_name[dbuf.name]
        self._wait(E, self._deps(reads, writes))
        ins = E.e.dma_start(out=out, in_=in_, **kw)
        ent[1] += 16
        ins.then_inc(self.sems[ent[0]], 16)
        tok = (ent[0], ent[1])
        self._mark(tok, reads, writes)
        return tok


def build_nc(nblk, stage=99, use_cc=False):
    ntok = nblk * TB
    nc = bass.Bass("TRN2", target_bir_lowering=False)

    def din(name, shape):
        return nc.dram_tensor(name, list(shape), F32, kind="ExternalInput").ap()

    x_d = din("x", [ntok, D])
    p_d = din("p", [ntok, 256])
    out_d = nc.dram_tensor("out", [ntok, D], F32, kind="ExternalOutput").ap()
    w1g_d, w1u_d, w1d_d = din("w1g", [D, DFF]), din("w1u", [D, DFF]), din("w1d", [DFF, D])
    w2g_d, w2u_d, w2d_d = din("w2g", [D, DFF]), din("w2u", [D, DFF]), din("w2d", [DFF, D])
    win_d = din("win", [D, 3072])
    wglu_d = din("wglu", [DSSM, DSSM])
    wout_d = din("wout", [D, D])
    wpg_d = din("wpg", [D, D])
    wpp_d = din("wpp", [256, D])
    cols_d = din("cols", [128, NCOL])
    nvbc_d = din("nvbc", [128, 1024])
    bsbc_d = din("bsbc", [128, 1024])
    wst_d = din("wst", [128, 1024])
    mask_d = din("mask", [128, 128])
    ident_d = din("ident", [128, 128])
    tau_d = din("tau", [128, SEG])
    ssa_d = din("ssa", [128, 96])
    ssb_d = din("ssb", [128, 2, 512])
    ssc_d = din("ssc", [128, 2, 1024])
    if use_cc:
        flags_d = din("flags", [128, 3 * nblk])
        ccin_d = [nc.dram_tensor(f"cc_in{k}", [128, 64], F32) for k in range(nblk)]
        ccout_d = [nc.dram_tensor(f"cc_out{k}", [256, 64], F32) for k in range(nblk)]

    es = contextlib.ExitStack()
    with es:
        T = Tracker(nc, es)
        PE = T.eng(nc.tensor, "pe")
        ACT = T.eng(nc.scalar, "act")
        DVE = T.eng(nc.vector, "dve")
        POOL = T.eng(nc.gpsimd, "pool")
        SP = T.eng(nc.sync, "sp")

        def sb(name, shape, dt):
            return es.enter_context(nc.sbuf_tensor("sb_" + name, list(shape), dt))

        h_t = sb("h", [128, KT, TB], F32)
        xn_t = sb("xn", [128, KT, TB], BF16)
        hB = [Buf(f"h{i}") for i in range(KT)]
        xnB = [Buf(f"xn{i}") for i in range(KT)]
        cols_t = sb("cols", [128, NCOL], F32)
        ident_t = sb("ident", [128, 128], F32)
        ones_t = sb("ones", [128, 128], BF16)
        nvbc_t = sb("nvbc", [128, 1024], F32)
        bsbc_t = sb("bsbc", [128, 1024], F32)
        wct_t = sb("wct", [128, 1024], BF16)
        cosT = sb("cosT", [128, 32, SEG], F32)
        sinT = sb("sinT", [128, 32, SEG], F32)
        magp = sb("magp", [128, 32], F32)
        tmaskp = sb("tmaskp", [128, SEG], F32)
        mtb = sb("mtb", [128, 2, 2 * SEG], F32)
        mtB = [Buf("mtb0"), Buf("mtb1")]
        bbT = sb("bbT", [128, 8, 2, 128], BF16)
        c32b = sb("c32b", [128, 32, 2, 32], BF16)
        rotm = sb("rotm", [128, 2, 2, 32], F32)
        carry = sb("carry", [128, 2, 32], F32)
        small = sb("small", [128, 64], F32)
        ptab = sb("ptab", [128, 32, 2, SEG], BF16)
        ttab = sb("ttab", [128, 2, NSEG, 32], F32)
        a512 = sb("a512", [128, 2, 32], F32)
        sprev = sb("sprev", [128, 2, 32], F32)
        ssms = sb("ssms", [128, 12, 32], F32)
        gbuf = sb("gbuf", [128, 2, 64], F32)
        kbuf = sb("kbuf", [128, 2, NSEG, 32], F32)
        wbuf = sb("wbuf", [128, 1, 2, 4, NSEG, 32], BF16)
        flags_t = sb("flags", [128, 3 * nblk], F32)
        ssmsB, gB, kB = Buf("ssms"), Buf("gbuf"), Buf("kbuf")
        wB = [Buf("wbuf0"), Buf("wbuf1")]
        sprevB = Buf("sprev")
        constB = Buf("const")
        carryB = [Buf(f"carry{i}") for i in range(8)]
        smallB = Buf("small")
        fxB = [Buf("fx_i1"), Buf("fx_i2a"), Buf("fx_i2b")]

        ARENA_W = 24576
        arena = sb("arena", [128, ARENA_W], F32)
        SLOT_W = 2048

        def a32(off, n):
            return arena[:, off:off + n]

        def a16(off, n_bf):
            return arena[:, off:off + n_bf // 2].bitcast(BF16)

        NSLOT_FFN = 8
        slot_off = [i * SLOT_W for i in range(NSLOT_FFN)]
        OFF_ACT = 16384
        OFF_SILU = 17408
        OFF_XIN = 0
        OFF_PIN = 4096
        OFF_OST = 4096
        OFF_ZB = 4096
        OFF_GU = OFF_ZB + 2048
        OFF_V = OFF_GU + 2048
        OFF_E = OFF_V + 2048
        OFF_S = OFF_E + 4096
        OFF_TMP = OFF_S + 1024
        assert OFF_TMP + 3072 == 18432
        OFF_SQ = 22528
        OFF_RT = 23040
        OFF_PT = 24064
        sqB = [Buf("sq0"), Buf("sq1")]
        rtB = [Buf("rt"), Buf("rstd")]
        ptB = Buf("pt")

        ps_t = [es.enter_context(nc.psum_tensor(f"ps{i}", [128, 512], F32)) for i in range(8)]
        psB = [Buf(f"ps{i}") for i in range(8)]
        ps_ctr = [0]

        def nextps():
            i = ps_ctr[0] % 8
            ps_ctr[0] += 1
            return ps_t[i], psB[i]

        arena_live = []

        def new_phase_bufs(names):
            inherit = {}
            for b in arena_live:
                if b.w is not None:
                    k, v = b.w
                    if inherit.get(k, 0) < v:
                        inherit[k] = v
                for k, v in b.r.items():
                    if inherit.get(k, 0) < v:
                        inherit[k] = v
            out = []
            for n in names:
                b = Buf(n)
                b.r = dict(inherit)
                out.append(b)
            arena_live.clear()
            arena_live.extend(out)
            return out

        scope_state = {"cm": None}

        def scope(name):
            if scope_state["cm"] is not None:
                scope_state["cm"].__exit__(None, None, None)
                scope_state["cm"] = None
            if name is not None:
                cm = nc.named_scope(name)
                cm.__enter__()
                scope_state["cm"] = cm

        scope("setup")

        def cload(dst, src):
            T.dma(SP, dst, src, constB, writes=[constB])

        cload(cols_t[:], cols_d)
        cload(ident_t[:], ident_d)
        cload(nvbc_t[:], nvbc_d)
        cload(bsbc_t[:], bsbc_d)
        if use_cc:
            cload(flags_t[:], flags_d)
        setupB = new_phase_bufs(["setup"])[0]
        wst_s = a32(0, 1024)
        mask_s = a32(1024, 128)
        tau_s = a32(1152, SEG)
        ssa_s = a32(1216, 96)
        ssb_s = a32(1312, 1024)
        ssc_s = a32(2336, 2048)
        cload(wst_s, wst_d)
        cload(mask_s, mask_d)
        cload(tau_s, tau_d)
        cload(ssa_s, ssa_d)
        cload(ssb_s, ssb_d.rearrange("p a b -> p (a b)"))
        cload(ssc_s, ssc_d.rearrange("p a b -> p (a b)"))
        W0 = 4384

        def V(fn, reads=(), writes=()):
            return T.op(DVE, fn, reads=list(reads) + [constB], writes=list(writes) + [setupB])

        def A(fn, reads=(), writes=()):
            return T.op(ACT, fn, reads=list(reads) + [constB], writes=list(writes) + [setupB])

        dv, ac = nc.vector, nc.scalar
        V(lambda: dv.memset(ones_t[:], 1.0))
        V(lambda: dv.memset(carry[:], 0.0))
        V(lambda: dv.tensor_tensor(out=wct_t[:].rearrange("p (h t) -> p h t", h=8),
                                   in0=wst_s.rearrange("p (h t) -> p h t", h=8),
                                   in1=mask_s.unsqueeze(1).to_broadcast([128, 8, 128]), op=ALU.mult))
        ldt, are, aim = ssa_s[:, 0:32], ssa_s[:, 32:64], ssa_s[:, 64:96]
        sc = [a32(W0 + 32 * i, 32) for i in range(24)]
        dtv, lr, mag, ang, cs, sn, abr, abi, den, xr, zr, zi, u1, u2, u3, u4 = sc[:16]
        A(lambda: ac.activation(out=dtv, in_=ldt, func=AF.Exp))
        V(lambda: dv.tensor_scalar(out=lr, in0=are, scalar1=-1e-4, scalar2=None, op0=ALU.min))
        lrdt = sc[16]
        V(lambda: dv.tensor_tensor(out=lrdt, in0=lr, in1=dtv, op=ALU.mult))
        A(lambda: ac.activation(out=mag, in_=lrdt, func=AF.Exp))
        V(lambda: dv.tensor_tensor(out=ang, in0=aim, in1=dtv, op=ALU.mult))

        BIGW = W0 + 1024

        def sincos(src, n, out_sin, out_cos, scale=1.0):
            y = a32(BIGW, n)
            yi = a32(BIGW + n, n).bitcast(I32)
            yf = a32(BIGW + 2 * n, n)
            m1 = a32(BIGW + 3 * n, n)
            for off, dst in ((0.0, out_sin), (0.25, out_cos)):
                V(lambda: dv.tensor_scalar(out=y, in0=src, scalar1=scale / (2 * math.pi), scalar2=off,
                                           op0=ALU.mult, op1=ALU.add))
                V(lambda: dv.tensor_copy(out=yi, in_=y))
                V(lambda: dv.tensor_copy(out=yf, in_=yi))
                V(lambda: dv.tensor_tensor(out=y, in0=y, in1=yf, op=ALU.subtract))
                V(lambda: dv.tensor_scalar(out=m1, in0=y, scalar1=0.5, scalar2=None, op0=ALU.is_gt))
                V(lambda: dv.tensor_tensor(out=y, in0=y, in1=m1, op=ALU.subtract))
                V(lambda: dv.tensor_scalar(out=m1, in0=y, scalar1=-0.5, scalar2=None, op0=ALU.is_lt))
                V(lambda: dv.tensor_tensor(out=y, in0=y, in1=m1, op=ALU.add))
                A(lambda: ac.activation(out=dst, in_=y, func=AF.Sin, scale=2 * math.pi))

        sincos(ang, 32, sn, cs)
        V(lambda: dv.tensor_tensor(out=abr, in0=mag, in1=cs, op=ALU.mult))
        V(lambda: dv.tensor_tensor(out=abi, in0=mag, in1=sn, op=ALU.mult))
        V(lambda: dv.tensor_tensor(out=u1, in0=lr, in1=lr, op=ALU.mult))
        V(lambda: dv.tensor_tensor(out=u2, in0=aim, in1=aim, op=ALU.mult))
        V(lambda: dv.tensor_tensor(out=den, in0=u1, in1=u2, op=ALU.add))
        V(lambda: dv.reciprocal(out=den, in_=den))
        V(lambda: dv.tensor_scalar(out=xr, in0=abr, scalar1=-1.0, scalar2=None, op0=ALU.add))
        V(lambda: dv.tensor_tensor(out=u1, in0=xr, in1=lr, op=ALU.mult))
        V(lambda: dv.tensor_tensor(out=u2, in0=abi, in1=aim, op=ALU.mult))
        V(lambda: dv.tensor_tensor(out=u1, in0=u1, in1=u2, op=ALU.add))
        V(lambda: dv.tensor_tensor(out=zr, in0=u1, in1=den, op=ALU.mult))
        V(lambda: dv.tensor_tensor(out=u1, in0=abi, in1=lr, op=ALU.mult))
        V(lambda: dv.tensor_tensor(out=u2, in0=xr, in1=aim, op=ALU.mult))
        V(lambda: dv.tensor_tensor(out=u1, in0=u1, in1=u2, op=ALU.subtract))
        V(lambda: dv.tensor_tensor(out=zi, in0=u1, in1=den, op=ALU.mult))
        bre = ssb_s[:, 0:512].rearrange("p (g q) -> p g q", q=16)
        bim = ssb_s[:, 512:1024].rearrange("p (g q) -> p g q", q=16)
        bw = BIGW + 8192
        bt1 = a32(bw, 512).rearrange("p (g q) -> p g q", q=16)
        bt2 = a32(bw + 512, 512).rearrange("p (g q) -> p g q", q=16)
        bbr = a32(bw + 1024, 512).rearrange("p (g q) -> p g q", q=16)
        bbi = a32(bw + 1536, 512).rearrange("p (g q) -> p g q", q=16)
        blk = [a32(bw + 2048, 1024), a32(bw + 3072, 1024)]
        zrb = zr.unsqueeze(2).to_broadcast([128, 32, 16])
        zib = zi.unsqueeze(2).to_broadcast([128, 32, 16])
        V(lambda: dv.tensor_tensor(out=bt1, in0=bre, in1=zrb, op=ALU.mult))
        V(lambda: dv.tensor_tensor(out=bt2, in0=bim, in1=zib, op=ALU.mult))
        V(lambda: dv.tensor_tensor(out=bbr, in0=bt1, in1=bt2, op=ALU.subtract))
        V(lambda: dv.tensor_tensor(out=bt1, in0=bim, in1=zrb, op=ALU.mult))
        V(lambda: dv.tensor_tensor(out=bt2, in0=bre, in1=zib, op=ALU.mult))
        V(lambda: dv.tensor_tensor(out=bbi, in0=bt1, in1=bt2, op=ALU.add))
        for ri, src in ((0, bbr), (1, bbi)):
            b3 = blk[ri].rearrange("p (g c) -> p g c", c=32)
            V(lambda: dv.memset(blk[ri], 0.0))
            V(lambda: dv.tensor_copy(out=b3[0:64, :, 0:16], in_=src[0:64]))
            V(lambda: dv.tensor_copy(out=b3[64:128, :, 16:32], in_=src[64:128]))
            for bt_ in range(8):
                pst, psb = nextps()
                T.op(PE, lambda: nc.tensor.transpose(pst[:, 0:128], blk[ri][:, bt_ * 128:(bt_ + 1) * 128], ident_t[:]),
                     reads=[setupB, constB], writes=[psb])
                T.op(DVE, lambda: dv.tensor_copy(out=bbT[:, bt_, ri, :], in_=pst[:, 0:128]), reads=[psb], writes=[setupB])
        cre = ssc_s[:, 0:1024].rearrange("p (g c) -> p g c", c=32)
        cim = ssc_s[:, 1024:2048].rearrange("p (g c) -> p g c", c=32)
        V(lambda: dv.tensor_copy(out=c32b[:, :, 0, :], in_=cre))
        V(lambda: dv.tensor_scalar(out=c32b[:, :, 1, :], in0=cim, scalar1=-1.0, scalar2=None, op0=ALU.mult))
        ph = a32(BIGW + 8192 + 4096, 2048)
        V(lambda: dv.tensor_tensor(out=ph.rearrange("p (g t) -> p g t", t=SEG),
                                   in0=ang.unsqueeze(2).to_broadcast([128, 32, SEG]),
                                   in1=tau_s.unsqueeze(1).to_broadcast([128, 32, SEG]), op=ALU.mult))
        sincos(ph, 2048, sinT[:].rearrange("p g t -> p (g t)"), cosT[:].rearrange("p g t -> p (g t)"))
        tm = a32(BIGW + 8192 + 4096 + 2048, SEG)
        V(lambda: dv.tensor_scalar(out=tm, in0=tau_s, scalar1=0.5, scalar2=None, op0=ALU.is_gt))
        V(lambda: dv.tensor_copy(out=magp[:], in_=mag))
        V(lambda: dv.tensor_copy(out=tmaskp[:], in_=tm))
        sincos(ang, 32, u3, u4, scale=float(SEG))
        V(lambda: dv.tensor_tensor(out=rotm[:, 0, 0, :], in0=u4, in1=mag, op=ALU.mult))
        V(lambda: dv.tensor_copy(out=rotm[:, 0, 1, :], in_=rotm[:, 0, 0, :]))
        V(lambda: dv.tensor_tensor(out=rotm[:, 1, 1, :], in0=u3, in1=mag, op=ALU.mult))
        V(lambda: dv.tensor_scalar(out=rotm[:, 1, 0, :], in0=rotm[:, 1, 1, :], scalar1=-1.0, scalar2=None, op0=ALU.mult))
        V(lambda: dv.memset(sprev[:], 0.0))
        phm = a32(BIGW + 8192 + 4096, 2048)
        V(lambda: dv.tensor_tensor(out=phm.rearrange("p (g t) -> p g t", t=SEG),
                                   in0=lrdt.unsqueeze(2).to_broadcast([128, 32, SEG]),
                                   in1=tau_s.unsqueeze(1).to_broadcast([128, 32, SEG]), op=ALU.mult))
        A(lambda: ac.activation(out=phm, in_=phm, func=AF.Exp))
        V(lambda: dv.tensor_tensor(out=ptab[:, :, 0, :], in0=phm.rearrange("p (g t) -> p g t", t=SEG), in1=cosT[:], op=ALU.mult))
        V(lambda: dv.tensor_tensor(out=ptab[:, :, 1, :], in0=phm.rearrange("p (g t) -> p g t", t=SEG), in1=sinT[:], op=ALU.mult))
        pw = [(sc[17], sc[18]), (sc[19], sc[20])]
        V(lambda: dv.tensor_copy(out=pw[0][0], in_=abr))
        V(lambda: dv.tensor_copy(out=pw[0][1], in_=abi))
        a64 = (sc[21], sc[22])
        cur = 0
        for it in range(9):
            r_, i_ = pw[cur]
            nr, ni = pw[1 - cur]
            V(lambda: dv.tensor_tensor(out=u1, in0=r_, in1=r_, op=ALU.mult))
            V(lambda: dv.tensor_tensor(out=u2, in0=i_, in1=i_, op=ALU.mult))
            V(lambda: dv.tensor_tensor(out=nr, in0=u1, in1=u2, op=ALU.subtract))
            V(lambda: dv.tensor_tensor(out=u1, in0=r_, in1=i_, op=ALU.mult))
            V(lambda: dv.tensor_scalar(out=ni, in0=u1, scalar1=2.0, scalar2=None, op0=ALU.mult))
            cur = 1 - cur
            if it == 5:
                V(lambda: dv.tensor_copy(out=a64[0], in_=nr))
                V(lambda: dv.tensor_copy(out=a64[1], in_=ni))
        V(lambda: dv.tensor_copy(out=a512[:, 0, :], in_=pw[cur][0]))
        V(lambda: dv.tensor_copy(out=a512[:, 1, :], in_=pw[cur][1]))
        V(lambda: dv.tensor_copy(out=ttab[:, 0, 0, :], in_=abr))
        V(lambda: dv.tensor_copy(out=ttab[:, 1, 0, :], in_=abi))
        for s_ in range(1, NSEG):
            pr, pi = ttab[:, 0, s_ - 1, :], ttab[:, 1, s_ - 1, :]
            V(lambda: dv.tensor_tensor(out=u1, in0=pr, in1=a64[0], op=ALU.mult))
            V(lambda: dv.tensor_tensor(out=u2, in0=pi, in1=a64[1], op=ALU.mult))
            V(lambda: dv.tensor_tensor(out=ttab[:, 0, s_, :], in0=u1, in1=u2, op=ALU.subtract))
            V(lambda: dv.tensor_tensor(out=u1, in0=pr, in1=a64[1], op=ALU.mult))
            V(lambda: dv.tensor_tensor(out=u2, in0=pi, in1=a64[0], op=ALU.mult))
            V(lambda: dv.tensor_tensor(out=ttab[:, 1, s_, :], in0=u1, in1=u2, op=ALU.add))
        T.op(DVE, lambda: dv.memset(small[:], 0.0), reads=[setupB, constB], writes=[smallB, constB])

        ring = {"slots": [], "i": 0, "offs": slot_off}

        def set_ring(nslots):
            bufs = [Buf(f"slot{j}") for j in range(nslots)]
            ring["slots"] = bufs
            ring["i"] = 0
            return bufs

        def load_w(view_fn, src):
            j = ring["i"] % len(ring["slots"])
            ring["i"] += 1
            b = ring["slots"][j]
            sl = a16(ring["offs"][j], 4096)
            dst = view_fn(sl)
            T.dma(POOL, dst, src, b, writes=[b])
            return dst, b

        def wview_k(nk, ncols):
            return lambda sl: sl[:, 0:nk * ncols].rearrange("p (k c) -> p k c", k=nk)

        pe, gp = nc.tensor, nc.gpsimd

        def norm(srcs, gcol, dsts, dim, sqB, rtB, in_place_fp32=False):
            n = len(srcs)
            pst, psb = nextps()
            for i, (sap, sbuf) in enumerate(srcs):
                sq = a16(OFF_SQ + 256 * (i % 2), 512)
                T.op(ACT, lambda: ac.activation(out=sq, in_=sap, func=AF.Square), reads=[sbuf], writes=[sqB[i % 2]])
                T.op(PE, lambda: pe.matmul(pst[:], lhsT=ones_t[:], rhs=sq, start=(i == 0), stop=(i == n - 1)),
                     reads=[sqB[i % 2], constB], writes=[psb])
            rt = a32(OFF_RT, 512)
            rstd = a32(OFF_RT + 512, 512)
            T.op(ACT, lambda: ac.activation(out=rt, in_=pst[:], func=AF.Sqrt, scale=1.0 / dim, bias=small[:, 1:2]),
                 reads=[psb, smallB], writes=[rtB[0]])
            T.op(DVE, lambda: dv.reciprocal(out=rstd, in_=rt), reads=[rtB[0]], writes=[rtB[1]])
            for i, (sap, sbuf) in enumerate(srcs):
                dap, dbuf = dsts[i]
                T.op(DVE, lambda: dv.scalar_tensor_tensor(out=dap, in0=sap, scalar=gcol[:, i:i + 1], in1=rstd,
                                                          op0=ALU.mult, op1=ALU.mult),
                     reads=[sbuf, rtB[1], constB], writes=[dbuf])

        T.op(DVE, lambda: dv.memset(small[:, 1:2], EPS), reads=[constB], writes=[smallB])

        hT = [(h_t[:, i, :], hB[i]) for i in range(KT)]
        xnT = [(xn_t[:, i, :], xnB[i]) for i in range(KT)]

        def ffn(wg_d, wu_d, wd_d):
            set_ring(NSLOT_FFN)
            fb = new_phase_bufs([f"slot{j}" for j in range(NSLOT_FFN)] + ["act0", "act1", "silu0", "silu1", "wpp"])
            ring["slots"] = fb[:NSLOT_FFN]
            ring["offs"] = slot_off
            actB, siluB = fb[8:10], fb[10:12]
            wgv = wg_d.rearrange("(k p) c -> p k c", p=128)
            wuv = wu_d.rearrange("(k p) c -> p k c", p=128)
            wdv = wd_d.rearrange("(j p) f -> p j f", p=128)
            NCH = DFF // 256
            pend = None

            def down(c, wd_ap, wdb, actv, ab):
                for i in range(KT):
                    pst, psb = nextps()
                    T.group(PE, [(lambda jj=jj: pe.matmul(pst[:], lhsT=wd_ap[:, jj, i * 128:(i + 1) * 128], rhs=actv[:, jj, :],
                                                         start=(jj == 0), stop=(jj == 1))) for jj in range(2)],
                            reads=[wdb, ab], writes=[psb])
                    T.op(DVE, lambda: dv.scalar_tensor_tensor(out=h_t[:, i, :], in0=pst[:], scalar=0.5, in1=h_t[:, i, :],
                                                              op0=ALU.mult, op1=ALU.add),
                         reads=[psb], writes=[hB[i]])

            for c in range(NCH):
                wg_ap, wgb = load_w(wview_k(16, 256), wgv[:, :, c * 256:(c + 1) * 256])
                wu_ap, wub = load_w(wview_k(16, 256), wuv[:, :, c * 256:(c + 1) * 256])
                wd_ap, wdb = load_w(wview_k(2, 2048), wdv[:, 2 * c:2 * c + 2, :])
                actv = a16(OFF_ACT + 512 * (c % 2), 1024).rearrange("p (j t) -> p j t", j=2)
                ab = actB[c % 2]
                for jj in range(2):
                    gps, gpb = nextps()
                    ups, upb = nextps()
                    T.group(PE, [(lambda k=k: pe.matmul(gps[:], lhsT=wg_ap[:, k, jj * 128:(jj + 1) * 128], rhs=xn_t[:, k, :],
                                                       start=(k == 0), stop=(k == KT - 1))) for k in range(KT)],
                            reads=[wgb], stagger=xnB, writes=[gpb])
                    T.group(PE, [(lambda k=k: pe.matmul(ups[:], lhsT=wu_ap[:, k, jj * 128:(jj + 1) * 128], rhs=xn_t[:, k, :],
                                                       start=(k == 0), stop=(k == KT - 1))) for k in range(KT)],
                            reads=[wub], stagger=xnB, writes=[upb])
                    sl = a32(OFF_SILU + 512 * jj, 512)
                    T.op(ACT, lambda: ac.activation(out=sl, in_=gps[:], func=AF.Silu), reads=[gpb], writes=[siluB[jj]])
                    T.op(DVE, lambda: dv.tensor_tensor(out=actv[:, jj, :], in0=sl, in1=ups[:], op=ALU.mult),
                         reads=[siluB[jj], upb], writes=[ab])
                if pend is not None:
                    down(*pend)
                pend = (c, wd_ap, wdb, actv, ab)
            down(*pend)
            return fb

        for blk_i in range(nblk):
            t0 = blk_i * TB
            scope(f"b{blk_i}_load")
            lb = new_phase_bufs(["xin0", "xin1", "pin"])
            xinB, pinB = lb[0:2], lb[2]
            for tt in range(4):
                xin = a32(OFF_XIN + 2048 * (tt % 2), 2048)
                T.dma(SP, xin, x_d[t0 + tt * 128:t0 + (tt + 1) * 128, :], xinB[tt % 2], writes=[xinB[tt % 2]])
                for q in range(4):
                    pst, psb = nextps()
                    T.group(PE, [(lambda kk=kk: pe.transpose(pst[:, kk * 128:(kk + 1) * 128],
                                                            xin[:, (4 * q + kk) * 128:(4 * q + kk + 1) * 128], ident_t[:]))
                                 for kk in range(4)], reads=[xinB[tt % 2], constB], writes=[psb])
                    T.op(ACT, lambda: ac.activation(out=h_t[:, 4 * q:4 * q + 4, tt * 128:(tt + 1) * 128],
                                                    in_=pst[:].rearrange("p (a b) -> p a b", a=4), func=AF.Copy),
                         reads=[psb], writes=hB[4 * q:4 * q + 4])
            pin = a32(OFF_PIN, 1024).rearrange("p (a b) -> p a b", a=4)
            T.dma(SP, pin, p_d[t0:t0 + TB, :].rearrange("(a p) c -> p a c", p=128), pinB, writes=[pinB])
            pT = a16(OFF_PT, 1024).rearrange("p (k t) -> p k t", k=2)
            for k2 in range(2):
                pst, psb = nextps()
                T.group(PE, [(lambda a=a: pe.transpose(pst[:, a * 128:(a + 1) * 128], pin[:, a, k2 * 128:(k2 + 1) * 128], ident_t[:]))
                             for a in range(4)], reads=[pinB, constB], writes=[psb])
                T.op(ACT, lambda: ac.activation(out=pT[:, k2, :], in_=pst[:], func=AF.Copy), reads=[psb], writes=[ptB])

            if stage >= 1:
                scope(f"b{blk_i}_ffn1")
                norm(hT, cols_t[:, C_G1:C_G1 + 16], xnT, D, sqB, rtB)
                fb = ffn(w1g_d, w1u_d, w1d_d)
            if stage >= 2:
                mb = new_phase_bufs(["slot0", "slot1", "slot2"] + [f"zb{i}" for i in range(8)] + [f"gu{i}" for i in range(8)]
                                    + [f"v{i}" for i in range(4)] + [f"E{i}" for i in range(12)] + ["S0", "S1"] + [f"t{i}" for i in range(6)])
                ring["slots"] = mb[0:3]
                ring["i"] = 0
                ring["offs"] = [0, 2048, 18432]
                zbB, guB, vB = mb[3:11], mb[11:19], mb[19:23]
                EBf, SB_, tB = mb[23:35], mb[35:37], mb[37:43]

                def EBc(buf, r, jj):
                    return EBf[buf * 4 + r * 2 + jj]

                def EBall(buf):
                    return EBf[buf * 4:buf * 4 + 4]
                zb = a16(OFF_ZB, 4096).rearrange("p (k t) -> p k t", k=8)
                gu = a16(OFF_GU, 4096).rearrange("p (k t) -> p k t", k=8)
                vv = a16(OFF_V, 4096).rearrange("p (a f) -> p a f", a=4)
                Ev2 = [a32(o_, 2048).rearrange("p (r s j t) -> p r s j t", r=2, s=NSEG, j=2) for o_ in (OFF_E, OFF_E + 2048, 20480)]
                tmp = [a32(OFF_TMP + 512 * i, 512) for i in range(6)]
                y1t = [a16(OFF_GU + 512 * i, 512) for i in range(8)]

                def ymB(b_):
                    return [guB[2 * b_], guB[2 * b_ + 1]] if b_ < 4 else [vB[b_ - 4]]
                y1B = [ymB(i) for i in range(8)]
                scope(f"b{blk_i}_inproj")
                norm(hT, cols_t[:, C_GM:C_GM + 16], xnT, D, sqB, rtB)
                winv = win_d.rearrange("(k p) c -> p k c", p=128)
                for c in range(8):
                    w_ap, wb = load_w(wview_k(16, 256), winv[:, :, c * 256:(c + 1) * 256])
                    for jj in range(2):
                        ft = 2 * c + jj
                        pst, psb = nextps()
                        T.group(PE, [(lambda k=k: pe.matmul(pst[:], lhsT=w_ap[:, k, jj * 128:(jj + 1) * 128], rhs=xn_t[:, k, :],
                                                           start=(k == 0), stop=(k == KT - 1))) for k in range(KT)],
                                reads=[wb], stagger=xnB, writes=[psb])
                        if ft < 8:
                            T.op(ACT, lambda: ac.activation(out=zb[:, ft, :], in_=pst[:], func=AF.Copy), reads=[psb], writes=[zbB[ft]])
                        else:
                            T.op(ACT, lambda: ac.activation(out=gu[:, ft - 8, :], in_=pst[:], func=AF.Gelu_apprx_tanh),
                                 reads=[psb], writes=[guB[ft - 8]])
                for c in range(4):
                    w_ap, wb = load_w(wview_k(16, 256), winv[:, :, 2048 + c * 256:2048 + (c + 1) * 256])
                    for tt in range(4):
                        pst, psb = nextps()
                        T.group(PE, [(lambda k=k: pe.matmul(pst[:, 0:256], lhsT=xn_t[:, k, tt * 128:(tt + 1) * 128], rhs=w_ap[:, k, :],
                                                           start=(k == 0), stop=(k == KT - 1))) for k in range(KT)],
                                reads=[wb], stagger=xnB, writes=[psb])
                        T.op(ACT, lambda: ac.activation(out=vv[:, tt, c * 256:(c + 1) * 256], in_=pst[:, 0:256], func=AF.Gelu_apprx_tanh),
                             reads=[psb], writes=[vB[tt]])
                scope(f"b{blk_i}_gmlp")
                t1024 = a32(OFF_TMP, 1024)
                for tt in range(4):
                    st = small[:, 8:20]
                    T.op(DVE, lambda: dv.bn_stats(out=small[:, 8:14], in_=vv[:, tt, 0:512]), reads=[vB[tt]], writes=[smallB])
                    T.op(DVE, lambda: dv.bn_stats(out=small[:, 14:20], in_=vv[:, tt, 512:1024]), reads=[vB[tt]], writes=[smallB])
                    T.op(DVE, lambda: dv.bn_aggr(out=small[:, 20:22], in_=st), reads=[smallB], writes=[smallB])
                    T.op(ACT, lambda: ac.activation(out=small[:, 22:23], in_=small[:, 21:22], func=AF.Sqrt, bias=small[:, 1:2]),
                         reads=[smallB], writes=[smallB])
                    T.op(DVE, lambda: dv.reciprocal(out=small[:, 23:24], in_=small[:, 22:23]), reads=[smallB], writes=[smallB])
                    T.op(DVE, lambda: dv.tensor_scalar(out=t1024, in0=vv[:, tt, :], scalar1=small[:, 20:21], scalar2=small[:, 23:24],
                                                       op0=ALU.subtract, op1=ALU.mult),
                         reads=[vB[tt], smallB], writes=[tB[0], tB[1]])
                    T.op(DVE, lambda: dv.tensor_tensor(out=vv[:, tt, :], in0=t1024, in1=nvbc_t[:], op=ALU.mult),
                         reads=[tB[0], tB[1], constB], writes=[vB[tt]])
                for hd in range(8):
                    pst, psb = nextps()
                    T.group(PE, [(lambda tt=tt: pe.matmul(pst[:, tt * 128:(tt + 1) * 128], lhsT=vv[:, tt, hd * 128:(hd + 1) * 128],
                                                         rhs=wct_t[:, hd * 128:(hd + 1) * 128], start=True, stop=True)) for tt in range(4)],
                            reads=vB + [constB], writes=[psb])
                    tq = tmp[2 + hd % 2]
                    T.op(DVE, lambda: dv.tensor_tensor(out=tq.rearrange("p (a t) -> p a t", a=4), in0=pst[:].rearrange("p (a t) -> p a t", a=4),
                                                       in1=bsbc_t[:, hd * 128:(hd + 1) * 128].unsqueeze(1).to_broadcast([128, 4, 128]), op=ALU.add),
                         reads=[psb, constB], writes=[tB[2 + hd % 2]])
                    T.op(DVE, lambda: dv.tensor_tensor(out=gu[:, hd, :], in0=gu[:, hd, :], in1=tq, op=ALU.mult),
                         reads=[tB[2 + hd % 2]], writes=[guB[hd]])
                norm([(gu[:, i, :], guB[i]) for i in range(8)], cols_t[:, C_GG:C_GG + 8], xnT[8:16], DSSM, sqB, rtB)
                scope(f"b{blk_i}_ssm")
                ymain = a32(OFF_GU, 4096).rearrange("p (k t) -> p k t", k=8)
                ypsd = {}

                def st_A(b2):
                    Eb, eb = Ev2[b2 % 3], b2 % 3
                    bt_ = b2 // 2
                    for jj in range(2):
                        j = 2 * (b2 % 2) + jj
                        g_ = 2 * b2 + jj
                        dre, dreb = nextps()
                        dim_, dimb = nextps()
                        T.op(PE, lambda: pe.matmul(dre[:], lhsT=bbT[32 * j:32 * j + 32, bt_, 0, :], rhs=zb[32 * j:32 * j + 32, bt_, :],
                                                   start=True, stop=True, tile_position=(32 * j, 0)), reads=[zbB[bt_], constB], writes=[dreb])
                        T.op(PE, lambda: pe.matmul(dim_[:], lhsT=bbT[32 * j:32 * j + 32, bt_, 1, :], rhs=zb[32 * j:32 * j + 32, bt_, :],
                                                   start=True, stop=True, tile_position=(32 * j, 0)), reads=[zbB[bt_], constB], writes=[dimb])
                        cb = cosT[:, g_, :].unsqueeze(1).to_broadcast([128, NSEG, SEG])
                        sbb = sinT[:, g_, :].unsqueeze(1).to_broadcast([128, NSEG, SEG])
                        d3r = dre[:].rearrange("p (s t) -> p s t", s=NSEG)
                        d3i = dim_[:].rearrange("p (s t) -> p s t", s=NSEG)
                        ta, tb_ = 0, 1
                        r1 = tmp[ta].rearrange("p (s t) -> p s t", s=NSEG)
                        r3 = tmp[tb_].rearrange("p (s t) -> p s t", s=NSEG)
                        ere, eim = Eb[:, 0, :, jj, :], Eb[:, 1, :, jj, :]
                        T.op(DVE, lambda: dv.tensor_tensor(out=ere, in0=d3r, in1=cb, op=ALU.mult), reads=[dreb, constB], writes=[EBc(eb, 0, jj)])
                        T.op(DVE, lambda: dv.tensor_tensor(out=r1, in0=d3i, in1=sbb, op=ALU.mult), reads=[dimb, constB], writes=[tB[ta]])
                        T.op(DVE, lambda: dv.tensor_tensor(out=eim, in0=d3i, in1=cb, op=ALU.mult), reads=[dimb, constB], writes=[EBc(eb, 1, jj)])
                        T.op(DVE, lambda: dv.tensor_tensor(out=r3, in0=d3r, in1=sbb, op=ALU.mult), reads=[dreb, constB], writes=[tB[tb_]])
                        T.op(POOL, lambda: gp.tensor_tensor(out=ere, in0=ere, in1=r1, op=ALU.add), reads=[tB[ta]], writes=[EBc(eb, 0, jj)])
                        T.op(POOL, lambda: gp.tensor_tensor(out=eim, in0=eim, in1=r3, op=ALU.subtract), reads=[tB[tb_]], writes=[EBc(eb, 1, jj)])

                def st_B(b2):
                    Eb, eb = Ev2[b2 % 3], b2 % 3
                    all4 = EBall(eb)
                    mt2 = mtb[:, b2 % 2, :]
                    T.op(DVE, lambda: dv.tensor_tensor(out=mt2.rearrange("p (g t) -> p g t", g=2),
                                                       in0=magp[:, 2 * b2:2 * b2 + 2].unsqueeze(2).to_broadcast([128, 2, SEG]),
                                                       in1=tmaskp[:].unsqueeze(1).to_broadcast([128, 2, SEG]), op=ALU.mult),
                         reads=[constB], writes=[mtB[b2 % 2]])
                    rc = rotm[:, 0, :, 2 * b2:2 * b2 + 2]
                    rs = rotm[:, 1, :, 2 * b2:2 * b2 + 2]
                    for s_ in range(NSEG):
                        if s_ > 0:
                            last = Eb[:, :, s_ - 1, :, SEG - 1]
                            i1 = small[:, 24:28].rearrange("p (r j) -> p r j", r=2)
                            i2 = small[:, 32:36].rearrange("p (r j) -> p r j", r=2)
                            T.op(DVE, lambda: dv.tensor_tensor(out=i1, in0=last, in1=rc, op=ALU.mult), reads=all4 + [constB], writes=[fxB[0]])
                            T.op(DVE, lambda: dv.tensor_tensor(out=i2[:, 0, :], in0=last[:, 1, :], in1=rs[:, 0, :], op=ALU.mult),
                                 reads=all4 + [constB], writes=[fxB[1]])
                            T.op(DVE, lambda: dv.tensor_tensor(out=i2[:, 1, :], in0=last[:, 0, :], in1=rs[:, 1, :], op=ALU.mult),
                                 reads=all4 + [constB], writes=[fxB[2]])
                            T.op(DVE, lambda: dv.tensor_tensor(out=i1, in0=i1, in1=i2, op=ALU.add), reads=[fxB[1], fxB[2]], writes=[fxB[0]])
                            T.op(DVE, lambda: dv.tensor_tensor(out=Eb[:, :, s_, :, 0], in0=Eb[:, :, s_, :, 0], in1=i1, op=ALU.add),
                                 reads=[fxB[0]], writes=all4)
                        for r in range(2):
                            er = Eb[:, r, s_, :, :].rearrange("p j t -> p (j t)")
                            T.op(DVE, lambda: dv.tensor_tensor_scan(out=er, data0=mt2, data1=er, initial=0.0, op0=ALU.mult, op1=ALU.add),
                                 reads=[constB, mtB[b2 % 2]], writes=[EBc(eb, r, 0), EBc(eb, r, 1)])
                    T.op(DVE, lambda: dv.tensor_copy(out=carry[:, :, 2 * b2:2 * b2 + 2], in_=Eb[:, :, NSEG - 1, :, SEG - 1]),
                         reads=all4, writes=[carryB[b2 // 2]])

                def st_C(b2):
                    Eb, eb = Ev2[b2 % 3], b2 % 3
                    for jj in range(2):
                        g_ = 2 * b2 + jj
                        cb = cosT[:, g_, :].unsqueeze(1).to_broadcast([128, NSEG, SEG])
                        sbb = sinT[:, g_, :].unsqueeze(1).to_broadcast([128, NSEG, SEG])
                        rre, rim = Eb[:, 0, :, jj, :], Eb[:, 1, :, jj, :]
                        u0 = tmp[2 + 2 * jj].rearrange("p (s t) -> p s t", s=NSEG)
                        u1 = tmp[3 + 2 * jj].rearrange("p (s t) -> p s t", s=NSEG)
                        T.op(POOL, lambda: gp.tensor_tensor(out=u0, in0=rre, in1=cb, op=ALU.mult), reads=[EBc(eb, 0, jj), constB], writes=[tB[2 + 2 * jj]])
                        T.op(POOL, lambda: gp.tensor_tensor(out=u1, in0=rim, in1=sbb, op=ALU.mult), reads=[EBc(eb, 1, jj), constB], writes=[tB[3 + 2 * jj]])
                        T.op(POOL, lambda: gp.tensor_tensor(out=rre, in0=rre, in1=sbb, op=ALU.mult), reads=[constB], writes=[EBc(eb, 0, jj)])
                        T.op(POOL, lambda: gp.tensor_tensor(out=rim, in0=rim, in1=cb, op=ALU.mult), reads=[constB], writes=[EBc(eb, 1, jj)])
                        sv = a16(OFF_S + 512 * jj, 1024).rearrange("p (r t) -> p r t", r=2)
                        s3i = sv[:, 1, :].rearrange("p (s t) -> p s t", s=NSEG)
                        T.op(POOL, lambda: gp.tensor_tensor(out=s3i, in0=rre, in1=rim, op=ALU.add),
                             reads=[EBc(eb, 0, jj), EBc(eb, 1, jj)], writes=[SB_[jj]])

                def st_D(b2):
                    Eb, eb = Ev2[b2 % 3], b2 % 3
                    bt_ = b2 // 2
                    if b2 % 2 == 0:
                        ypsd[bt_] = nextps()
                    yps, ypb = ypsd[bt_]
                    for jj in range(2):
                        j = 2 * (b2 % 2) + jj
                        g_ = 2 * b2 + jj
                        rre, rim = Eb[:, 0, :, jj, :], Eb[:, 1, :, jj, :]
                        u0 = tmp[2 + 2 * jj].rearrange("p (s t) -> p s t", s=NSEG)
                        u1 = tmp[3 + 2 * jj].rearrange("p (s t) -> p s t", s=NSEG)
                        sv = a16(OFF_S + 512 * jj, 1024).rearrange("p (r t) -> p r t", r=2)
                        s3 = [sv[:, r, :].rearrange("p (s t) -> p s t", s=NSEG) for r in range(2)]
                        T.op(POOL, lambda: gp.tensor_tensor(out=s3[0], in0=u0, in1=u1, op=ALU.subtract),
                             reads=[tB[2 + 2 * jj], tB[3 + 2 * jj]], writes=[SB_[jj]])
                        T.group(PE, [(lambda r=r: pe.matmul(yps[32 * j:32 * j + 32, :], lhsT=c32b[:, g_, r, :], rhs=sv[:, r, :],
                                                           start=(r == 0), stop=(r == 1), tile_position=(0, 32 * j))) for r in range(2)],
                                reads=[SB_[jj], constB], writes=[ypb])
                    if b2 % 2 == 1:
                        T.op(DVE, lambda: dv.scalar_tensor_tensor(out=ymain[:, bt_, :], in0=zb[:, bt_, :], scalar=cols_t[:, C_DD + bt_:C_DD + bt_ + 1],
                                                                  in1=yps[:], op0=ALU.mult, op1=ALU.add),
                             reads=[zbB[bt_], ypb, constB], writes=ymB(bt_))

                NB2 = 16
                for it in range(NB2 + 3):
                    if 0 <= it - 3 < NB2:
                        st_D(it - 3)
                    if it < NB2:
                        st_A(it)
                    if 0 <= it - 1 < NB2:
                        st_B(it - 1)
                    if 0 <= it - 2 < NB2:
                        st_C(it - 2)

                def SV(fn, reads=(), writes=()):
                    return T.op(DVE, fn, reads=list(reads) + [ssmsB, constB], writes=list(writes) + [ssmsB])
                R_ = [ssms[:, i, :] for i in range(12)]
                c63, s63 = cosT[:, :, SEG - 1], sinT[:, :, SEG - 1]
                lfr, lfi = carry[:, 0, :], carry[:, 1, :]
                sllr, slli, spr, spi, sinr, sini, q1, q2 = R_[0], R_[1], R_[2], R_[3], R_[4], R_[5], R_[6], R_[7]
                SV(lambda: dv.tensor_tensor(out=q1, in0=lfr, in1=c63, op=ALU.mult), reads=carryB)
                SV(lambda: dv.tensor_tensor(out=q2, in0=lfi, in1=s63, op=ALU.mult), reads=carryB)
                SV(lambda: dv.tensor_tensor(out=sllr, in0=q1, in1=q2, op=ALU.subtract))
                SV(lambda: dv.tensor_tensor(out=q1, in0=lfr, in1=s63, op=ALU.mult), reads=carryB)
                SV(lambda: dv.tensor_tensor(out=q2, in0=lfi, in1=c63, op=ALU.mult), reads=carryB)
                SV(lambda: dv.tensor_tensor(out=slli, in0=q1, in1=q2, op=ALU.add))

                def sout_from(xr, xi, outr, outi, extra_r=(), extra_w=()):
                    SV(lambda: dv.tensor_tensor(out=q1, in0=a512[:, 0, :], in1=xr, op=ALU.mult), reads=extra_r)
                    SV(lambda: dv.tensor_tensor(out=q2, in0=a512[:, 1, :], in1=xi, op=ALU.mult), reads=extra_r)
                    SV(lambda: dv.tensor_tensor(out=q1, in0=q1, in1=q2, op=ALU.subtract))
                    SV(lambda: dv.tensor_tensor(out=q1, in0=q1, in1=sllr, op=ALU.add))
                    SV(lambda: dv.tensor_tensor(out=q2, in0=a512[:, 0, :], in1=xi, op=ALU.mult), reads=extra_r)
                    SV(lambda: dv.tensor_tensor(out=R_[8], in0=a512[:, 1, :], in1=xr, op=ALU.mult), reads=extra_r)
                    SV(lambda: dv.tensor_tensor(out=q2, in0=q2, in1=R_[8], op=ALU.add))
                    SV(lambda: dv.tensor_tensor(out=outi, in0=q2, in1=slli, op=ALU.add), writes=extra_w)
                    SV(lambda: dv.tensor_copy(out=outr, in_=q1), writes=extra_w)

                sout_from(sprev[:, 0, :], sprev[:, 1, :], spr, spi, extra_r=[sprevB])
                ccinB, ccoutB = Buf("ccin"), Buf("ccout")
                T.dma(SP, ccin_d[blk_i].ap(), ssms[:, 2:4, :].rearrange("p a b -> p (a b)"), ccinB, reads=[ssmsB], writes=[ccinB])
                T.custom(POOL, lambda: gp.collective_compute("AllGather", ALU.bypass, replica_groups=[[0, 1], [2, 3], [4, 5], [6, 7]],
                                                             ins=[ccin_d[blk_i].ap().opt()], outs=[ccout_d[blk_i].ap().opt()]),
                         f"cc{blk_i}", reads=[ccinB], writes=[ccoutB])
                T.dma(SP, gbuf[:], ccout_d[blk_i].ap().rearrange("(r p) n -> p r n", p=128), gB, reads=[ccoutB], writes=[gB])
                fl = flags_t[:, 3 * blk_i:3 * blk_i + 3]
                sin64 = ssms[:, 4:6, :].rearrange("p a b -> p (a b)")
                sp64 = sprev[:].rearrange("p a b -> p (a b)")
                SV(lambda: dv.tensor_scalar(out=sin64, in0=sp64, scalar1=fl[:, 0:1], scalar2=None, op0=ALU.mult), reads=[sprevB])
                SV(lambda: dv.scalar_tensor_tensor(out=sin64, in0=gbuf[:, 0, :], scalar=fl[:, 1:2], in1=sin64, op0=ALU.mult, op1=ALU.add), reads=[gB])
                SV(lambda: dv.scalar_tensor_tensor(out=sin64, in0=gbuf[:, 1, :], scalar=fl[:, 2:3], in1=sin64, op0=ALU.mult, op1=ALU.add), reads=[gB])
                sout_from(sinr, sini, sprev[:, 0, :], sprev[:, 1, :], extra_w=[sprevB])
                sinr_b = sinr.unsqueeze(1).to_broadcast([128, NSEG, 32])
                sini_b = sini.unsqueeze(1).to_broadcast([128, NSEG, 32])
                kq = [a32(OFF_TMP + 256 * i, 256).rearrange("p (s g) -> p s g", s=NSEG) for i in range(2)]
                SV(lambda: dv.tensor_tensor(out=kq[0], in0=ttab[:, 0], in1=sinr_b, op=ALU.mult), writes=[tB[0]])
                SV(lambda: dv.tensor_tensor(out=kq[1], in0=ttab[:, 1], in1=sini_b, op=ALU.mult), writes=[tB[0]])
                SV(lambda: dv.tensor_tensor(out=kbuf[:, 0], in0=kq[0], in1=kq[1], op=ALU.subtract), reads=[tB[0]], writes=[kB])
                SV(lambda: dv.tensor_tensor(out=kq[0], in0=ttab[:, 0], in1=sini_b, op=ALU.mult), writes=[tB[0]])
                SV(lambda: dv.tensor_tensor(out=kq[1], in0=ttab[:, 1], in1=sinr_b, op=ALU.mult), writes=[tB[0]])
                SV(lambda: dv.tensor_tensor(out=kbuf[:, 1], in0=kq[0], in1=kq[1], op=ALU.add), reads=[tB[0]], writes=[kB])
                w4 = [a32(OFF_E + 1024 * i, 1024).rearrange("p (j s c) -> p j s c", j=4, s=NSEG) for i in range(4)]
                w4B = [Buf(f"w4_{i}") for i in range(4)]
                inh = {}
                for b_ in EBf:
                    for k_, v_ in ([b_.w] if b_.w else []) + list(b_.r.items()):
                        if inh.get(k_, 0) < v_:
                            inh[k_] = v_
                for b_ in w4B:
                    b_.r = dict(inh)
                arena_live.extend(w4B)
                for bt_ in range(8):
                    wv = wbuf[:, 0]
                    wb_ = wB[0]
                    CR = c32b[:, 4 * bt_:4 * bt_ + 4, 0, :].unsqueeze(2).to_broadcast([128, 4, NSEG, 32])
                    CN = c32b[:, 4 * bt_:4 * bt_ + 4, 1, :].unsqueeze(2).to_broadcast([128, 4, NSEG, 32])
                    Kr = kbuf[:, 0, :, 4 * bt_:4 * bt_ + 4].rearrange("p s j -> p j s").unsqueeze(3).to_broadcast([128, 4, NSEG, 32])
                    Ki = kbuf[:, 1, :, 4 * bt_:4 * bt_ + 4].rearrange("p s j -> p j s").unsqueeze(3).to_broadcast([128, 4, NSEG, 32])
                    T.op(DVE, lambda: dv.tensor_tensor(out=w4[0], in0=CR, in1=Kr, op=ALU.mult), reads=[kB, constB], writes=[w4B[0]])
                    T.op(DVE, lambda: dv.tensor_tensor(out=w4[1], in0=CN, in1=Ki, op=ALU.mult), reads=[kB, constB], writes=[w4B[1]])
                    T.op(DVE, lambda: dv.tensor_tensor(out=wv[:, 0], in0=w4[0], in1=w4[1], op=ALU.add), reads=[w4B[0]], writes=[wb_])
                    T.op(POOL, lambda: gp.tensor_tensor(out=w4[2], in0=CN, in1=Kr, op=ALU.mult), reads=[kB, constB], writes=[w4B[2]])
                    T.op(POOL, lambda: gp.tensor_tensor(out=w4[3], in0=CR, in1=Ki, op=ALU.mult), reads=[kB, constB], writes=[w4B[3]])
                    T.op(POOL, lambda: gp.tensor_tensor(out=wv[:, 1], in0=w4[2], in1=w4[3], op=ALU.subtract), reads=[w4B[2]], writes=[wb_])
                    cps, cpb = nextps()
                    fns = []
                    for j in range(4):
                        for s_ in range(NSEG):
                            for r in range(2):
                                fns.append(lambda j=j, s_=s_, r=r: pe.matmul(cps[32 * j:32 * j + 32, SEG * s_:SEG * (s_ + 1)], lhsT=wv[:, r, j, s_, :],
                                                                            rhs=ptab[:, 4 * bt_ + j, r, :], start=(r == 0), stop=(r == 1),
                                                                            tile_position=(0, 32 * j)))
                    T.group(PE, fns, reads=[wb_, constB], writes=[cpb])
                    T.op(DVE, lambda: dv.tensor_tensor(out=tmp[2 + bt_ % 2], in0=ymain[:, bt_, :], in1=cps[:], op=ALU.add),
                         reads=ymB(bt_) + [cpb], writes=[tB[2 + bt_ % 2]])
                    T.op(ACT, lambda: ac.activation(out=y1t[bt_], in_=tmp[2 + bt_ % 2], func=AF.Gelu_apprx_tanh),
                         reads=[tB[2 + bt_ % 2]], writes=y1B[bt_])
                scope(f"b{blk_i}_glu_out")
                wgluv = wglu_d.rearrange("(k p) c -> p k c", p=128)
                for c in range(4):
                    w_ap, wb = load_w(wview_k(8, 256), wgluv[:, :, c * 256:(c + 1) * 256])
                    for jj in range(2):
                        i = 2 * c + jj
                        pst, psb = nextps()
                        T.group(PE, [(lambda k=k: pe.matmul(pst[:], lhsT=w_ap[:, k, jj * 128:(jj + 1) * 128], rhs=y1t[k],
                                                           start=(k == 0), stop=(k == 7))) for k in range(8)],
                                reads=[wb] + [x_ for l_ in y1B for x_ in l_], writes=[psb])
                        T.op(ACT, lambda: ac.activation(out=tmp[4 + i % 2], in_=pst[:], func=AF.Sigmoid), reads=[psb], writes=[tB[4 + i % 2]])
                        T.op(DVE, lambda: dv.tensor_tensor(out=zb[:, i, :], in0=y1t[i], in1=tmp[4 + i % 2], op=ALU.mult),
                             reads=y1B[i] + [tB[4 + i % 2]], writes=[zbB[i]])
                norm([(zb[:, i, :], zbB[i]) for i in range(8)], cols_t[:, C_GS:C_GS + 8], xnT[0:8], DSSM, sqB, rtB)
                woutv = wout_d.rearrange("(k p) c -> p k c", p=128)
                for c in range(8):
                    w_ap, wb = load_w(wview_k(16, 256), woutv[:, :, c * 256:(c + 1) * 256])
                    for jj in range(2):
                        i = 2 * c + jj
                        pst, psb = nextps()
                        T.group(PE, [(lambda k=k: pe.matmul(pst[:], lhsT=w_ap[:, k, jj * 128:(jj + 1) * 128], rhs=xn_t[:, k, :],
                                                           start=(k == 0), stop=(k == KT - 1))) for k in range(KT)],
                                reads=[wb], stagger=xnB, writes=[psb])
                        T.op(DVE, lambda: dv.tensor_tensor(out=h_t[:, i, :], in0=pst[:], in1=h_t[:, i, :], op=ALU.add), reads=[psb], writes=[hB[i]])
            if stage >= 3:
                scope(f"b{blk_i}_ffn2")
                norm(hT, cols_t[:, C_G2:C_G2 + 16], xnT, D, sqB, rtB)
                fb = ffn(w2g_d, w2u_d, w2d_d)
            if stage >= 4:
                scope(f"b{blk_i}_ple")
                norm(hT, cols_t[:, C_GP:C_GP + 16], xnT, D, sqB, rtB)
                wpgv = wpg_d.rearrange("(k p) c -> p k c", p=128)
                wppb = fb[12]
                wpp_ap = a16(18432, 4096).rearrange("p (k c) -> p k c", k=2)
                T.dma(POOL, wpp_ap, wpp_d.rearrange("(k p) c -> p k c", p=128), wppb, writes=[wppb])
                siluB = fb[10:12]
                for c in range(8):
                    w_ap, wb = load_w(wview_k(16, 256), wpgv[:, :, c * 256:(c + 1) * 256])
                    for jj in range(2):
                        i = 2 * c + jj
                        gps, gpb = nextps()
                        pps, ppb = nextps()
                        T.group(PE, [(lambda k=k: pe.matmul(gps[:], lhsT=w_ap[:, k, jj * 128:(jj + 1) * 128], rhs=xn_t[:, k, :],
                                                           start=(k == 0), stop=(k == KT - 1))) for k in range(KT)],
                                reads=[wb], stagger=xnB, writes=[gpb])
                        T.group(PE, [(lambda k=k: pe.matmul(pps[:], lhsT=wpp_ap[:, k, i * 128:(i + 1) * 128], rhs=pT[:, k, :],
                                                           start=(k == 0), stop=(k == 1))) for k in range(2)],
                                reads=[wppb, ptB], writes=[ppb])
                        sl = a32(OFF_SILU + 512 * jj, 512)
                        T.op(ACT, lambda: ac.activation(out=sl, in_=gps[:], func=AF.Sigmoid), reads=[gpb], writes=[siluB[jj]])
                        T.op(DVE, lambda: dv.tensor_tensor(out=sl, in0=sl, in1=pps[:], op=ALU.mult), reads=[ppb], writes=[siluB[jj]])
                        T.op(DVE, lambda: dv.tensor_tensor(out=h_t[:, i, :], in0=sl, in1=h_t[:, i, :], op=ALU.add), reads=[siluB[jj]], writes=[hB[i]])
            scope(f"b{blk_i}_store")
            ob = new_phase_bufs(["ost0", "ost1"])
            ostB = ob[0:2]
            norm(hT, cols_t[:, C_GF:C_GF + 16], hT, D, sqB, rtB)
            for tt in range(4):
                ost = a32(OFF_OST + 2048 * (tt % 2), 2048)
                for q in range(4):
                    pst, psb = nextps()
                    T.group(PE, [(lambda kk=kk: pe.transpose(pst[:, kk * 128:(kk + 1) * 128],
                                                            h_t[:, 4 * q + kk, tt * 128:(tt + 1) * 128], ident_t[:]))
                                 for kk in range(4)], reads=hB[4 * q:4 * q + 4] + [constB], writes=[psb])
                    T.op(ACT, lambda: ac.activation(out=ost[:, q * 512:(q + 1) * 512], in_=pst[:], func=AF.Copy),
                         reads=[psb], writes=[ostB[tt % 2]])
                T.dma(SP, out_d[t0 + tt * 128:t0 + (tt + 1) * 128, :], ost, ostB[tt % 2], reads=[ostB[tt % 2]])
        scope(None)
        for nm in ("ost0", "ost1"):
            ent = T.dsem_by_name[nm]
            SP.e.wait_ge(T.sems[ent[0]], ent[1])
        nc.n_sems_used = len(T.sems)
    return nc


def _host_consts(inp):
    f32 = np.float32
    L = 0

    def col16(v):
        return np.ascontiguousarray(v.reshape(-1, 128).T)

    cols = np.zeros((128, NCOL), f32)
    cols[:, C_G1:C_G1 + 16] = col16(inp["norm_ffn1"][L])
    cols[:, C_GM:C_GM + 16] = col16(inp["norm_mix"][L])
    cols[:, C_G2:C_G2 + 16] = col16(inp["norm_ffn2"][L])
    cols[:, C_GP:C_GP + 16] = col16(inp["norm_ple"][L])
    cols[:, C_GF:C_GF + 16] = col16(inp["norm_final"])
    cols[:, C_GS:C_GS + 8] = col16(inp["norm_ssm_out"][L])
    cols[:, C_GG:C_GG + 8] = col16(inp["norm_gmlp_out"][L])
    cols[:, C_DD:C_DD + 8] = col16(inp["ssm_d"][L])
    nvbc = np.ascontiguousarray(np.broadcast_to(inp["gmlp_norm_v"][L][None, :], (128, 1024))).astype(f32)
    bsbc = np.ascontiguousarray(np.broadcast_to(inp["gmlp_b_s"][L].reshape(1, 1024), (128, 1024))).astype(f32)
    wst = np.ascontiguousarray(inp["gmlp_w_s"][L].transpose(2, 0, 1).reshape(128, 1024)).astype(f32)
    mask = np.triu(np.ones((128, 128), f32))
    ident = np.eye(128, dtype=f32)
    tau = np.ascontiguousarray(np.broadcast_to(np.arange(SEG, dtype=f32)[None, :], (128, SEG)))

    def alay(a):
        return a.reshape(32, 2, 64).transpose(1, 2, 0).reshape(128, 32)

    ldt = np.broadcast_to(inp["ssm_log_dt"][L][:, None], (64, 64))
    ssa = np.concatenate([alay(ldt), alay(inp["ssm_a_re"][L]), alay(inp["ssm_a_im"][L])], axis=1).astype(f32)

    def blay(b):
        return b.reshape(32, 2, 64, 16).transpose(1, 2, 0, 3).reshape(128, 512)

    ssb = np.stack([blay(inp["ssm_b_re"][L]), blay(inp["ssm_b_im"][L])], axis=1).astype(f32)

    def clay(c):
        c4 = c.reshape(32, 2, 16, 64)
        o = np.zeros((2, 64, 32, 2, 16), f32)
        for g2 in range(2):
            o[g2, :, :, g2, :] = c4[:, g2, :, :].transpose(2, 0, 1)
        return o.reshape(128, 1024)

    ssc = np.stack([clay(inp["ssm_c_re"][L]), clay(inp["ssm_c_im"][L])], axis=1).astype(f32)
    return dict(cols=cols, nvbc=nvbc, bsbc=bsbc, wst=wst, mask=mask, ident=ident, tau=tau, ssa=ssa, ssb=ssb, ssc=ssc)


_NC_CACHE = {}
BLOCKS_OF = [[0, 3, 4, 7], [1, 2, 5, 6]]
FIRST_RANK = [0, 1, 0, 1]


def _flags(rank):
    f = np.zeros((128, 12), np.float32)
    for k in range(4):
        if FIRST_RANK[k] == rank:
            f[:, 3 * k + 0] = 1.0
        elif rank == 0:
            f[:, 3 * k + 2] = 1.0
        else:
            f[:, 3 * k + 1] = 1.0
    return f


def kernel(**inp):
    L = 0
    consts = _host_consts(inp)
    shared = dict(
        w1g=np.ascontiguousarray(inp["w1_gate"][L]), w1u=np.ascontiguousarray(inp["w1_up"][L]), w1d=np.ascontiguousarray(inp["w1_down"][L]),
        w2g=np.ascontiguousarray(inp["w2_gate"][L]), w2u=np.ascontiguousarray(inp["w2_up"][L]), w2d=np.ascontiguousarray(inp["w2_down"][L]),
        win=np.ascontiguousarray(inp["w_in"][L]), wglu=np.ascontiguousarray(inp["ssm_w_glu"][L]), wout=np.ascontiguousarray(inp["w_out"][L]),
        wpg=np.ascontiguousarray(inp["w_ple_gate"][L]), wpp=np.ascontiguousarray(inp["w_ple_proj"][L]), **consts)
    nblk = 4
    key = ("v2", nblk)
    if key not in _NC_CACHE:
        _NC_CACHE[key] = build_nc(nblk, use_cc=True)
    nc = _NC_CACHE[key]
    x = inp["x"]
    p = inp["p"][L]
    in_maps = []
    for c in range(N_CORES):
        b, r = c // 2, c % 2
        m = dict(shared)
        m["x"] = np.ascontiguousarray(np.concatenate([x[b, g * TB:(g + 1) * TB] for g in BLOCKS_OF[r]], axis=0))
        m["p"] = np.ascontiguousarray(np.concatenate([p[b, g * TB:(g + 1) * TB] for g in BLOCKS_OF[r]], axis=0))
        m["flags"] = _flags(r)
        in_maps.append(m)
    res = run_bass_kernel_spmd(nc, in_maps, core_ids=list(range(N_CORES)))
    out = np.empty((4, SEQ, D), np.float32)
    for c in range(N_CORES):
        b, r = c // 2, c % 2
        o = res.results[c]["out"]
        for k, g in enumerate(BLOCKS_OF[r]):
            out[b, g * TB:(g + 1) * TB] = o[k * TB:(k + 1) * TB]
    return out
```
